# Optimizing a Trainium2 kernel written in Bass

```python
import math
import jax
import jax.numpy as jnp
from jax import lax
import numpy as np

D_MODEL = 1024
BATCH = 8
SEQ = 2048
DEPTH = 2
DEC_BATCH = 4
DEC_SEQ = 8192
PAST_LEN = 128

MIX_WIDTH = D_MODEL
S5_WIDTH = MIX_WIDTH // 4
S5_GROUP = 16
S5_GROUPS = S5_WIDTH // S5_GROUP
S5_STATE = 64
FNET_WIDTH = MIX_WIDTH // 4
FNET_GROUPS = 4
FNET_GROUP = FNET_WIDTH // FNET_GROUPS
SSD_WIDTH = MIX_WIDTH - S5_WIDTH - FNET_WIDTH
SSD_HEAD_DIM = 64
SSD_HEADS = SSD_WIDTH // SSD_HEAD_DIM
SSD_GROUPS = 2
SSD_HEADS_PER_GROUP = SSD_HEADS // SSD_GROUPS
SSD_STATE = 128
SSD_CONV = 5
SSD_CHUNK = 128
SSD_BC = SSD_GROUPS * SSD_STATE
SSD_CONV_DIM = SSD_WIDTH + 2 * SSD_BC
SSD_PROJ = SSD_WIDTH + SSD_CONV_DIM + SSD_HEADS
IN_PROJ = S5_WIDTH + FNET_WIDTH + SSD_PROJ
D_FF = int(math.ceil(8 * D_MODEL / 3 / 256)) * 256
EPS = 1e-6

kernel_name = 'hybrid_s5_fnet_ssd_encoder'


def rmsnorm(x, g):
    xf = x.astype(jnp.float32)
    y = xf * lax.rsqrt(jnp.mean(xf * xf, axis=-1, keepdims=True) + EPS)
    return (y * g.astype(jnp.float32)).astype(x.dtype)


def s5_direction(u, b_re, b_im, lam_re, lam_im, log_step, c_re, c_im, reverse):
    lam = lax.complex(lam_re.astype(jnp.float32), lam_im.astype(jnp.float32))
    step = jnp.exp(log_step.astype(jnp.float32))[:, None]
    lam_bar = jnp.exp(lam * step)
    b = lax.complex(b_re.astype(jnp.float32), b_im.astype(jnp.float32))
    b_bar = ((lam_bar - 1.0) / lam)[..., None] * b
    bu = jnp.einsum('gpc,blgc->blgp', b_bar, u.astype(jnp.complex64))
    a = jnp.broadcast_to(lam_bar, bu.shape)

    def combine(e1, e2):
        a1, h1 = e1
        a2, h2 = e2
        return a1 * a2, a2 * h1 + h2

    _, h = lax.associative_scan(combine, (a, bu), axis=1, reverse=reverse)
    c = lax.complex(c_re.astype(jnp.float32), c_im.astype(jnp.float32))
    return jnp.einsum('gcp,blgp->blgc', c, h).real


def s5_mixer(u, b_re, b_im, lam_re_f, lam_im_f, log_step_f, c_re_f, c_im_f,
             lam_re_b, lam_im_b, log_step_b, c_re_b, c_im_b, d, w_glu, b_glu):
    bsz, l, _ = u.shape
    ug = u.astype(jnp.float32).reshape(bsz, l, S5_GROUPS, S5_GROUP)
    y = (s5_direction(ug, b_re, b_im, lam_re_f, lam_im_f, log_step_f, c_re_f, c_im_f, False)
         + s5_direction(ug, b_re, b_im, lam_re_b, lam_im_b, log_step_b, c_re_b, c_im_b, True))
    y = y.reshape(bsz, l, S5_WIDTH) + ug.reshape(bsz, l, S5_WIDTH) * d.astype(jnp.float32)
    g = jax.nn.gelu(y)
    return g * jax.nn.sigmoid(g @ w_glu.astype(jnp.float32) + b_glu.astype(jnp.float32))


def fnet_mixer(v, w, b):
    bsz, l, _ = v.shape
    vg = v.astype(jnp.float32).reshape(bsz, l, FNET_GROUPS, FNET_GROUP)
    f = jnp.fft.fftn(vg, axes=(1, 3), norm='ortho').real
    y = jnp.einsum('blgc,gcd->blgd', f, w.astype(jnp.float32)) + b.astype(jnp.float32)
    return y.reshape(bsz, l, FNET_WIDTH)


def segsum(a):
    t = a.shape[-1]
    a_rep = jnp.broadcast_to(a[..., :, None], a.shape + (t,))
    strict = jnp.tril(jnp.ones((t, t), dtype=bool), -1)
    seg = jnp.cumsum(jnp.where(strict, a_rep, 0.0), axis=-2)
    return jnp.where(jnp.tril(jnp.ones((t, t), dtype=bool)), seg, -jnp.inf)


def ssd_scan(x, dt, a, bm, cm):
    bsz, l = x.shape[:2]
    nc = l // SSD_CHUNK
    t = SSD_CHUNK
    g, r = SSD_GROUPS, SSD_HEADS_PER_GROUP
    xd = (x * dt[..., None]).reshape(bsz, nc, t, g, r, SSD_HEAD_DIM)
    da = (dt * a).reshape(bsz, nc, t, g, r).transpose(0, 3, 4, 1, 2)
    bc = bm.reshape(bsz, nc, t, g, SSD_STATE)
    cc = cm.reshape(bsz, nc, t, g, SSD_STATE)
    a_cs = jnp.cumsum(da, axis=-1)
    lmat = jnp.exp(segsum(da))
    cb = jnp.einsum('bclgn,bcsgn->bcgls', cc, bc)
    y_diag = jnp.einsum('bcgls,bgrcls,bcsgrp->bclgrp', cb, lmat, xd)
    decay_states = jnp.exp(a_cs[..., -1:] - a_cs)
    states = jnp.einsum('bclgn,bgrcl,bclgrp->bcgrpn', bc, decay_states, xd)
    states = jnp.concatenate([jnp.zeros_like(states[:, :1]), states], axis=1)
    chunk_tot = jnp.pad(a_cs[..., -1], ((0, 0), (0, 0), (0, 0), (1, 0)))
    decay_chunk = jnp.exp(segsum(chunk_tot))
    states = jnp.einsum('bgrzc,bcgrpn->bzgrpn', decay_chunk, states)[:, :-1]
    y_off = jnp.einsum('bclgn,bcgrpn,bgrcl->bclgrp', cc, states, jnp.exp(a_cs))
    return (y_diag + y_off).reshape(bsz, l, SSD_HEADS, SSD_HEAD_DIM)


def centred_depthwise_conv(x, w, b):
    pad = (SSD_CONV - 1) // 2
    y = lax.conv_general_dilated(x, w[:, None, :], window_strides=(1,), padding=[(pad, pad)],
                                 dimension_numbers=('NWC', 'WIO', 'NWC'),
                                 feature_group_count=x.shape[-1])
    return y + b


def ssd_mixer(p, conv_w, conv_b, a_log_f, dt_bias_f, a_log_b, dt_bias_b, d, norm_g):
    bsz, l, _ = p.shape
    p = p.astype(jnp.float32)
    z = p[..., :SSD_WIDTH]
    xbc = p[..., SSD_WIDTH:SSD_WIDTH + SSD_CONV_DIM]
    dt_raw = p[..., SSD_WIDTH + SSD_CONV_DIM:]
    xbc = jax.nn.silu(centred_depthwise_conv(xbc, conv_w.astype(jnp.float32), conv_b.astype(jnp.float32)))
    x = xbc[..., :SSD_WIDTH].reshape(bsz, l, SSD_HEADS, SSD_HEAD_DIM)
    bm = xbc[..., SSD_WIDTH:SSD_WIDTH + SSD_BC].reshape(bsz, l, SSD_GROUPS, SSD_STATE)
    cm = xbc[..., SSD_WIDTH + SSD_BC:].reshape(bsz, l, SSD_GROUPS, SSD_STATE)
    dt_f = jax.nn.softplus(dt_raw + dt_bias_f.astype(jnp.float32))
    dt_b = jax.nn.softplus(dt_raw + dt_bias_b.astype(jnp.float32))
    a_f = -jnp.exp(a_log_f.astype(jnp.float32))
    a_b = -jnp.exp(a_log_b.astype(jnp.float32))
    y_f = ssd_scan(x, dt_f, a_f, bm, cm)
    y_b = jnp.flip(ssd_scan(jnp.flip(x, 1), jnp.flip(dt_b, 1), a_b, jnp.flip(bm, 1), jnp.flip(cm, 1)), 1)
    y = y_f + y_b + x * d.astype(jnp.float32)[:, None]
    y = y.reshape(bsz, l, SSD_WIDTH) * jax.nn.silu(z)
    return rmsnorm(y, norm_g)


def trunk(x, norm_mix_g, w_in, s5_b_re, s5_b_im,
          s5_lam_re_f, s5_lam_im_f, s5_log_step_f, s5_c_re_f, s5_c_im_f,
          s5_lam_re_b, s5_lam_im_b, s5_log_step_b, s5_c_re_b, s5_c_im_b,
          s5_d, s5_w_glu, s5_b_glu, fnet_w, fnet_b,
          ssd_conv_w, ssd_conv_b, ssd_a_log_f, ssd_dt_bias_f, ssd_a_log_b, ssd_dt_bias_b,
          ssd_d, ssd_norm_g, w_out, norm_ffn_g, w_gate, w_up, w_down, final_norm_g):
    for i in range(DEPTH):
        h = rmsnorm(x, norm_mix_g[i])
        proj = h @ w_in[i]
        u = proj[..., :S5_WIDTH]
        v = proj[..., S5_WIDTH:S5_WIDTH + FNET_WIDTH]
        p = proj[..., S5_WIDTH + FNET_WIDTH:]
        y_a = s5_mixer(u, s5_b_re[i], s5_b_im[i],
                       s5_lam_re_f[i], s5_lam_im_f[i], s5_log_step_f[i], s5_c_re_f[i], s5_c_im_f[i],
                       s5_lam_re_b[i], s5_lam_im_b[i], s5_log_step_b[i], s5_c_re_b[i], s5_c_im_b[i],
                       s5_d[i], s5_w_glu[i], s5_b_glu[i])
        y_b = fnet_mixer(v, fnet_w[i], fnet_b[i])
        y_c = ssd_mixer(p, ssd_conv_w[i], ssd_conv_b[i], ssd_a_log_f[i], ssd_dt_bias_f[i],
                        ssd_a_log_b[i], ssd_dt_bias_b[i], ssd_d[i], ssd_norm_g[i])
        mix = jnp.concatenate([y_a, y_b, y_c], axis=-1).astype(x.dtype)
        x = x + mix @ w_out[i]
        h = rmsnorm(x, norm_ffn_g[i])
        x = x + (jax.nn.silu(h @ w_gate[i]) * (h @ w_up[i])) @ w_down[i]
    return rmsnorm(x, final_norm_g)


def setup_inputs(seed: int = 0) -> dict:
    key = jax.random.key(seed)
    ks = list(jax.random.split(key, 48))
    L, G, P, C = DEPTH, S5_GROUPS, S5_STATE, S5_GROUP
    n_idx = jnp.arange(P, dtype=jnp.float32)

    def nrm(i, shape, scale):
        return jax.random.normal(ks[i], shape, jnp.float32) * scale

    def unif(i, shape, lo, hi):
        return jax.random.uniform(ks[i], shape, jnp.float32, lo, hi)

    def gain(i, shape):
        return 1.0 + nrm(i, shape, 0.02)

    def dt_bias(i):
        dt = jnp.exp(unif(i, (L, SSD_HEADS), math.log(1e-3), math.log(1e-1)))
        return dt + jnp.log(-jnp.expm1(-dt))

    return {
        'x_prompt': nrm(0, (BATCH, SEQ, D_MODEL), 1.0),
        'x_sample': nrm(1, (DEC_BATCH, DEC_SEQ, D_MODEL), 1.0),
        'norm_mix_g': gain(2, (L, D_MODEL)),
        'w_in': nrm(3, (L, D_MODEL, IN_PROJ), D_MODEL ** -0.5),
        's5_b_re': nrm(4, (L, G, P, C), (2 * C) ** -0.5),
        's5_b_im': nrm(5, (L, G, P, C), (2 * C) ** -0.5),
        's5_lam_re_f': -0.5 + nrm(6, (L, G, P), 0.01),
        's5_lam_im_f': math.pi * n_idx + nrm(7, (L, G, P), 0.01),
        's5_log_step_f': unif(8, (L, G), math.log(1e-3), math.log(1e-1)),
        's5_c_re_f': nrm(9, (L, G, C, P), P ** -0.5),
        's5_c_im_f': nrm(10, (L, G, C, P), P ** -0.5),
        's5_lam_re_b': -0.5 + nrm(11, (L, G, P), 0.01),
        's5_lam_im_b': math.pi * n_idx + nrm(12, (L, G, P), 0.01),
        's5_log_step_b': unif(13, (L, G), math.log(1e-3), math.log(1e-1)),
        's5_c_re_b': nrm(14, (L, G, C, P), P ** -0.5),
        's5_c_im_b': nrm(15, (L, G, C, P), P ** -0.5),
        's5_d': nrm(16, (L, S5_WIDTH), 1.0),
        's5_w_glu': nrm(17, (L, S5_WIDTH, S5_WIDTH), S5_WIDTH ** -0.5),
        's5_b_glu': nrm(18, (L, S5_WIDTH), 0.02),
        'fnet_w': nrm(19, (L, FNET_GROUPS, FNET_GROUP, FNET_GROUP), FNET_GROUP ** -0.5),
        'fnet_b': nrm(20, (L, FNET_GROUPS, FNET_GROUP), 0.02),
        'ssd_conv_w': nrm(21, (L, SSD_CONV, SSD_CONV_DIM), SSD_CONV ** -0.5),
        'ssd_conv_b': nrm(22, (L, SSD_CONV_DIM), 0.02),
        'ssd_a_log_f': jnp.log(unif(23, (L, SSD_HEADS), 1.0, 16.0)),
        'ssd_dt_bias_f': dt_bias(24),
        'ssd_a_log_b': jnp.log(unif(25, (L, SSD_HEADS), 1.0, 16.0)),
        'ssd_dt_bias_b': dt_bias(26),
        'ssd_d': gain(27, (L, SSD_HEADS)),
        'ssd_norm_g': gain(28, (L, SSD_WIDTH)),
        'w_out': nrm(29, (L, MIX_WIDTH, D_MODEL), MIX_WIDTH ** -0.5),
        'norm_ffn_g': gain(30, (L, D_MODEL)),
        'w_gate': nrm(31, (L, D_MODEL, D_FF), D_MODEL ** -0.5),
        'w_up': nrm(32, (L, D_MODEL, D_FF), D_MODEL ** -0.5),
        'w_down': nrm(33, (L, D_FF, D_MODEL), D_FF ** -0.5),
        'final_norm_g': gain(34, (D_MODEL,)),
    }


def reference(x_prompt, x_sample, norm_mix_g, w_in, s5_b_re, s5_b_im,
              s5_lam_re_f, s5_lam_im_f, s5_log_step_f, s5_c_re_f, s5_c_im_f,
              s5_lam_re_b, s5_lam_im_b, s5_log_step_b, s5_c_re_b, s5_c_im_b,
              s5_d, s5_w_glu, s5_b_glu, fnet_w, fnet_b,
              ssd_conv_w, ssd_conv_b, ssd_a_log_f, ssd_dt_bias_f, ssd_a_log_b, ssd_dt_bias_b,
              ssd_d, ssd_norm_g, w_out, norm_ffn_g, w_gate, w_up, w_down, final_norm_g):
    y_prompt = trunk(x_prompt, norm_mix_g, w_in, s5_b_re, s5_b_im,
                     s5_lam_re_f, s5_lam_im_f, s5_log_step_f, s5_c_re_f, s5_c_im_f,
                     s5_lam_re_b, s5_lam_im_b, s5_log_step_b, s5_c_re_b, s5_c_im_b,
                     s5_d, s5_w_glu, s5_b_glu, fnet_w, fnet_b,
                     ssd_conv_w, ssd_conv_b, ssd_a_log_f, ssd_dt_bias_f, ssd_a_log_b, ssd_dt_bias_b,
                     ssd_d, ssd_norm_g, w_out, norm_ffn_g, w_gate, w_up, w_down, final_norm_g)
    y_sample = trunk(x_sample, norm_mix_g, w_in, s5_b_re, s5_b_im,
                     s5_lam_re_f, s5_lam_im_f, s5_log_step_f, s5_c_re_f, s5_c_im_f,
                     s5_lam_re_b, s5_lam_im_b, s5_log_step_b, s5_c_re_b, s5_c_im_b,
                     s5_d, s5_w_glu, s5_b_glu, fnet_w, fnet_b,
                     ssd_conv_w, ssd_conv_b, ssd_a_log_f, ssd_dt_bias_f, ssd_a_log_b, ssd_dt_bias_b,
                     ssd_d, ssd_norm_g, w_out, norm_ffn_g, w_gate, w_up, w_down, final_norm_g)
    return (y_prompt, y_sample)
```

```python
import math
import numpy as np
import ml_dtypes
import concourse.bass as bass
import concourse.mybir as mybir
from concourse.bass_utils import run_bass_kernel_spmd

F32 = mybir.dt.float32
BF16 = mybir.dt.bfloat16
I32 = mybir.dt.int32
AF = mybir.ActivationFunctionType
ALU = mybir.AluOpType

D = 1024
DEPTH = 2
NT = 8192
SEG = 2048
NSEG = 4
TT = 512
NTT = NT // TT
INP = 2056
DFF = 2816
NFF = DFF // 128
EPS = 1e-6


class Buf:
    __slots__ = ("name", "w", "rs")

    def __init__(self, name):
        self.name = name
        self.w = None
        self.rs = []


class Sched:
    ENG = ("pe", "act", "dve", "pool", "sp")

    def __init__(self, nc, n_dma_sems=32):
        self.nc = nc
        self.ops = {e: [] for e in self.ENG}
        self.cnt = {e: 0 for e in self.ENG}
        self.seen = {e: {} for e in self.ENG}
        self.n_dma_sems = n_dma_sems
        self.dma_rr = 0
        self.dma_cnt = [0] * (n_dma_sems + self.N_BG)
        self.bg_rr = 0
        self.nops = 0
        self.opidx = {e: {} for e in self.ENG}

    def _deps(self, reads, writes):
        toks = []
        for b in reads:
            if b.w is not None:
                toks.append(b.w)
        for b in writes:
            if b.w is not None:
                toks.append(b.w)
            toks.extend(b.rs)
        return toks

    def _filter(self, e, toks, skip_same):
        best = {}
        seen = self.seen[e]
        for (s, v) in toks:
            if skip_same and s == ("eng", e):
                continue
            if seen.get(s, 0) >= v:
                continue
            best[s] = max(best.get(s, 0), v)
        for s, v in best.items():
            seen[s] = v
        return list(best.items())

    def _commit(self, tok, reads, writes):
        for b in reads:
            b.rs.append(tok)
        for b in writes:
            b.w = tok
            b.rs = []
        self.nops += 1

    def op(self, e, fn, reads=(), writes=()):
        waits = self._filter(e, self._deps(reads, writes), skip_same=(e == "pe"))
        self.cnt[e] += 1
        tok = (("eng", e), self.cnt[e])
        self.opidx[e][self.cnt[e]] = len(self.ops[e])
        self.ops[e].append((waits, fn, None))
        self._commit(tok, reads, writes)
        return tok

    N_BG = 8

    def dma(self, e, out, in_, reads=(), writes=(), bg=False, **kw):
        if bg:
            k = self.n_dma_sems + (self.bg_rr % self.N_BG)
            self.bg_rr += 1
        else:
            k = self.dma_rr
            self.dma_rr = (self.dma_rr + 1) % self.n_dma_sems
        s = ("dma", k)
        toks = self._deps(reads, writes)
        if self.dma_cnt[k] > 0:
            toks.append((s, 16 * self.dma_cnt[k]))
        waits = self._filter(e, toks, skip_same=False)
        self.dma_cnt[k] += 1
        tok = (s, 16 * self.dma_cnt[k])

        def fn(h, out=out, in_=in_, kw=kw):
            return h.dma_start(out=out, in_=in_, **kw)
        self.ops[e].append((waits, fn, s))
        self._commit(tok, reads, writes)
        return tok

    def all_tokens(self, bg=True):
        toks = []
        for en in self.ENG:
            if self.cnt[en] > 0:
                toks.append((("eng", en), self.cnt[en]))
        for k in range(self.n_dma_sems + (self.N_BG if bg else 0)):
            if self.dma_cnt[k] > 0:
                toks.append((("dma", k), 16 * self.dma_cnt[k]))
        return toks

    def barrier(self, bg=True):
        toks = self.all_tokens(bg)
        for e in self.ENG:
            waits = self._filter(e, [t for t in toks if t[0] != ("eng", e)], skip_same=False)
            if waits:
                self.ops[e].append((waits, None, None))

    def finish_wait_all(self, e="sp"):
        self.barrier()
        import os
        for en in self.ENG:
            for _ in range(int(os.environ.get("END_NOPS", "0"))):
                self.ops[en].append(([], "nop", None))

    def run(self):
        nc = self.nc
        from contextlib import ExitStack
        ms = {e: set() for e in self.ENG}
        for e in self.ENG:
            for waits, fn, ds in self.ops[e]:
                for (s, v) in waits:
                    if s != "drain" and s[0] == "eng":
                        ms[s[1]].add(v)
        msval = {}
        for e in self.ENG:
            c = 0
            mv = {}
            for n in range(1, self.cnt[e] + 1):
                if n in ms[e]:
                    c += 1
                    mv[n] = c
            msval[e] = mv
        self.n_milestones = {e: len(ms[e]) for e in self.ENG}
        with ExitStack() as st:
            semh = {}
            for en in self.ENG:
                semh[("eng", en)] = st.enter_context(nc.semaphore("s_" + en))
            for k in range(self.n_dma_sems + self.N_BG):
                semh[("dma", k)] = st.enter_context(nc.semaphore("s_dma%d" % k))
            block = st.enter_context(nc.Block())
            ops = self.ops

            def emit(en, h):
                n = 0
                for waits, fn, ds in ops[en]:
                    for (s, v) in waits:
                        if s == "drain":
                            h.drain()
                        elif s[0] == "eng":
                            h.wait_ge(semh[s], msval[s[1]][v])
                        else:
                            h.wait_ge(semh[s], v)
                    if fn is None:
                        continue
                    if fn == "nop":
                        h.nop()
                        continue
                    ins = fn(h)
                    if ds is not None:
                        ins.then_inc(semh[ds], 16)
                    else:
                        n += 1
                        if n in ms[en]:
                            ins.then_inc(semh[("eng", en)], 1)

            @block.tensor
            def _(h):
                emit("pe", h)

            @block.scalar
            def _(h):
                emit("act", h)

            @block.vector
            def _(h):
                emit("dve", h)

            @block.gpsimd
            def _(h):
                emit("pool", h)

            @block.sync
            def _(h):
                emit("sp", h)


class Prog:
    def __init__(self, debug_outputs=(), stages=None, depth=DEPTH):
        self.nc = bass.Bass("TRN2", target_bir_lowering=False)
        self.S = Sched(self.nc)
        self.debug_outputs = set(debug_outputs)
        self.stages = stages
        self.depth = depth
        self.dram = {}
        self.dbuf = {}
        self.sb_off = 0
        self._names = 0

    def din(self, name, shape, dt=F32):
        t = self.nc.dram_tensor(name, list(shape), dt, kind="ExternalInput").ap()
        self.dram[name] = t
        self.dbuf[name] = Buf(name)
        return t

    def dout(self, name, shape, dt=F32):
        t = self.nc.dram_tensor(name, list(shape), dt, kind="ExternalOutput").ap()
        self.dram[name] = t
        self.dbuf[name] = Buf(name)
        return t

    def dscr(self, name, shape, dt):
        kind = "ExternalOutput" if name in self.debug_outputs else "Internal"
        t = self.nc.dram_tensor(name, list(shape), dt, kind=kind).ap()
        self.dram[name] = t
        self.dbuf[name] = Buf(name)
        return t

    def sb(self, name, shape, dt):
        self._names += 1
        t = self.nc.alloc_sbuf_tensor("%s_%d" % (name, self._names), list(shape), dt)
        return t, Buf(name)

    def free_sb(self, *ts):
        pass

    def dbg(self, name, ap, b, shape, dt=F32):
        import os
        if name not in os.environ.get("KDBG", "").split(",") or name in self.dram:
            return
        t = self.dout(name, shape, dt)
        self.S.dma("sp", t, ap, reads=[b])

    def act(self, fn, reads=(), writes=()):
        return self.S.op("act", fn, reads, writes)

    def dve(self, fn, reads=(), writes=()):
        return self.S.op("dve", fn, reads, writes)

    def pool(self, fn, reads=(), writes=()):
        return self.S.op("pool", fn, reads, writes)

    def pe(self, fn, reads=(), writes=()):
        return self.S.op("pe", fn, reads, writes)

    def mm(self, out, lhsT, rhs, start, stop, reads, writes):
        return self.S.op("pe", lambda h: h.matmul(out, lhsT=lhsT, rhs=rhs, start=start, stop=stop), reads, writes)

    def ld(self, out, in_, reads=(), writes=(), q="sp", **kw):
        return self.S.dma(q, out, in_, reads, writes, **kw)

    def st(self, out, in_, reads=(), writes=(), q="pool", **kw):
        return self.S.dma(q, out, in_, reads, writes, **kw)

    ARENA_BYTES = 206 * 1024

    def arena_init(self):
        self.arena = self.nc.alloc_sbuf_tensor("arena", [128, self.ARENA_BYTES // 2], BF16)
        self.a_off = 0
        self.a_perm = 0
        self.psum = []
        for i in range(8):
            t = self.nc.alloc_psum_tensor("psb%d" % i, [128, 512], F32)
            self.psum.append((t, Buf("ps%d" % i)))
        self.ps_rr = 0

    def alloc(self, name, shape, dt, perm=False):
        esz = {F32: 4, BF16: 2, I32: 4}[dt]
        n = 1
        for s in shape[1:]:
            n *= s
        nbytes = (n * esz + 63) // 64 * 64
        off = self.a_off
        assert off + nbytes <= self.ARENA_BYTES, ("SBUF arena overflow", name, off, nbytes)
        self.a_off += nbytes
        v = self.arena[0:shape[0], off // 2: off // 2 + (n * esz) // 2]
        if dt != BF16:
            v = v.bitcast(dt)
        if len(shape) == 3:
            v = v.rearrange("p (a b) -> p a b", a=shape[1])
        elif len(shape) == 4:
            v = v.rearrange("p (a b c) -> p a b c", a=shape[1], b=shape[2])
        elif len(shape) == 5:
            v = v.rearrange("p (a b c d) -> p a b c d", a=shape[1], b=shape[2], c=shape[3])
        return v, Buf(name)

    def ps(self):
        t, b = self.psum[self.ps_rr]
        self.ps_rr = (self.ps_rr + 1) % 8
        return t, b

    def mark_perm(self):
        self.a_perm = self.a_off

    def barrier(self, bg=None):
        self.S.barrier(getattr(self, "bg_wait", True) if bg is None else bg)

    def stage_reset(self, bg=True):
        self.bg_wait = bg
        self.barrier(bg)
        self.a_off = self.a_perm

    WEIGHT_SHAPES = {
        'norm_mix_g': (DEPTH, D), 'w_in': (DEPTH, D, INP),
        's5_b_re': (DEPTH, 16, 64, 16), 's5_b_im': (DEPTH, 16, 64, 16),
        's5_lam_re_f': (DEPTH, 16, 64), 's5_lam_im_f': (DEPTH, 16, 64), 's5_log_step_f': (DEPTH, 16),
        's5_c_re_f': (DEPTH, 16, 16, 64), 's5_c_im_f': (DEPTH, 16, 16, 64),
        's5_lam_re_b': (DEPTH, 16, 64), 's5_lam_im_b': (DEPTH, 16, 64), 's5_log_step_b': (DEPTH, 16),
        's5_c_re_b': (DEPTH, 16, 16, 64), 's5_c_im_b': (DEPTH, 16, 16, 64),
        's5_d': (DEPTH, 256), 's5_w_glu': (DEPTH, 256, 256), 's5_b_glu': (DEPTH, 256),
        'fnet_w': (DEPTH, 4, 64, 64), 'fnet_b': (DEPTH, 4, 64),
        'ssd_conv_w': (DEPTH, 5, 1024), 'ssd_conv_b': (DEPTH, 1024),
        'ssd_a_log_f': (DEPTH, 8), 'ssd_dt_bias_f': (DEPTH, 8), 'ssd_a_log_b': (DEPTH, 8),
        'ssd_dt_bias_b': (DEPTH, 8), 'ssd_d': (DEPTH, 8), 'ssd_norm_g': (DEPTH, 512),
        'w_out': (DEPTH, D, D), 'norm_ffn_g': (DEPTH, D),
        'w_gate': (DEPTH, D, DFF), 'w_up': (DEPTH, D, DFF), 'w_down': (DEPTH, DFF, D),
        'final_norm_g': (D,),
    }

    def declare(self, debug_inputs=()):
        self.w = {}
        for k, shp in self.WEIGHT_SHAPES.items():
            self.w[k] = self.din(k, shp)
        self.x_in = self.din("x", (NT, D))
        self.flags = self.din("flags", (128, 2))
        self.c_ident = self.din("c_ident", (128, 128))
        self.y_out = self.dout("y", (NT, D))
        self.c_m1t = self.din("c_m1t", (64, 192), BF16)
        self.c_E = self.din("c_E", (128, 64, 2, 128), BF16)
        self.c_cspad = self.din("c_cspad", (64, 2, 2, 128))
        self.c_tri = self.din("c_tri", (128, 5, 128))
        self.c_iota = self.din("c_iota", (128, 48))
        self.c_s5mask = self.din("c_s5mask", (128, 2, 128))

        def scr(name, shape, dt):
            if name in debug_inputs:
                return self.din(name, shape, dt)
            return self.dscr(name, shape, dt)
        self.scr = scr
        self.wb = {
            'w_in': scr("wb_in", (DEPTH, D, INP), BF16),
            'w_out': scr("wb_out", (DEPTH, D, D), BF16),
            'w_gate': scr("wb_gate", (DEPTH, D, DFF), BF16),
            'w_up': scr("wb_up", (DEPTH, D, DFF), BF16),
            'w_down': scr("wb_down", (DEPTH, DFF, D), BF16),
            's5_w_glu': scr("wb_glu", (DEPTH, 256, 256), BF16),
        }
        self.x1 = scr("x1", (NT, D), F32)
        self.uvz = scr("uvz_tm", (NT, 1024), BF16)
        self.xbc = scr("xbc_fm", (1024, NT), BF16)
        self.dtr = scr("dt_tm", (NT, 8), F32)
        self.mix = scr("mix_fm", (1024, NT), BF16)
        self.yf_d = scr("yf_tm", (NT, 512), F32)

    def prep_weights(self):
        jobs = [('w_in', 0, False), ('s5_w_glu', 0, False)]
        jobs += [(k, 0, True) for k in ('w_out', 'w_gate', 'w_up', 'w_down')]
        jobs += [(k, l, True) for l in range(1, self.depth) for k in ('w_in', 's5_w_glu', 'w_out', 'w_gate', 'w_up', 'w_down')]
        for k, l, bg in jobs:
            src = self.w[k]
            dst = self.wb[k]
            rows = src.shape[1]
            r = 0
            while r < rows:
                n = min(256, rows - r)
                self.S.dma("pool", dst[l, r:r + n, :], src[l, r:r + n, :], bg=bg)
                r += n

    def load_consts(self):
        self.ident_f, self.b_ident_f = self.alloc("ident_f", [128, 128], F32, perm=True)
        self.ident, self.b_ident = self.alloc("ident", [128, 128], BF16, perm=True)
        self.flg, self.b_flg = self.alloc("flags", [128, 2], F32, perm=True)
        self.ld(self.ident_f, self.c_ident, writes=[self.b_ident_f])
        self.ld(self.flg, self.flags, writes=[self.b_flg])
        self.dve(lambda h: h.tensor_copy(out=self.ident, in_=self.ident_f), [self.b_ident_f], [self.b_ident])
        self.mark_perm()

    def norm_transpose(self, x_tm, b_x, gT, b_gT, hT, b_hT, tmp):
        h_tm, b_h, junk, b_junk, ss, b_ss = tmp
        eps_ap, b_eps_ = self.eps_t, self.b_eps
        self.dve(lambda h: h.memset(ss, 0.0), [], [b_ss])
        for s in range(4):
            self.act(lambda h, s=s: h.activation(out=junk, in_=x_tm[:, s, :], func=AF.Square,
                                                 accum_out=ss[:, s:s + 1]), [b_x], [b_junk, b_ss])
        self.act(lambda h, eps_ap=eps_ap: h.activation(out=ss[:, 4:8], in_=ss[:, 0:4], func=AF.Sqrt, bias=eps_ap[:, 0:1],
                                                       scale=1.0 / D), [b_ss, b_eps_], [b_ss])
        self.dve(lambda h: h.reciprocal(out=ss[:, 8:12], in_=ss[:, 4:8]), [b_ss], [b_ss])
        for s in range(4):
            self.dve(lambda h, s=s: h.tensor_scalar(out=h_tm[:, s, :], in0=x_tm[:, s, :], scalar1=ss[:, 8 + s:9 + s],
                                                    scalar2=None, op0=ALU.mult), [b_x, b_ss], [b_h])
        for s in range(4):
            pt, b_pt = self.ps()
            ptb = pt[:, :].bitcast(BF16)
            for kt in range(8):
                self.pe(lambda h, s=s, kt=kt, ptb=ptb: h.transpose(ptb[:, kt * 128:(kt + 1) * 128],
                                                                    h_tm[:, s, kt * 128:(kt + 1) * 128], self.ident),
                        [b_h, self.b_ident], [b_pt])
            self.dve(lambda h, s=s, ptb=ptb: h.tensor_tensor(
                out=hT[:, :, s * 128:(s + 1) * 128], in0=ptb.rearrange("p (a b) -> p a b", a=8),
                in1=gT.unsqueeze(2).broadcast_to([128, 8, 128]), op=ALU.mult), [b_pt, b_gT], [b_hT])

    def load_cols(self, name, rows_aps):
        n = sum(a.shape[0] for a in rows_aps)
        tmp, b_tmp = self.alloc(name + "_rows", [n, 128], F32)
        r = 0
        for a in rows_aps:
            self.ld(tmp[r:r + a.shape[0], :], a, writes=[b_tmp])
            r += a.shape[0]
        dst, b_dst = self.alloc(name, [128, n], F32)
        pt, b_pt = self.ps()
        self.mm(pt[:, 0:n], tmp[0:n, :], self.ident_f[0:n, 0:n], True, True, [b_tmp, self.b_ident_f], [b_pt])
        self.dve(lambda h: h.tensor_copy(out=dst, in_=pt[:, 0:n]), [b_pt], [b_dst])
        return dst, b_dst

    def load_gT(self, name, g_ap):
        return self.load_cols(name, [g_ap.rearrange("(kt p) -> kt p", p=128)])

    def stage_A(self, l, x_src):
        W, b_W = self.alloc("W_in", [128, 8, INP], BF16)
        wsrc = self.wb['w_in'][l].rearrange("(kt p) n -> p kt n", p=128)
        for kt in range(8):
            self.ld(W[:, kt, :], wsrc[:, kt, :], writes=[b_W])
        gT, b_gT = self.load_gT("gT_mix", self.w['norm_mix_g'][l])
        self.eps_t, self.b_eps = self.alloc("eps", [128, 1], F32)
        self.MS("dve", self.eps_t, EPS, [self.b_eps])
        xs = [self.alloc("x_tm%d" % i, [128, 4, D], F32) for i in range(2)]
        h_tm, b_h = self.alloc("h_tm", [128, 4, D], BF16)
        junk, b_junk = self.alloc("junk", [128, D], BF16)
        ss, b_ss = self.alloc("ss", [128, 12], F32)
        hT, b_hT = self.alloc("hT", [128, 8, TT], BF16)
        stg_tm = [self.alloc("stg_tm%d" % i, [128, 4, 1024], BF16) for i in range(2)]
        stg_fm = [self.alloc("stg_fm%d" % i, [128, 8, TT], BF16) for i in range(2)]
        stg_dt = [self.alloc("stg_dt%d" % i, [128, 4, 8], F32) for i in range(2)]

        def load_x(t):
            x_tm, b_x = xs[t % 2]
            self.ld(x_tm, x_src[t * TT:(t + 1) * TT, :].rearrange("(s p) d -> p s d", p=128), writes=[b_x])
        load_x(0)
        for t in range(NTT):
            if t + 1 < NTT:
                load_x(t + 1)
            x_tm, b_x = xs[t % 2]
            self.norm_transpose(x_tm, b_x, gT, b_gT, hT, b_hT, (h_tm, b_h, junk, b_junk, ss, b_ss))
            s_tm, b_stm = stg_tm[t % 2]
            s_fm, b_sfm = stg_fm[t % 2]
            s_dt, b_sdt = stg_dt[t % 2]
            for s in range(4):
                for half in range(2):
                    pt, b_pt = self.ps()
                    for kt in range(8):
                        self.mm(pt[:, 0:512], hT[:, kt, s * 128:(s + 1) * 128], W[:, kt, half * 512:(half + 1) * 512],
                                kt == 0, kt == 7, [b_hT, b_W], [b_pt])
                    fn = AF.Copy if half == 0 else AF.Silu
                    self.act(lambda h, s=s, half=half, pt=pt, fn=fn, s_tm=s_tm: h.activation(
                        out=s_tm[:, s, half * 512:(half + 1) * 512], in_=pt[:, 0:512], func=fn), [b_pt], [b_stm])
            pt, b_pt = self.ps()
            for s in range(4):
                for kt in range(8):
                    self.mm(pt[:, s * 8:(s + 1) * 8], hT[:, kt, s * 128:(s + 1) * 128], W[:, kt, 2048:2056],
                            kt == 0, kt == 7, [b_hT, b_W], [b_pt])
            self.dve(lambda h, pt=pt, s_dt=s_dt: h.tensor_copy(out=s_dt, in_=pt[:, 0:32].rearrange("p (a b) -> p a b", a=4)),
                     [b_pt], [b_sdt])
            for m in range(8):
                pt, b_pt = self.ps()
                for kt in range(8):
                    self.mm(pt[:, 0:512], W[:, kt, 1024 + m * 128:1024 + (m + 1) * 128], hT[:, kt, :],
                            kt == 0, kt == 7, [b_hT, b_W], [b_pt])
                if m % 2 == 0:
                    self.dve(lambda h, m=m, pt=pt, s_fm=s_fm: h.tensor_copy(out=s_fm[:, m, :], in_=pt[:, 0:512]), [b_pt], [b_sfm])
                else:
                    self.act(lambda h, m=m, pt=pt, s_fm=s_fm: h.copy(out=s_fm[:, m, :], in_=pt[:, 0:512]), [b_pt], [b_sfm])
            self.st(self.uvz[t * TT:(t + 1) * TT, :].rearrange("(s p) c -> p s c", p=128), s_tm, reads=[b_stm])
            self.st(self.xbc[:, t * TT:(t + 1) * TT].rearrange("(m p) n -> p m n", p=128), s_fm, reads=[b_sfm])
            self.st(self.dtr[t * TT:(t + 1) * TT, :].rearrange("(s p) c -> p s c", p=128), s_dt, reads=[b_sdt])

    def stage_C(self, l, x_src, x_dst, last):
        Wo, b_Wo = self.alloc("W_out", [128, 8, D], BF16)
        self.ld(Wo, self.wb['w_out'][l].rearrange("(kt p) n -> p kt n", p=128), writes=[b_Wo])
        Wd, b_Wd = self.alloc("W_down", [128, NFF, D], BF16)
        wdsrc = self.wb['w_down'][l].rearrange("(f p) n -> p f n", p=128)
        for f0 in range(0, NFF, 6):
            f1 = min(NFF, f0 + 6)
            self.ld(Wd[:, f0:f1, :], wdsrc[:, f0:f1, :], writes=[b_Wd])
        gT, b_gT = self.load_gT("gT_ffn", self.w['norm_ffn_g'][l])
        self.eps_t, self.b_eps = self.alloc("eps", [128, 1], F32)
        self.MS("dve", self.eps_t, EPS, [self.b_eps])
        if last:
            grow, b_grow = self.alloc("g_fin", [128, D], F32)
            self.ld(grow, self.w['final_norm_g'].partition_broadcast(128), writes=[b_grow])
        xs = [self.alloc("x_tm%d" % i, [128, 4, D], F32) for i in range(2)]
        ms = [self.alloc("mixT%d" % i, [128, 8, TT], BF16) for i in range(2)]
        h_tm, b_h = self.alloc("h_tm", [128, 4, D], BF16)
        junk, b_junk = self.alloc("junk", [128, D], BF16)
        ss, b_ss = self.alloc("ss", [128, 12], F32)
        hT, b_hT = self.alloc("hT", [128, 8, TT], BF16)
        actb, b_act = self.alloc("act", [128, NFF, TT], BF16)
        sgs = [self.alloc("sg%d" % i, [128, TT], BF16) for i in range(2)]
        GW = 2
        NG = NFF // GW
        wgs = [(self.alloc("Wg%d" % i, [128, 8, GW * 128], BF16), self.alloc("Wu%d" % i, [128, 8, GW * 128], BF16))
               for i in range(2)]
        wg_src = self.wb['w_gate'][l].rearrange("(kt p) n -> p kt n", p=128)
        wu_src = self.wb['w_up'][l].rearrange("(kt p) n -> p kt n", p=128)
        self._wq = 0

        def load_w(q):
            g = q % NG
            (Wg, b_Wg), (Wu, b_Wu) = wgs[q % 2]
            self.ld(Wg, wg_src[:, :, g * GW * 128:(g + 1) * GW * 128], writes=[b_Wg])
            self.ld(Wu, wu_src[:, :, g * GW * 128:(g + 1) * GW * 128], writes=[b_Wu])

        def load_t(t):
            x_tm, b_x = xs[t % 2]
            mT, b_m = ms[t % 2]
            self.ld(x_tm, x_src[t * TT:(t + 1) * TT, :].rearrange("(s p) d -> p s d", p=128), writes=[b_x])
            self.ld(mT, self.mix[:, t * TT:(t + 1) * TT].rearrange("(kt p) n -> p kt n", p=128), writes=[b_m])
        load_t(0)
        load_w(0)
        q = 0
        for t in range(NTT):
            if t + 1 < NTT:
                load_t(t + 1)
            x_tm, b_x = xs[t % 2]
            mT, b_m = ms[t % 2]
            for s in range(4):
                for half in range(2):
                    pt, b_pt = self.ps()
                    for kt in range(8):
                        self.mm(pt[:, 0:512], mT[:, kt, s * 128:(s + 1) * 128], Wo[:, kt, half * 512:(half + 1) * 512],
                                kt == 0, kt == 7, [b_m, b_Wo], [b_pt])
                    self.dve(lambda h, s=s, half=half, pt=pt, x_tm=x_tm: h.tensor_tensor(
                        out=x_tm[:, s, half * 512:(half + 1) * 512], in0=pt[:, 0:512],
                        in1=x_tm[:, s, half * 512:(half + 1) * 512], op=ALU.add), [b_pt, b_x], [b_x])
            self.norm_transpose(x_tm, b_x, gT, b_gT, hT, b_hT, (h_tm, b_h, junk, b_junk, ss, b_ss))
            for g in range(NG):
                if not (t == NTT - 1 and g == NG - 1):
                    load_w(q + 1)
                (Wg, b_Wg), (Wu, b_Wu) = wgs[q % 2]
                for j in range(GW):
                    f = g * GW + j
                    pg, b_pg = self.ps()
                    for kt in range(8):
                        self.mm(pg[:, 0:512], Wg[:, kt, j * 128:(j + 1) * 128], hT[:, kt, :], kt == 0, kt == 7,
                                [b_hT, b_Wg], [b_pg])
                    pu, b_pu = self.ps()
                    for kt in range(8):
                        self.mm(pu[:, 0:512], Wu[:, kt, j * 128:(j + 1) * 128], hT[:, kt, :], kt == 0, kt == 7,
                                [b_hT, b_Wu], [b_pu])
                    sg, b_sg = sgs[f % 2]
                    self.act(lambda h, pg=pg, sg=sg: h.activation(out=sg, in_=pg[:, 0:512], func=AF.Silu), [b_pg], [b_sg])
                    self.dve(lambda h, f=f, pu=pu, sg=sg: h.tensor_tensor(out=actb[:, f, :], in0=pu[:, 0:512], in1=sg,
                                                                       op=ALU.mult), [b_pu, b_sg], [b_act])
                q += 1
            for s in range(4):
                for half in range(2):
                    pt, b_pt = self.ps()
                    for f in range(NFF):
                        self.mm(pt[:, 0:512], actb[:, f, s * 128:(s + 1) * 128], Wd[:, f, half * 512:(half + 1) * 512],
                                f == 0, f == NFF - 1, [b_act, b_Wd], [b_pt])
                    self.dve(lambda h, s=s, half=half, pt=pt, x_tm=x_tm: h.tensor_tensor(
                        out=x_tm[:, s, half * 512:(half + 1) * 512], in0=pt[:, 0:512],
                        in1=x_tm[:, s, half * 512:(half + 1) * 512], op=ALU.add), [b_pt, b_x], [b_x])
            if last:
                self.dve(lambda h: h.memset(ss, 0.0), [], [b_ss])
                for s in range(4):
                    self.act(lambda h, s=s, x_tm=x_tm: h.activation(out=junk, in_=x_tm[:, s, :], func=AF.Square,
                                                                    accum_out=ss[:, s:s + 1]), [b_x], [b_junk, b_ss])
                self.act(lambda h, eps_ap=self.eps_t: h.activation(out=ss[:, 4:8], in_=ss[:, 0:4], func=AF.Sqrt, bias=eps_ap[:, 0:1],
                                                                   scale=1.0 / D), [b_ss, self.b_eps], [b_ss])
                self.dve(lambda h: h.reciprocal(out=ss[:, 8:12], in_=ss[:, 4:8]), [b_ss], [b_ss])
                for s in range(4):
                    self.dve(lambda h, s=s, x_tm=x_tm: h.scalar_tensor_tensor(
                        out=x_tm[:, s, :], in0=x_tm[:, s, :], scalar=ss[:, 8 + s:9 + s], in1=grow,
                        op0=ALU.mult, op1=ALU.mult), [b_x, b_ss, b_grow], [b_x])
            self.st(x_dst[t * TT:(t + 1) * TT, :].rearrange("(s p) d -> p s d", p=128), x_tm, reads=[b_x])

    def stage_fnet(self, l):
        M1T, b_M1T = self.alloc("M1T", [64, 192], BF16)
        self.ld(M1T, self.c_m1t, writes=[b_M1T])
        E, b_E = self.alloc("E", [128, 64, 2, 128], BF16)
        for k0 in range(0, 64, 16):
            self.ld(E[:, k0:k0 + 16], self.c_E[:, k0:k0 + 16], writes=[b_E])
        cspad, b_cs = self.alloc("cspad", [64, 2, 2, 128], F32)
        self.ld(cspad, self.c_cspad, writes=[b_cs])
        wsb, b_wsb = self.alloc("fw", [64, 4, 64], F32)
        self.ld(wsb, self.w['fnet_w'][l].rearrange("g j d -> j g d"), writes=[b_wsb])
        V1, b_V1 = self.alloc("V1", [64, 128, 128], BF16)
        Y, b_Y = self.alloc("Y", [128, 128, 192], BF16)
        PQ, b_PQ = self.alloc("PQ", [128, 2, 64, 128], BF16)
        Wt, b_Wt = self.alloc("Wt", [128, 4, 128], BF16)
        Wf, b_Wf = self.alloc("Wf", [128, 2, 128], F32)
        bcol, b_bcol = self.load_cols("fb", [self.w['fnet_b'][l].rearrange("(c gl) d -> c (gl d)", gl=2)])
        stgs = [self.alloc("fstg%d" % i, [128, 4, TT], BF16) for i in range(2)]
        for ch in range(2):
            for cs in range(2):
                pt, b_pt = self.ps()
                for gl in range(2):
                    self.mm(pt[:, gl * 64:(gl + 1) * 64], cspad[:, cs, gl, :], wsb[:, 2 * ch + gl, :], True, True,
                            [b_cs, b_wsb], [b_pt])
                self.dve(lambda h, cs=cs, pt=pt: h.tensor_copy(out=Wf[:, cs, :], in_=pt[:, 0:128]), [b_pt], [b_Wf])
            for v in range(2):
                self.dve(lambda h, v=v: h.tensor_scalar(out=Wt[:, 2 * v:2 * v + 2, :], in0=Wf, scalar1=self.flg[:, v:v + 1],
                                                        scalar2=None, op0=ALU.mult), [b_Wf, self.b_flg], [b_Wt])
            self.ld(V1, self.uvz[:, 256 + ch * 128:256 + (ch + 1) * 128].rearrange("(a b) c -> a b c", a=64), writes=[b_V1])
            for c0 in range(0, 128, 2):
                pt, b_pt = self.ps()
                for j in range(2):
                    self.mm(pt[:, j * 192:(j + 1) * 192], V1[:, :, c0 + j], M1T, True, True, [b_V1, b_M1T], [b_pt])
                src = pt[:, 0:384].rearrange("p (a b) -> p a b", a=2)
                if (c0 // 2) % 2 == 0:
                    self.act(lambda h, c0=c0, src=src: h.copy(out=Y[:, c0:c0 + 2, :], in_=src), [b_pt], [b_Y])
                else:
                    self.dve(lambda h, c0=c0, src=src: h.tensor_copy(out=Y[:, c0:c0 + 2, :], in_=src), [b_pt], [b_Y])
            for k0 in range(0, 64, 4):
                pP, b_pP = self.ps()
                pQ, b_pQ = self.ps()
                for j in range(4):
                    k1 = k0 + j
                    yr, yi, ynr = Y[:, :, 3 * k1], Y[:, :, 3 * k1 + 1], Y[:, :, 3 * k1 + 2]
                    o = slice(j * 128, (j + 1) * 128)
                    self.mm(pP[:, o], yr, E[:, k1, 0, :], True, False, [b_Y, b_E], [b_pP])
                    self.mm(pP[:, o], yi, E[:, k1, 1, :], False, True, [b_Y, b_E], [b_pP])
                    self.mm(pQ[:, o], yi, E[:, k1, 0, :], True, False, [b_Y, b_E], [b_pQ])
                    self.mm(pQ[:, o], ynr, E[:, k1, 1, :], False, True, [b_Y, b_E], [b_pQ])
                self.act(lambda h, k0=k0, pP=pP: h.copy(out=PQ[:, 0, k0:k0 + 4, :], in_=pP[:, :].rearrange("p (a b) -> p a b", a=4)),
                         [b_pP], [b_PQ])
                self.dve(lambda h, k0=k0, pQ=pQ: h.tensor_copy(out=PQ[:, 1, k0:k0 + 4, :],
                                                                in_=pQ[:, :].rearrange("p (a b) -> p a b", a=4)), [b_pQ], [b_PQ])
            for t in range(NTT):
                stg, b_stg = stgs[(t // 4) % 2]
                pt, b_pt = self.ps()
                sgm, tq = t // 4, t % 4
                for i, (wi, pq, view) in enumerate(((0, 0, 0), (1, 1, 0), (2, 0, 1), (3, 1, 1))):
                    if view == 0:
                        rhs = PQ[:, pq, :, 8 * t:8 * t + 8].rearrange("p a b -> p b a")
                    else:
                        rhs = PQ[:, pq, 16 * sgm:16 * sgm + 16, 32 * tq:32 * tq + 32].rearrange("p a b -> p b a")
                    self.mm(pt[:, 0:512], Wt[:, wi, :], rhs, i == 0, i == 3, [b_Wt, b_PQ], [b_pt])
                self.act(lambda h, t=t, pt=pt, stg=stg, ch=ch: h.activation(out=stg[:, t % 4, :], in_=pt[:, 0:512], func=AF.Identity,
                                                                        bias=bcol[:, ch:ch + 1]), [b_pt, b_bcol], [b_stg])
                if t % 4 == 3:
                    t0 = t - 3
                    self.st(self.mix[256 + ch * 128:256 + (ch + 1) * 128, t0 * TT:(t0 + 4) * TT],
                            stg[:, :, :].rearrange("p a b -> p (a b)"), reads=[b_stg])

    def stage_ssd(self, l):
        import os
        NCH = 4
        tri, b_tri = self.alloc("tri", [128, 5, 128], F32)
        self.ld(tri, self.c_tri, writes=[b_tri])
        hp, b_hp = self.alloc("hp", [128, 5, 8], F32)
        for i, k in enumerate(('ssd_a_log_f', 'ssd_a_log_b', 'ssd_dt_bias_f', 'ssd_dt_bias_b', 'ssd_d')):
            self.ld(hp[:, i, :], self.w[k][l].partition_broadcast(128), writes=[b_hp])
        self.act(lambda h: h.activation(out=hp[:, 0:2, :], in_=hp[:, 0:2, :], func=AF.Exp), [b_hp], [b_hp])
        self.dve(lambda h: h.tensor_scalar(out=hp[:, 0:2, :], in0=hp[:, 0:2, :], scalar1=-1.0, scalar2=None, op0=ALU.mult),
                 [b_hp], [b_hp])
        cwb, b_cw = self.load_cols("cwb", [self.w['ssd_conv_w'][l].rearrange("k (m p) -> (k m) p", p=128),
                                           self.w['ssd_conv_b'][l].rearrange("(m p) -> m p", p=128)])
        cw = cwb[:, 0:40].rearrange("p (k m) -> p k m", k=5)
        cb, b_cb = cwb[:, 40:48], b_cw
        diagW, b_dW = self.alloc("diagW", [128, 8, 5, 128], BF16)
        for m in range(8):
            for k in range(5):
                self.dve(lambda h, m=m, k=k: h.tensor_scalar(out=diagW[:, m, k, :], in0=self.ident_f, scalar1=cw[:, k, m:m + 1],
                                                             scalar2=None, op0=ALU.mult), [b_cw, self.b_ident_f], [b_dW])
        gnorm, b_gn = self.alloc("gnorm", [128, 512], F32)
        self.ld(gnorm, self.w['ssd_norm_g'][l].partition_broadcast(128), writes=[b_gn])
        eps_t, b_eps = self.alloc("eps", [128, 1], F32)
        self.dve(lambda h: h.memset(eps_t, EPS), [], [b_eps])
        xins = [self.alloc("xin%d" % i, [128, 8, TT + 4], BF16) for i in range(2)]
        dtrs = [self.alloc("dtr%d" % i, [128, NCH, 8], F32) for i in range(2)]
        yfs = [self.alloc("yfs%d" % i, [128, NCH, 512], F32) for i in range(2)]
        szs = [self.alloc("szs%d" % i, [128, NCH, 512], BF16) for i in range(2)]
        mstg = [self.alloc("mstg%d" % i, [128, 4, TT], BF16) for i in range(2)]
        css = [self.alloc("cs%d" % i, [128, 8, TT], BF16) for i in range(2)]
        xBs = [self.alloc("sxB_tm%d" % i, [128, NCH, 768], BF16) for i in range(2)]
        sms = [self.alloc("ssm%d" % i, [128, 8, NCH * 8], F32) for i in range(2)]
        xds = [self.alloc("xd%d" % i, [128, NCH, 512], BF16) for i in range(2)]
        xdds = [self.alloc("xdd%d" % i, [128, NCH, 512], BF16) for i in range(2)]
        daTris = [self.alloc("daTri%d" % i, [128, 8, 128], BF16) for i in range(NCH)]
        Lms = [self.alloc("Lm%d" % i, [128, 8, 128], BF16) for i in range(NCH)]
        tri_bf, b_tribf = self.alloc("tri_bf", [128, 2, 128], BF16)
        self.CP("dve", tri_bf, tri[:, 2:4, :], [b_tri], [b_tribf])
        Gs = [self.alloc("G%d" % i, [128, 8, 128], BF16) for i in range(NCH)]
        CBms = [self.alloc("CBm%d" % i, [128, 2, 128], BF16) for i in range(NCH)]
        tmp, b_tmp = self.alloc("stmp", [128, 512], F32)
        tmp2, b_tmp2 = self.alloc("stmp2", [128, 512], F32)
        yb, b_yb = self.alloc("syb", [128, 512], F32)
        yn, b_yn = self.alloc("syn", [128, 512], BF16)
        H, b_H = self.alloc("H", [128, 512], F32)
        St4s = [self.alloc("St4_%d" % i, [128, NCH, 512], F32) for i in range(2)]
        Hin4s = [self.alloc("Hin4_%d" % i, [128, NCH, 512], BF16) for i in range(2)]
        junk, b_junk = self.alloc("sjunk", [128, 512], BF16)
        ssq, b_ssq = self.alloc("sssq", [128, 4], F32)
        link = self.flg[:, 0:1]
        STOP = int(os.environ.get("SSD_STOP", "99"))
        if STOP <= 0:
            return
        for _ in range(int(os.environ.get("SSD_PAD", "0"))):
            self.dve(lambda h: h.memset(junk, 0.0), [], [b_junk])
        for _ in range(int(os.environ.get("SSD_PADA", "0"))):
            self.act(lambda h: h.copy(out=junk, in_=gnorm), [b_gn], [b_junk])

        def load_pro(t, i):
            xin, b_xin = xins[i % 2]
            dtr, b_dtr = dtrs[i % 2]
            lo = t * TT - 2
            hi = t * TT + TT + 2
            src = self.xbc.rearrange("(m p) n -> p m n", p=128)
            if t == 0:
                self.dve(lambda h, xin=xin: h.memset(xin[:, :, 0:2], 0.0), [], [b_xin])
                self.ld(xin[:, :, 2:TT + 4], src[:, :, 0:hi], writes=[b_xin])
            elif t == NTT - 1:
                self.dve(lambda h, xin=xin: h.memset(xin[:, :, TT + 2:TT + 4], 0.0), [], [b_xin])
                self.ld(xin[:, :, 0:TT + 2], src[:, :, lo:NT], writes=[b_xin])
            else:
                self.ld(xin, src[:, :, lo:hi], writes=[b_xin])
            self.ld(dtr, self.dtr[t * TT:(t + 1) * TT, :].rearrange("(c p) e -> p c e", p=128), writes=[b_dtr])

        def load_chk(t, i, sweep):
            if sweep == 1:
                yf, b_yf = yfs[i % 2]
                sz, b_sz = szs[i % 2]
                self.ld(yf, self.yf_d[t * TT:(t + 1) * TT, :].rearrange("(c p) e -> p c e", p=128), writes=[b_yf])
                self.ld(sz, self.uvz[t * TT:(t + 1) * TT, 512:1024].rearrange("(c p) e -> p c e", p=128), writes=[b_sz])

        for sweep in range(int(os.environ.get("SSD_SWEEPS", "2"))):
            order = list(range(NTT)) if sweep == 0 else list(range(NTT - 1, -1, -1))
            order = order[:int(os.environ.get("SSD_TILES", "16"))]
            corder = list(range(NCH)) if sweep == 0 else list(range(NCH - 1, -1, -1))
            if sweep == 1:
                self.barrier()
            TRI_CUM = 0 if sweep == 0 else 1
            TRI_LHS = 2 if sweep == 0 else 3
            self.dve(lambda h: h.memset(H, 0.0), [], [b_H])
            def prologue(i, t):
                cs, b_cs = css[i % 2]
                xB_tm, b_xtm = xBs[i % 2]
                x_tm = xB_tm[:, :, 0:512]
                sm, b_sm = sms[i % 2]
                xd, b_xd = xds[i % 2]
                xdd, b_xdd = xdds[i % 2]
                xin, b_xin = xins[i % 2]
                dtr, b_dtr = dtrs[i % 2]
                if t % 4 == 0 and t > 0:
                    self.dve(lambda h, xin=xin: h.tensor_scalar(out=xin[:, :, 0:2], in0=xin[:, :, 0:2], scalar1=link, scalar2=None,
                                                                op0=ALU.mult), [b_xin, self.b_flg], [b_xin])
                if t % 4 == 3 and t < NTT - 1:
                    self.dve(lambda h, xin=xin: h.tensor_scalar(out=xin[:, :, TT + 2:TT + 4], in0=xin[:, :, TT + 2:TT + 4], scalar1=link,
                                                                scalar2=None, op0=ALU.mult), [b_xin, self.b_flg], [b_xin])
                for m in range(8):
                    pt, b_pt = self.ps()
                    for k in range(5):
                        self.mm(pt[:, 0:512], diagW[:, m, k, :], xin[:, m, k:k + TT], k == 0, k == 4, [b_dW, b_xin], [b_pt])
                    self.act(lambda h, m=m, pt=pt: h.activation(out=cs[:, m, :], in_=pt[:, 0:512], func=AF.Silu, bias=cb[:, m:m + 1]),
                             [b_pt, b_cb], [b_cs])
                if STOP <= 1:
                    return
                self.dbg("d_cs", cs, b_cs, [128, 8, TT], BF16)
                for c in range(NCH):
                    pt, b_pt = self.ps()
                    ptb = pt[:, :].bitcast(BF16)
                    for m in range(6):
                        self.pe(lambda h, m=m, c=c, ptb=ptb: h.transpose(ptb[:, m * 128:(m + 1) * 128], cs[:, m, c * 128:(c + 1) * 128],
                                                                        self.ident), [b_cs, self.b_ident], [b_pt])
                    if c % 2 == 0:
                        self.dve(lambda h, c=c, ptb=ptb: h.tensor_copy(out=xB_tm[:, c, :], in_=ptb[:, 0:768]), [b_pt], [b_xtm])
                    else:
                        self.act(lambda h, c=c, ptb=ptb: h.copy(out=xB_tm[:, c, :], in_=ptb[:, 0:768]), [b_pt], [b_xtm])
                if STOP <= 2:
                    return
                bias_row = hp[:, 2 + sweep, :].unsqueeze(1).broadcast_to([128, NCH, 8])
                a_row = hp[:, sweep, :].unsqueeze(1).broadcast_to([128, NCH, 8])
                v = lambda j: sm[:, j, :].rearrange("p (c e) -> p c e", c=NCH)
                self.dve(lambda h, dtr=dtr, bias_row=bias_row: h.tensor_tensor(out=v(0), in0=dtr, in1=bias_row, op=ALU.add), [b_dtr, b_hp], [b_sm])
                self.act(lambda h: h.activation(out=sm[:, 0, :], in_=sm[:, 0, :], func=AF.Exp), [b_sm], [b_sm])
                self.act(lambda h: h.activation(out=sm[:, 1, :], in_=sm[:, 0, :], func=AF.Ln, bias=1.0), [b_sm], [b_sm])
                self.dve(lambda h, a_row=a_row: h.tensor_tensor(out=v(2), in0=v(1), in1=a_row, op=ALU.mult), [b_sm, b_hp], [b_sm])
                self.dve(lambda h: h.tensor_tensor(
                    out=xd[:, :, :].rearrange("p c (e q) -> p c e q", e=8),
                    in0=x_tm[:, :, :].rearrange("p c (e q) -> p c e q", e=8),
                    in1=v(1).unsqueeze(3).broadcast_to([128, NCH, 8, 64]), op=ALU.mult), [b_xtm, b_sm], [b_xd])
                if STOP <= 3:
                    return
                self.dbg("d_xB", xB_tm, b_xtm, [128, NCH, 768], BF16)
                self.dbg("d_xd", xd, b_xd, [128, NCH, 512], BF16)
                pc, b_pc = self.ps()
                self.mm(pc[:, 0:32], tri[:, TRI_CUM, :], sm[:, 2, :], True, True, [b_tri, b_sm], [b_pc])
                self.mm(pc[:, 32:64], tri[:, 4, :], sm[:, 2, :], True, True, [b_tri, b_sm], [b_pc])
                self.dve(lambda h, pc=pc: h.tensor_copy(out=sm[:, 3, :], in_=pc[:, 0:32]), [b_pc], [b_sm])
                self.dve(lambda h, pc=pc: h.tensor_copy(out=sm[:, 0, :], in_=pc[:, 32:64]), [b_pc], [b_sm])
                self.dve(lambda h: h.tensor_tensor(out=sm[:, 7, :], in0=sm[:, 0, :], in1=sm[:, 3, :], op=ALU.subtract), [b_sm], [b_sm])
                self.act(lambda h: h.activation(out=sm[:, 4, :], in_=sm[:, 3, :], func=AF.Exp), [b_sm], [b_sm])
                self.act(lambda h: h.activation(out=sm[:, 6, :], in_=sm[:, 0, :], func=AF.Exp), [b_sm], [b_sm])
                self.act(lambda h: h.activation(out=sm[:, 5, :], in_=sm[:, 7, :], func=AF.Exp), [b_sm], [b_sm])
                self.dve(lambda h: h.tensor_tensor(
                    out=xdd[:, :, :].rearrange("p c (e q) -> p c e q", e=8),
                    in0=xd[:, :, :].rearrange("p c (e q) -> p c e q", e=8),
                    in1=v(5).unsqueeze(3).broadcast_to([128, NCH, 8, 64]), op=ALU.mult), [b_xd, b_sm], [b_xdd])
                self.dbg("d_sm", sm, b_sm, [128, 8, NCH * 8], F32)

            def chunks(i, t):
                cs, b_cs = css[i % 2]
                xB_tm, b_xtm = xBs[i % 2]
                x_tm = xB_tm[:, :, 0:512]
                B_tm = xB_tm[:, :, 512:768]
                b_Btm = b_xtm
                sm, b_sm = sms[i % 2]
                xd, b_xd = xds[i % 2]
                xdd, b_xdd = xdds[i % 2]
                if STOP <= 4:
                    return
                if sweep == 1:
                    yf, b_yf = yfs[i % 2]
                    sz, b_sz = szs[i % 2]
                    ms_, b_ms = mstg[i % 2]
                else:
                    yf, b_yf = yfs[i % 2]
                for c in corder:
                    daTri, b_daTri = daTris[c]
                    self.TT("dve", daTri, sm[:, 2, c * 8:(c + 1) * 8].unsqueeze(2).broadcast_to([128, 8, 128]),
                            tri[:, TRI_CUM, :].unsqueeze(1).broadcast_to([128, 8, 128]), ALU.mult, [b_sm, b_tri], [b_daTri])
                pcbs, pds = {}, {}
                for c in corder:
                    csl = slice(c * 128, (c + 1) * 128)
                    daTri, b_daTri = daTris[c]
                    pcb, b_pcb = self.ps()
                    for gg in range(2):
                        self.mm(pcb[:, gg * 128:(gg + 1) * 128], cs[:, 4 + gg, csl], cs[:, 6 + gg, csl], True, True, [b_cs], [b_pcb])
                    pcbs[c] = (pcb, b_pcb)
                    self.TT("dve", CBms[c][0], pcb[:, 0:256].rearrange("p (a b) -> p a b", a=2),
                            tri[:, TRI_CUM, :].unsqueeze(1).broadcast_to([128, 2, 128]), ALU.mult, [b_pcb, b_tri], [CBms[c][1]])
                    for hh in range(2):
                        pd, b_pd = self.ps()
                        self.mm(pd[:, 0:512], tri_bf[:, TRI_LHS - 2, :], daTri[:, hh * 4:(hh + 1) * 4, :].rearrange("p a b -> p (a b)"),
                                True, True, [b_tribf, b_daTri], [b_pd])
                        self.ACTF(Lms[c][0][:, hh * 4:(hh + 1) * 4, :].rearrange("p a b -> p (a b)"), pd[:, 0:512], AF.Exp, [b_pd], [Lms[c][1]])
                for c in corder:
                    G, b_G = Gs[c]
                    Lm, b_Lm = Lms[c]
                    CBm, b_CBm = CBms[c]
                    self.TT("dve", G[:, :, :].rearrange("p (g e) l -> p g e l", g=2), Lm[:, :, :].rearrange("p (g e) l -> p g e l", g=2),
                            CBm.unsqueeze(2).broadcast_to([128, 2, 4, 128]), ALU.mult, [b_Lm, b_CBm], [b_G])
                St4, b_St4 = St4s[i % 2]
                Hin4, b_Hin4 = Hin4s[i % 2]
                for c in corder:
                    pst, b_pst = self.ps()
                    for gg in range(2):
                        self.mm(pst[:, gg * 256:(gg + 1) * 256], B_tm[:, c, gg * 128:(gg + 1) * 128], xdd[:, c, gg * 256:(gg + 1) * 256],
                                True, True, [b_Btm, b_xdd], [b_pst])
                    self.CP("act", St4[:, c, :], pst[:, 0:512], [b_pst], [b_St4])
                for c in corder:
                    gc = t * NCH + c
                    self.CP("dve", Hin4[:, c, :], H, [b_H], [b_Hin4])
                    self.TT("dve", H[:, :].rearrange("p (e q) -> p e q", e=8), H[:, :].rearrange("p (e q) -> p e q", e=8),
                            sm[:, 6, c * 8:(c + 1) * 8].unsqueeze(2).broadcast_to([128, 8, 64]), ALU.mult, [b_H, b_sm], [b_H])
                    self.TT("dve", H, H, St4[:, c, :], ALU.add, [b_H, b_St4], [b_H])
                    nxt = gc + 1 if sweep == 0 else gc - 1
                    if 0 <= nxt < NT // 128 and (nxt // 16) != (gc // 16):
                        self.TS("dve", H, H, link, ALU.mult, [b_H, self.b_flg], [b_H])
                for c in corder:
                    csl = slice(c * 128, (c + 1) * 128)
                    G, b_G = Gs[c]
                    py, b_py = self.ps()
                    for e in range(8):
                        self.mm(py[:, e * 64:(e + 1) * 64], G[:, e, :], xd[:, c, e * 64:(e + 1) * 64], True, True, [b_G, b_xd], [b_py])
                    pch, b_pch = self.ps()
                    for gg in range(2):
                        self.mm(pch[:, gg * 256:(gg + 1) * 256], cs[:, 6 + gg, csl], Hin4[:, c, gg * 256:(gg + 1) * 256], True, True,
                                [b_cs, b_Hin4], [b_pch])
                    self.TT("dve", tmp[:, :].rearrange("p (e q) -> p e q", e=8), pch[:, 0:512].rearrange("p (e q) -> p e q", e=8),
                            sm[:, 4, c * 8:(c + 1) * 8].unsqueeze(2).broadcast_to([128, 8, 64]), ALU.mult, [b_pch, b_sm], [b_tmp])
                    if sweep == 0:
                        self.TT("dve", yf[:, c, :], py[:, 0:512], tmp, ALU.add, [b_py, b_tmp], [b_yf])
                    else:
                        self.TT("dve", yb, py[:, 0:512], tmp, ALU.add, [b_py, b_tmp], [b_yb])
                    if sweep == 1:
                        self.dve(lambda h, c=c, yf=yf: h.tensor_tensor(out=yb, in0=yb, in1=yf[:, c, :], op=ALU.add), [b_yb, b_yf], [b_yb])
                        self.dve(lambda h, c=c: h.tensor_tensor(
                            out=tmp2[:, :].rearrange("p (e q) -> p e q", e=8), in0=x_tm[:, c, :].rearrange("p (e q) -> p e q", e=8),
                            in1=hp[:, 4, :].unsqueeze(2).broadcast_to([128, 8, 64]), op=ALU.mult), [b_xtm, b_hp], [b_tmp2])
                        self.dve(lambda h: h.tensor_tensor(out=yb, in0=yb, in1=tmp2, op=ALU.add), [b_yb, b_tmp2], [b_yb])
                        self.dve(lambda h, c=c, sz=sz: h.tensor_tensor(out=yb, in0=yb, in1=sz[:, c, :], op=ALU.mult), [b_yb, b_sz], [b_yb])
                        self.dve(lambda h: h.memset(ssq, 0.0), [], [b_ssq])
                        self.act(lambda h: h.activation(out=junk, in_=yb, func=AF.Square, accum_out=ssq[:, 0:1]), [b_yb], [b_junk, b_ssq])
                        self.act(lambda h: h.activation(out=ssq[:, 1:2], in_=ssq[:, 0:1], func=AF.Sqrt, bias=eps_t[:, 0:1], scale=1.0 / 512),
                                 [b_ssq, b_eps], [b_ssq])
                        self.dve(lambda h: h.reciprocal(out=ssq[:, 2:3], in_=ssq[:, 1:2]), [b_ssq], [b_ssq])
                        self.dve(lambda h: h.scalar_tensor_tensor(out=yn, in0=yb, scalar=ssq[:, 2:3], in1=gnorm, op0=ALU.mult, op1=ALU.mult),
                                 [b_yb, b_ssq, b_gn], [b_yn])
                        pt, b_pt = self.ps()
                        ptb = pt[:, :].bitcast(BF16)
                        for m in range(4):
                            self.pe(lambda h, m=m, ptb=ptb: h.transpose(ptb[:, m * 128:(m + 1) * 128], yn[:, m * 128:(m + 1) * 128], self.ident),
                                    [b_yn, self.b_ident], [b_pt])
                        self.act(lambda h, c=c, ptb=ptb, ms_=ms_: h.copy(out=ms_[:, :, c * 128:(c + 1) * 128],
                                                                       in_=ptb[:, 0:512].rearrange("p (m q) -> p m q", m=4)), [b_pt], [b_ms])
                if sweep == 0:
                    self.dbg("d_yf", yf, b_yf, [128, NCH, 512], F32)
                    self.st(self.yf_d[t * TT:(t + 1) * TT, :].rearrange("(c p) e -> p c e", p=128), yf, reads=[b_yf])
                else:
                    self.st(self.mix[512:1024, t * TT:(t + 1) * TT].rearrange("(m p) n -> p m n", p=128), ms_, reads=[b_ms])

            load_pro(order[0], 0)
            if len(order) > 1:
                load_pro(order[1], 1)
            prologue(0, order[0])
            load_chk(order[0], 0, sweep)
            for i, t in enumerate(order):
                if i + 1 < len(order):
                    prologue(i + 1, order[i + 1])
                    load_chk(order[i + 1], i + 1, sweep)
                if i + 2 < len(order):
                    load_pro(order[i + 2], i + 2)
                chunks(i, t)

    def TT(self, e, out, in0, in1, op, r, w):
        return self.S.op(e, lambda h: h.tensor_tensor(out=out, in0=in0, in1=in1, op=op), r, w)

    def TS(self, e, out, in0, s1, op0, r, w, s2=None, op1=None):
        if op1 is None:
            return self.S.op(e, lambda h: h.tensor_scalar(out=out, in0=in0, scalar1=s1, scalar2=None, op0=op0), r, w)
        return self.S.op(e, lambda h: h.tensor_scalar(out=out, in0=in0, scalar1=s1, scalar2=s2, op0=op0, op1=op1), r, w)

    def STT(self, e, out, in0, scalar, in1, op0, op1, r, w):
        return self.S.op(e, lambda h: h.scalar_tensor_tensor(out=out, in0=in0, scalar=scalar, in1=in1, op0=op0, op1=op1), r, w)

    def CP(self, e, out, in_, r, w):
        if e == "act":
            return self.S.op(e, lambda h: h.copy(out=out, in_=in_), r, w)
        return self.S.op(e, lambda h: h.tensor_copy(out=out, in_=in_), r, w)

    def ACTF(self, out, in_, func, r, w, bias=None, scale=None):
        kw = {}
        if bias is not None:
            kw['bias'] = bias
        if scale is not None:
            kw['scale'] = scale
        return self.S.op("act", lambda h: h.activation(out=out, in_=in_, func=func, **kw), r, w)

    def MS(self, e, ap, val, w):
        return self.S.op(e, lambda h: h.memset(ap, val), [], w)

    def TR(self, out, in_, r, w, ident=None):
        ident = self.ident if ident is None else ident
        return self.S.op("pe", lambda h: h.transpose(out, in_, ident), r, w)

    def stage_s5(self, l):
        import os
        PI2 = 2.0 * math.pi
        NSG = NT // 8
        NCK = NSG // 4
        W_ = self.w
        ADD, MUL, SUB = ALU.add, ALU.mult, ALU.subtract
        NE = 48
        vt, b_vt = self.alloc("vt", [128, 5, 32, NE], F32)
        co, b_co = self.alloc("s5co", [128, 4, 2, 32], F32)
        Bb, b_Bb = self.alloc("Bbar", [128, 2, 2, 16, 16], F32)
        CC, b_CC = self.alloc("CC", [128, 4, 256], F32)
        Dcol, b_Dcol = self.alloc("Dcol", [128, 16], F32)
        Wg, b_Wg = self.alloc("Wglu", [128, 2, 256], BF16)
        Um, b_Um = self.alloc("Um", [128, 16, NSG], BF16)
        Hbf, b_Hbf = self.alloc("Hbf", [128, 2, 16, NCK], BF16)
        s5m, b_s5m = self.alloc("s5mask", [128, 2, 128], F32)
        bglu, b_bglu = self.load_cols("bglu", [W_['s5_b_glu'][l].rearrange("(h p) -> h p", p=128)])
        mark = self.a_off
        iota, b_iota = self.alloc("iota", [128, 48], F32)
        self.ld(iota, self.c_iota, writes=[b_iota])
        self.ld(s5m, self.c_s5mask, writes=[b_s5m])
        Lr, b_Lr = self.alloc("lamrows", [64, 128], F32)
        for i, k in enumerate(('s5_lam_re_f', 's5_lam_im_f', 's5_lam_re_b', 's5_lam_im_b')):
            for hf in range(2):
                self.ld(Lr[i * 16:(i + 1) * 16, hf * 64:(hf + 1) * 64], W_[k][l], writes=[b_Lr])
        LAM, b_LAM = self.alloc("LAM", [128, 64], F32)
        pt, b_pt = self.ps()
        self.mm(pt[:, 0:64], Lr, self.ident_f[0:64, 0:64], True, True, [b_Lr, self.b_ident_f], [b_pt])
        self.CP("dve", LAM, pt[:, 0:64], [b_pt], [b_LAM])
        LAMv = LAM.rearrange("p (d r g) -> p d r g", d=2, r=2)
        sc, b_sc = self.alloc("s5sc", [128, 12, 32], F32)
        stp = sc[:, 0, :]
        self.ld(sc[:, 0, 0:16], W_['s5_log_step_f'][l].partition_broadcast(128), writes=[b_sc])
        self.ld(sc[:, 0, 16:32], W_['s5_log_step_b'][l].partition_broadcast(128), writes=[b_sc])
        self.ACTF(stp, stp, AF.Exp, [b_sc], [b_sc])
        v32 = lambda j: sc[:, j, :]
        v216 = lambda j: sc[:, j, :].rearrange("p (d g) -> p d g", d=2)
        self.CP("dve", v216(1), LAMv[:, :, 0, :], [b_LAM], [b_sc])
        self.CP("dve", v216(2), LAMv[:, :, 1, :], [b_LAM], [b_sc])
        self.TT("dve", v32(3), v32(1), stp, MUL, [b_sc], [b_sc])
        self.TT("dve", v32(4), v32(2), stp, MUL, [b_sc], [b_sc])
        tw, b_tw = self.alloc("tw", [128, 4, 32, NE], F32)
        ki, b_ki = self.alloc("ki", [128, 32, NE], I32)
        io3 = iota.unsqueeze(1).broadcast_to([128, 32, NE])
        lam_i3 = v32(4).unsqueeze(2).broadcast_to([128, 32, NE])
        lam_r3 = v32(3).unsqueeze(2).broadcast_to([128, 32, NE])
        self.TT("dve", tw[:, 2], io3, lam_i3, MUL, [b_iota, b_sc], [b_tw])
        for which, off in ((1, 0.0), (0, math.pi / 2)):
            if off != 0.0:
                self.TS("dve", tw[:, 2], tw[:, 2], off, ADD, [b_tw], [b_tw])
            self.TS("dve", ki, tw[:, 2], 1.0 / PI2, MUL, [b_tw], [b_ki])
            self.CP("dve", tw[:, 3], ki, [b_ki], [b_tw])
            self.STT("dve", tw[:, 3], tw[:, 3], -PI2, tw[:, 2], MUL, ADD, [b_tw], [b_tw])
            self.ACTF(tw[:, which], tw[:, 3], AF.Sin, [b_tw], [b_tw])
        self.TT("dve", tw[:, 2], io3, lam_r3, MUL, [b_iota, b_sc], [b_tw])
        self.ACTF(tw[:, 2], tw[:, 2], AF.Exp, [b_tw], [b_tw])
        self.TT("dve", tw[:, 0], tw[:, 0], tw[:, 2], MUL, [b_tw], [b_tw])
        self.TT("dve", tw[:, 1], tw[:, 1], tw[:, 2], MUL, [b_tw], [b_tw])
        lo, hi = slice(0, 64), slice(64, 128)
        for var, (slo, sglo, shi, sghi) in enumerate(((0, 1, 1, 1), (1, -1, 0, 1), (0, -1, 1, -1), (0, 1, 1, -1), (1, -1, 0, -1))):
            self.TS("dve", vt[lo, var], tw[lo, slo], float(sglo), MUL, [b_tw], [b_vt])
            self.ACTF(vt[hi, var], tw[hi, shi], AF.Copy, [b_tw], [b_vt], scale=float(sghi))
        wr1, wi1 = tw[:, 0, :, 8], tw[:, 1, :, 8]
        self.TS("dve", v32(5), wr1, -1.0, ADD, [b_tw], [b_sc])
        self.TT("dve", v32(6), v32(1), v32(1), MUL, [b_sc], [b_sc])
        self.TT("dve", v32(7), v32(2), v32(2), MUL, [b_sc], [b_sc])
        self.TT("dve", v32(6), v32(6), v32(7), ADD, [b_sc], [b_sc])
        self.S.op("dve", lambda h: h.reciprocal(out=v32(6), in_=v32(6)), [b_sc], [b_sc])
        self.TT("dve", v32(7), v32(5), v32(1), MUL, [b_sc], [b_sc])
        self.TT("dve", v32(8), wi1, v32(2), MUL, [b_sc, b_tw], [b_sc])
        self.TT("dve", v32(7), v32(7), v32(8), ADD, [b_sc], [b_sc])
        self.TT("dve", v32(7), v32(7), v32(6), MUL, [b_sc], [b_sc])
        self.TT("dve", v32(8), wi1, v32(1), MUL, [b_sc, b_tw], [b_sc])
        self.TT("dve", v32(9), v32(5), v32(2), MUL, [b_sc], [b_sc])
        self.TT("dve", v32(8), v32(8), v32(9), SUB, [b_sc], [b_sc])
        self.TT("dve", v32(8), v32(8), v32(6), MUL, [b_sc], [b_sc])
        ar, ai = tw[:, 0, :, 39], tw[:, 1, :, 39]
        for f in range(2):
            self.CP("dve", co[:, 0, f, :], ar, [b_tw], [b_co])
            self.TS("dve", co[:, 1, f, :], ai, 1.0 if f == 0 else -1.0, MUL, [b_tw], [b_co])
        self.TS("dve", co[:, 2:4], co[:, 0:2], self.flg[:, 0:1], MUL, [b_co, self.b_flg], [b_co])
        Bri, b_Bri = self.alloc("Bri", [128, 2, 16, 16], F32)
        for i, k in enumerate(('s5_b_re', 's5_b_im')):
            for hf in range(2):
                self.ld(Bri[hf * 64:(hf + 1) * 64, i], W_[k][l].rearrange("g p c -> p g c"), writes=[b_Bri])
        tB, b_tB = self.alloc("tB", [128, 2, 16, 16], F32)
        qr4 = v216(7).unsqueeze(3).broadcast_to([128, 2, 16, 16])
        qi4 = v216(8).unsqueeze(3).broadcast_to([128, 2, 16, 16])
        bre4 = Bri[:, 0].unsqueeze(1).broadcast_to([128, 2, 16, 16])
        bim4 = Bri[:, 1].unsqueeze(1).broadcast_to([128, 2, 16, 16])
        self.TT("dve", Bb[:, 0], qr4, bre4, MUL, [b_sc, b_Bri], [b_Bb])
        self.TT("dve", tB, qi4, bim4, MUL, [b_sc, b_Bri], [b_tB])
        self.TT("dve", Bb[:, 0], Bb[:, 0], tB, SUB, [b_Bb, b_tB], [b_Bb])
        self.TT("dve", Bb[:, 1], qr4, bim4, MUL, [b_sc, b_Bri], [b_Bb])
        self.TT("dve", tB, qi4, bre4, MUL, [b_sc, b_Bri], [b_tB])
        self.TT("dve", Bb[:, 1], Bb[:, 1], tB, ADD, [b_Bb, b_tB], [b_Bb])
        crow = [self.alloc("crow%d" % i, [128, 128], F32) for i in range(2)]
        n = 0
        for i, k in enumerate(('s5_c_re_f', 's5_c_im_f', 's5_c_re_b', 's5_c_im_b')):
            src = W_[k][l].rearrange("g c p -> (g c) p")
            for hf in range(2):
                cr_, b_cr = crow[n % 2]
                n += 1
                for rep in range(2):
                    self.ld(cr_[:, rep * 64:(rep + 1) * 64], src[hf * 128:(hf + 1) * 128, :], writes=[b_cr])
                pt, b_pt = self.ps()
                self.mm(pt[:, 0:128], cr_, self.ident_f, True, True, [b_cr, self.b_ident_f], [b_pt])
                self.CP("dve", CC[:, i, hf * 128:(hf + 1) * 128], pt[:, 0:128], [b_pt], [b_CC])
        Drep, b_Drep = self.alloc("Drep", [16, 8, 16], F32)
        self.ld(Drep, W_['s5_d'][l].rearrange("(g c) -> g c", c=16).unsqueeze(1).broadcast_to([16, 8, 16]), writes=[b_Drep])
        pt, b_pt = self.ps()
        self.mm(pt[:, 0:16], Drep[:, :, :].rearrange("g a c -> g (a c)"), self.ident_f[0:16, 0:16], True, True, [b_Drep, self.b_ident_f], [b_pt])
        self.CP("dve", Dcol, pt[:, 0:16], [b_pt], [b_Dcol])
        self.ld(Wg, self.wb['s5_w_glu'][l].rearrange("(k p) n -> p k n", p=128), writes=[b_Wg])
        self.barrier()
        self.a_off = mark
        uts = [self.alloc("UT%d" % i, [128, 8, 256], BF16) for i in range(2)]
        utgs = [self.alloc("UTg%d" % i, [128, 16, 128], BF16) for i in range(2)]
        for j in range(NSG // 128):
            UT, b_UT = uts[j % 2]
            self.ld(UT, self.uvz[j * 1024:(j + 1) * 1024, 0:256].rearrange("(p s) c -> p s c", s=8), writes=[b_UT])
            UTg, b_UTg = utgs[j % 2]
            self.CP("dve" if j % 2 == 0 else "act", UTg[:, :, :].rearrange("p g (s c) -> p g s c", s=8),
                    UT[:, :, :].rearrange("p s (g c) -> p g s c", g=16), [b_UT], [b_UTg])
            for hb in range(2):
                pt, b_pt = self.ps()
                ptb = pt[:, :].bitcast(BF16)
                for gg in range(8):
                    g = hb * 8 + gg
                    self.TR(ptb[:, gg * 128:(gg + 1) * 128], UTg[:, g, :], [b_UTg, self.b_ident], [b_pt])
                self.CP("dve" if hb == 0 else "act", Um[:, hb * 8:(hb + 1) * 8, j * 128:(j + 1) * 128],
                        ptb.rearrange("p (g s) -> p g s", g=8), [b_pt], [b_Um])
        X, b_X = self.alloc("X", [128, 2, 2, 16, NCK], F32)
        mark2 = self.a_off
        self.MS("dve", X[:, :, :, :, 0:1], 0.0, [b_X])
        VinTs = [self.alloc("VinT%d" % i, [128, 4, 4, 128], BF16) for i in range(2)]
        Vins = [self.alloc("Vin%d" % i, [128, 4, 4, 128], BF16) for i in range(2)]
        tgs = [[self.alloc("tg%d_%d" % (k, i), [128, 4, 8, 16], F32) for i in range(2)] for k in range(2)]
        cur = {"tg": tgs[0]}

        def tbl(var, d, g):
            return vt[:, var, d * 16 + g, :]

        def cplx_tile(out4, xr, xi, ta, tb_, eng="dve"):
            A, S_ = out4.shape[1], out4.shape[2]
            (t0, b_t0), (t1, b_t1) = cur["tg"]
            t0v, t1v = t0[:, 0:A, 0:S_, :], t1[:, 0:A, 0:S_, :]
            xr4 = xr.unsqueeze(1).unsqueeze(1).broadcast_to([128, A, S_, 16])
            xi4 = xi.unsqueeze(1).unsqueeze(1).broadcast_to([128, A, S_, 16])
            ta4 = ta.unsqueeze(3).broadcast_to([128, A, S_, 16])
            tb4 = tb_.unsqueeze(3).broadcast_to([128, A, S_, 16])
            self.TT(eng, t0v, xr4, ta4, MUL, [b_Bb, b_CC, b_vt], [b_t0])
            self.TT(eng, t1v, xi4, tb4, MUL, [b_Bb, b_CC, b_vt], [b_t1])
            return t0v, t1v, b_t0, b_t1

        def e_fwd(tb_, j0, n_):
            return tb_[:, j0:j0 + n_].rearrange("p (a s) -> p a s", s=8)

        def e_rev(tb_, j0, n_):
            return tb_[:, j0:j0 + n_][:, ::-1].rearrange("p (a s) -> p a s", s=8)

        G_LIST = list(range(int(os.environ.get("S5_GROUPS", "16"))))
        for g in G_LIST:
            VinT, b_VinT = VinTs[g % 2]
            Vin, b_Vin = Vins[g % 2]
            cur["tg"] = tgs[g % 2]
            for d in range(2):
                xr, xi = Bb[:, 0, d, g, :], Bb[:, 1, d, g, :]
                ev = e_rev if d == 0 else e_fwd
                for form, (va, vb) in enumerate(((0, 1), (1, 2))):
                    t0v, t1v, b_t0, b_t1 = cplx_tile(VinT[:, form * 2 + d].rearrange("p a (s c) -> p a s c", s=8), xr, xi,
                                                     ev(tbl(va, d, g), 7, 32), ev(tbl(vb, d, g), 7, 32))
                    self.TT("dve", VinT[:, form * 2 + d].rearrange("p a (s c) -> p a s c", s=8), t0v, t1v, ADD, [b_t0, b_t1], [b_VinT])
            for hb in range(2):
                pt, b_pt = self.ps()
                ptb = pt[:, :].bitcast(BF16)
                for k in range(8):
                    fd, a = (hb * 8 + k) // 4, (hb * 8 + k) % 4
                    self.TR(ptb[:, k * 128:(k + 1) * 128], VinT[:, fd, a, :], [b_VinT, self.b_ident], [b_pt])
                self.CP("act", Vin[:, hb * 2:(hb + 1) * 2].rearrange("p f a c -> p (f a c)"), ptb, [b_pt], [b_Vin])
            Umg = Um[:, g, :].rearrange("p (c a) -> p c a", a=4)
            for form in range(2):
                for d in range(2):
                    pt, b_pt = self.ps()
                    for a in range(4):
                        self.mm(pt[:, 0:NCK], Vin[:, form * 2 + d, a, :], Umg[:, :, a], a == 0, a == 3, [b_Vin, b_Um], [b_pt])
                    src = pt[:, 0:NCK - 1] if d == 0 else pt[:, 0:NCK][:, ::-1][:, 0:NCK - 1]
                    self.CP("dve" if d == 0 else "act", X[:, form, d, g, 1:NCK], src, [b_pt], [b_X])
        self.barrier()
        self.a_off = mark2
        Xs = X[:, :, :, :, :].rearrange("p f d g (s k) -> p f (d g) s k", s=4)
        ts4, b_ts4 = self.alloc("scan_t4", [128, 2, 32, 4], F32)
        co0s = co[:, 0].unsqueeze(3).broadcast_to([128, 2, 32, 4])
        co1s = co[:, 1].unsqueeze(3).broadcast_to([128, 2, 32, 4])
        for k in range(1, 64):
            self.TT("dve", ts4, Xs[:, :, :, :, k - 1], co0s, MUL, [b_X, b_co], [b_ts4])
            self.TT("dve", Xs[:, :, :, :, k], Xs[:, :, :, :, k], ts4, ADD, [b_X, b_ts4], [b_X])
            self.TT("dve", ts4, Xs[:, ::-1, :, :, k - 1], co1s, MUL, [b_X, b_co], [b_ts4])
            self.TT("dve", Xs[:, :, :, :, k], Xs[:, :, :, :, k], ts4, ADD, [b_X, b_ts4], [b_X])
        PW, b_PW = self.alloc("PW", [128, 2, 32, 64], F32)
        pm, b_pm = self.alloc("pm", [128, 4, 32], F32)
        pt_, b_pt_ = self.alloc("pwt", [128, 2, 32, 32], F32)
        ar_, ai_ = co[:, 0, 0, :], co[:, 1, 0, :]
        self.MS("dve", PW[:, 0, :, 0:1], 1.0, [b_PW])
        self.MS("dve", PW[:, 1, :, 0:1], 0.0, [b_PW])
        m = 1
        while m < 64:
            lr_, li_ = PW[:, 0, :, m - 1], PW[:, 1, :, m - 1]
            self.TT("dve", pm[:, 0], lr_, ar_, MUL, [b_PW, b_co], [b_pm])
            self.TT("dve", pm[:, 2], li_, ai_, MUL, [b_PW, b_co], [b_pm])
            self.TT("dve", pm[:, 0], pm[:, 0], pm[:, 2], SUB, [b_pm], [b_pm])
            self.TT("dve", pm[:, 1], lr_, ai_, MUL, [b_PW, b_co], [b_pm])
            self.TT("dve", pm[:, 3], li_, ar_, MUL, [b_PW, b_co], [b_pm])
            self.TT("dve", pm[:, 1], pm[:, 1], pm[:, 3], ADD, [b_pm], [b_pm])
            qr_ = pm[:, 0].unsqueeze(2).broadcast_to([128, 32, m])
            qi_ = pm[:, 1].unsqueeze(2).broadcast_to([128, 32, m])
            pr0, pi0 = PW[:, 0, :, 0:m], PW[:, 1, :, 0:m]
            self.TT("dve", PW[:, 0, :, m:2 * m], pr0, qr_, MUL, [b_PW, b_pm], [b_PW])
            self.TT("dve", pt_[:, 0, :, 0:m], pi0, qi_, MUL, [b_PW, b_pm], [b_pt_])
            self.TT("dve", PW[:, 0, :, m:2 * m], PW[:, 0, :, m:2 * m], pt_[:, 0, :, 0:m], SUB, [b_PW, b_pt_], [b_PW])
            self.TT("dve", PW[:, 1, :, m:2 * m], pr0, qi_, MUL, [b_PW, b_pm], [b_PW])
            self.TT("dve", pt_[:, 1, :, 0:m], pi0, qr_, MUL, [b_PW, b_pm], [b_pt_])
            self.TT("dve", PW[:, 1, :, m:2 * m], PW[:, 1, :, m:2 * m], pt_[:, 1, :, 0:m], ADD, [b_PW, b_pt_], [b_PW])
            m *= 2
        dl, b_dl = self.alloc("sdelta", [128, 2, 32], F32)
        d2, b_d2 = self.alloc("sdelta2", [128, 2, 32], F32)
        fx, b_fx = pt_[:, :, :, :].rearrange("p a b c -> p (a b c)").rearrange("p (b c) -> p b c", b=32), b_pt_
        for sg in range(1, 4):
            hp_ = Xs[:, :, :, sg - 1, 63]
            hps = Xs[:, ::-1, :, sg - 1, 63]
            x0 = Xs[:, :, :, sg, 0]
            self.TT("dve", dl, hp_, co[:, 0], MUL, [b_X, b_co], [b_dl])
            self.TT("dve", d2, hps, co[:, 1], MUL, [b_X, b_co], [b_d2])
            self.TT("dve", dl, dl, d2, ADD, [b_dl, b_d2], [b_dl])
            self.TT("dve", dl, dl, x0, ADD, [b_dl, b_X], [b_dl])
            self.TS("dve", dl, dl, self.flg[:, 0:1], MUL, [b_dl, self.b_flg], [b_dl])
            self.TT("dve", dl, dl, x0, SUB, [b_dl, b_X], [b_dl])
            self.CP("dve", d2[:, 0], dl[:, 1], [b_dl], [b_d2])
            self.TS("dve", d2[:, 1], dl[:, 0], -1.0, MUL, [b_dl], [b_d2])
            for f in range(2):
                dl3 = dl[:, f].unsqueeze(2).broadcast_to([128, 32, 64])
                d23 = d2[:, f].unsqueeze(2).broadcast_to([128, 32, 64])
                xseg = Xs[:, f, :, sg, :]
                self.TT("dve", fx, PW[:, 0], dl3, MUL, [b_PW, b_dl], [b_fx])
                self.TT("dve", xseg, xseg, fx, ADD, [b_X, b_fx], [b_X])
                self.TT("dve", fx, PW[:, 1], d23, MUL, [b_PW, b_d2], [b_fx])
                self.TT("dve", xseg, xseg, fx, ADD, [b_X, b_fx], [b_X])
        self.CP("dve", Hbf[:, 0], X[:, 0, 0], [b_X], [b_Hbf])
        self.CP("act", Hbf[:, 1], X[:, 0, 1, :, ::-1], [b_X], [b_Hbf])
        self.barrier()
        self.a_off = mark2
        tgs = [[self.alloc("tgo%d_%d" % (k, i), [128, 4, 8, 16], F32) for i in range(2)] for k in range(2)]
        Yg, b_Yg = X[:, :, :, :, :].rearrange("p f d g c -> p (f d g c)")[:, 0:8192].bitcast(BF16).rearrange("p (g s) -> p g s", g=16), Buf("Yg")
        CoutTs = [self.alloc("Cout%d" % i, [128, 2, 4, 128], BF16) for i in range(2)]
        LBs = [self.alloc("LB%d" % i, [128, 2, 128], BF16) for i in range(2)]
        RCs = [self.alloc("RC%d" % i, [128, 2, 4, 128], BF16) for i in range(2)]
        Wts = [self.alloc("Wtoep%d" % i, [128, 7, 128], BF16) for i in range(2)]
        w0ts = [self.alloc("w0t%d" % i, [128, 2, 128], F32) for i in range(2)]
        yts = [self.alloc("ytmp%d" % i, [128, 512], F32) for i in range(2)]
        t2s = [self.alloc("yt2%d" % i, [128, 512], F32) for i in range(2)]
        for g in G_LIST:
            CoutT, b_Cout = CoutTs[g % 2]
            LB, b_LB = LBs[g % 2]
            RC, b_RC = RCs[g % 2]
            Wt, b_Wt = Wts[g % 2]
            w0t, b_w0t = w0ts[g % 2]
            cur["tg"] = tgs[g % 2]
            for d in range(2):
                cr_, ci_ = CC[:, 2 * d, g * 16:(g + 1) * 16], CC[:, 2 * d + 1, g * 16:(g + 1) * 16]
                br_, bi_ = Bb[:, 0, d, g, :], Bb[:, 1, d, g, :]
                ev = e_fwd if d == 0 else e_rev
                t0v, t1v, b_t0, b_t1 = cplx_tile(CoutT[:, d].rearrange("p a (s c) -> p a s c", s=8), cr_, ci_,
                                                 ev(tbl(3, d, g), 8, 32), ev(tbl(4, d, g), 8, 32))
                self.TT("dve", CoutT[:, d].rearrange("p a (s c) -> p a s c", s=8), t0v, t1v, ADD, [b_t0, b_t1], [b_Cout])
                ta = e_rev(tbl(0, d, g), 0, 8) if d == 0 else e_fwd(tbl(0, d, g), 7, 8)
                tb_ = e_rev(tbl(1, d, g), 0, 8) if d == 0 else e_fwd(tbl(1, d, g), 7, 8)
                t0v, t1v, b_t0, b_t1 = cplx_tile(LB[:, d].rearrange("p (a s c) -> p a s c", a=1, s=8), br_, bi_, ta, tb_)
                self.TT("dve", LB[:, d].rearrange("p (a s c) -> p a s c", a=1, s=8), t0v, t1v, ADD, [b_t0, b_t1], [b_LB])
                if d == 0:
                    ta, tb_ = e_fwd(tbl(3, d, g), 7, 32), e_fwd(tbl(4, d, g), 7, 32)
                else:
                    ta = tbl(3, d, g)[:, 0:32].rearrange("p (k s) -> p k s", s=8)[:, :, ::-1]
                    tb_ = tbl(4, d, g)[:, 0:32].rearrange("p (k s) -> p k s", s=8)[:, :, ::-1]
                t0v, t1v, b_t0, b_t1 = cplx_tile(RC[:, d].rearrange("p a (s c) -> p a s c", s=8), cr_, ci_, ta, tb_)
                self.TT("dve", RC[:, d].rearrange("p a (s c) -> p a s c", s=8), t0v, t1v, ADD, [b_t0, b_t1], [b_RC])
            pF, b_pF = self.ps()
            pB, b_pB = self.ps()
            for k in range(4):
                self.mm(pF[:, k * 128:(k + 1) * 128], LB[:, 0, :], RC[:, 0, k, :], True, True, [b_LB, b_RC], [b_pF])
                self.mm(pB[:, k * 128:(k + 1) * 128], LB[:, 1, :], RC[:, 1, k, :], True, True, [b_LB, b_RC], [b_pB])
            self.CP("dve", Wt[:, 4:7, :], pF[:, 128:512].rearrange("p (k c) -> p k c", k=3), [b_pF], [b_Wt])
            self.CP("act", Wt[:, 0:3, :][:, ::-1, :], pB[:, 128:512].rearrange("p (k c) -> p k c", k=3), [b_pB], [b_Wt])
            self.TT("dve", w0t[:, 0, :], pF[:, 0:128], s5m[:, 0, :], MUL, [b_pF, b_s5m], [b_w0t])
            self.CP("act", w0t[:, 1, :], pB[:, 0:128], [b_pB], [b_w0t])
            self.TT("dve", w0t[:, 1, :], w0t[:, 1, :], s5m[:, 1, :], MUL, [b_w0t, b_s5m], [b_w0t])
            self.TT("dve", Wt[:, 3, :], w0t[:, 0, :], w0t[:, 1, :], ADD, [b_w0t], [b_Wt])
            for bnk in range(NSG // 512):
                pt, b_pt = self.ps()
                psv = pt[:, 0:512].rearrange("p (c a) -> p c a", a=4)
                Umv = Um[:, g, bnk * 512:(bnk + 1) * 512].rearrange("p (c a) -> p c a", a=4)
                self.mm(pt[:, 0:512], Wt[:, 3, :], Um[:, g, bnk * 512:(bnk + 1) * 512], True, False, [b_Wt, b_Um], [b_pt])
                for dl in (-3, -2, -1, 1, 2, 3):
                    a_lo, a_hi = max(0, dl), min(3, 3 + dl)
                    self.mm(psv[:, :, a_lo:a_hi + 1], Wt[:, dl + 3, :], Umv[:, :, a_lo - dl:a_hi - dl + 1], False, False,
                            [b_Wt, b_Um], [b_pt])
                for d in range(2):
                    for a in range(4):
                        self.mm(psv[:, :, a], CoutT[:, d, a, :], Hbf[:, d, g, bnk * 128:(bnk + 1) * 128], False, (d == 1 and a == 3),
                                [b_Cout, b_Hbf], [b_pt])
                yt, b_yt = yts[bnk % 2]
                t2, b_t2 = t2s[bnk % 2]
                usl = Um[:, g, bnk * 512:(bnk + 1) * 512]
                self.STT("dve", yt, usl, Dcol[:, g:g + 1], pt[:, 0:512], MUL, ADD, [b_Um, b_Dcol, b_pt], [b_yt])
                self.ACTF(t2, yt, AF.Square, [b_yt], [b_t2])
                self.TS("dve", t2, t2, 0.044715, MUL, [b_t2], [b_t2], 1.0, ADD)
                self.TT("dve", t2, t2, yt, MUL, [b_t2, b_yt], [b_t2])
                self.ACTF(t2, t2, AF.Sigmoid, [b_t2], [b_t2], scale=1.5957691216)
                self.TT("dve", Yg[:, g, bnk * 512:(bnk + 1) * 512], t2, yt, MUL, [b_t2, b_yt], [b_Yg])
        self.barrier()
        self.a_off = mark2
        Gfm, b_Gfm = X[:, :, :, :, :].rearrange("p f d g c -> p (f d g c)")[:, 8192:16384].bitcast(BF16).rearrange("p (h n) -> p h n", h=2), Buf("Gfm")
        yTs = [self.alloc("yT%d" % i, [128, 8, 256], BF16) for i in range(2)]
        for j in range(NSG // 128):
            yT, b_yT = yTs[j % 2]
            for hb in range(2):
                pt, b_pt = self.ps()
                ptb = pt[:, :].bitcast(BF16)
                for gg in range(8):
                    self.TR(ptb[:, gg * 128:(gg + 1) * 128], Yg[:, hb * 8 + gg, j * 128:(j + 1) * 128], [b_Yg, self.b_ident], [b_pt])
                self.CP("dve" if hb == 0 else "act",
                        yT[:, :, hb * 128:(hb + 1) * 128].rearrange("p l (g c) -> p g l c", g=8),
                        ptb.rearrange("p (g l c) -> p g l c", g=8, l=8), [b_pt], [b_yT])
            for hb in range(2):
                pt, b_pt = self.ps()
                ptb = pt[:, :].bitcast(BF16)
                for ll in range(8):
                    self.TR(ptb[:, ll * 128:(ll + 1) * 128], yT[:, ll, hb * 128:(hb + 1) * 128], [b_yT, self.b_ident], [b_pt])
                self.CP("dve" if hb == 0 else "act",
                        Gfm[:, hb, j * 1024:(j + 1) * 1024].rearrange("p (s l) -> p l s", l=8),
                        ptb.rearrange("p (l s) -> p l s", l=8), [b_pt], [b_Gfm])
        ostg = [self.alloc("s5o%d" % i, [128, 2, TT], BF16) for i in range(2)]
        sgt = [self.alloc("s5sg%d" % i, [128, TT], BF16) for i in range(2)]
        for t in range(NTT):
            og, b_og = ostg[t % 2]
            for oh in range(2):
                pt, b_pt = self.ps()
                for kh in range(2):
                    self.mm(pt[:, 0:512], Wg[:, kh, oh * 128:(oh + 1) * 128], Gfm[:, kh, t * TT:(t + 1) * TT], kh == 0, kh == 1,
                            [b_Wg, b_Gfm], [b_pt])
                sg, b_sg = sgt[oh]
                self.ACTF(sg, pt[:, 0:512], AF.Sigmoid, [b_pt, b_bglu], [b_sg], bias=bglu[:, oh:oh + 1])
                self.TT("dve", og[:, oh, :], sg, Gfm[:, oh, t * TT:(t + 1) * TT], MUL, [b_sg, b_Gfm], [b_og])
            self.st(self.mix[0:256, t * TT:(t + 1) * TT].rearrange("(h p) n -> p h n", p=128), og, reads=[b_og])


def host_consts(link):
    c = {}
    c['c_ident'] = np.eye(128, dtype=np.float32)
    fl = np.zeros((128, 2), np.float32)
    fl[:, 0] = link
    fl[:, 1] = 1.0 - link
    c['flags'] = fl
    bf = ml_dtypes.bfloat16
    l1 = np.arange(64)
    k1 = np.arange(64)
    if link > 0.5:
        th = 2 * np.pi * np.outer(l1, k1) / 64.0
        blk = np.ones((64, 64))
        Lseq = 8192.0
        kk = k1[:, None] + 64 * np.arange(128)[None, :]
    else:
        th = 2 * np.pi * np.outer(l1 % 16, k1 % 16) / 16.0
        blk = (l1[:, None] // 16 == k1[None, :] // 16).astype(np.float64)
        Lseq = 2048.0
        kk = (k1 % 16)[:, None] + 16 * np.arange(128)[None, :]
    m1t = np.stack([np.cos(th) * blk, -np.sin(th) * blk, -np.cos(th) * blk], axis=-1)
    c['c_m1t'] = np.ascontiguousarray(m1t.reshape(64, 192)).astype(bf)
    l2 = np.arange(128)
    ph = 2 * np.pi * (l2[:, None, None] * kk[None, :, :] % Lseq) / Lseq
    E = np.stack([np.cos(ph), np.sin(ph)], axis=2) / np.sqrt(Lseq)
    c['c_E'] = np.ascontiguousarray(E).astype(bf)
    j = np.arange(64)
    a = 2 * np.pi * np.outer(j, j) / 64.0
    cs = np.stack([np.cos(a), np.sin(a)], axis=0) / 8.0
    pad = np.zeros((64, 2, 2, 128), np.float32)
    for gl in range(2):
        pad[:, 0, gl, gl * 64:(gl + 1) * 64] = cs[0]
        pad[:, 1, gl, gl * 64:(gl + 1) * 64] = cs[1]
    c['c_cspad'] = pad
    r = np.arange(128)
    tri = np.zeros((128, 5, 128), np.float32)
    tri[:, 0, :] = r[:, None] <= r[None, :]
    tri[:, 1, :] = r[:, None] >= r[None, :]
    tri[:, 2, :] = r[:, None] > r[None, :]
    tri[:, 3, :] = r[:, None] < r[None, :]
    tri[:, 4, :] = 1.0
    c['c_tri'] = tri
    c['c_iota'] = np.tile(np.arange(-7, 41, dtype=np.float32)[None, :], (128, 1))
    sp = np.arange(128) // 16
    m = np.zeros((128, 2, 128), np.float32)
    m[:, 0, :] = sp[None, :] >= sp[:, None]
    m[:, 1, :] = sp[None, :] <= sp[:, None]
    c['c_s5mask'] = m
    return c


def build_program(depth=DEPTH, debug_outputs=()):
    P = Prog(debug_outputs=debug_outputs, depth=depth)
    P.declare()
    P.arena_init()
    P.prep_weights()
    P.load_consts()
    x_src = P.x_in
    for l in range(depth):
        last = (l == depth - 1)
        import os
        stg = os.environ.get("K_STAGES", "A,S5,F,SSD,C").split(",")
        bgw = (l > 0)
        if "A" in stg:
            P.stage_reset(bgw)
            P.stage_A(l, x_src)
        if "S5" in stg:
            P.stage_reset(bgw)
            P.stage_s5(l)
        if "F" in stg:
            P.stage_reset(bgw)
            P.stage_fnet(l)
        if "SSD" in stg:
            P.stage_reset(bgw)
            P.stage_ssd(l)
        if "C" in stg:
            P.stage_reset()
            P.stage_C(l, x_src, P.y_out if last else P.x1, last)
        x_src = P.x1
    P.S.finish_wait_all()
    P.S.run()
    return P


def kernel(**inputs):
    x_prompt = np.asarray(inputs['x_prompt'], np.float32)
    x_sample = np.asarray(inputs['x_sample'], np.float32)
    P = build_program()
    weights = {k: np.ascontiguousarray(np.asarray(inputs[k], np.float32)) for k in Prog.WEIGHT_SHAPES}
    consts = {1.0: host_consts(1.0), 0.0: host_consts(0.0)}
    in_maps = []
    for c in range(8):
        m = dict(weights)
        if c < 4:
            m['x'] = np.ascontiguousarray(x_sample[c])
            m.update(consts[1.0])
        else:
            j = c - 4
            x = np.zeros((NT, D), np.float32)
            x[0:SEG] = x_prompt[2 * j]
            x[SEG:2 * SEG] = x_prompt[2 * j + 1]
            m['x'] = x
            m.update(consts[0.0])
        in_maps.append(m)
    res = run_bass_kernel_spmd(P.nc, in_maps, core_ids=list(range(8)))
    y_prompt = np.zeros(x_prompt.shape, np.float32)
    y_sample = np.zeros(x_sample.shape, np.float32)
    for c in range(8):
        y = np.asarray(res.results[c]['y'], np.float32)
        if c < 4:
            y_sample[c] = y
        else:
            j = c - 4
            y_prompt[2 * j] = y[0:SEG]
            y_prompt[2 * j + 1] = y[SEG:2 * SEG]
    return (y_prompt, y_sample)
```

```python
import math
import numpy as np
import ml_dtypes
import concourse.bass as bass
import concourse.mybir as mybir
from concourse.bass_utils import run_bass_kernel_spmd

F32 = mybir.dt.float32
BF16 = mybir.dt.bfloat16
I32 = mybir.dt.int32
AF = mybir.ActivationFunctionType
ALU = mybir.AluOpType

D = 1024
DEPTH = 2
NT = 8192
SEG = 2048
NSEG = 4
TT = 512
NTT = NT // TT
INP = 2056
DFF = 2816
NFF = DFF // 128
EPS = 1e-6


class Buf:
    __slots__ = ("name", "w", "rs")

    def __init__(self, name):
        self.name = name
        self.w = None
        self.rs = []


class Sched:
    ENG = ("pe", "act", "dve", "pool", "sp")

    def __init__(self, nc, n_dma_sems=32):
        self.nc = nc
        self.ops = {e: [] for e in self.ENG}
        self.cnt = {e: 0 for e in self.ENG}
        self.seen = {e: {} for e in self.ENG}
        self.n_dma_sems = n_dma_sems
        self.dma_rr = 0
        self.dma_cnt = [0] * (n_dma_sems + self.N_BG)
        self.bg_rr = 0
        self.sw_rr = 0
        self.nops = 0
        self.opidx = {e: {} for e in self.ENG}

    def _deps(self, reads, writes):
        toks = []
        for b in reads:
            if b.w is not None:
                toks.append(b.w)
        for b in writes:
            if b.w is not None:
                toks.append(b.w)
            toks.extend(b.rs)
        return toks

    def _filter(self, e, toks, skip_same):
        best = {}
        seen = self.seen[e]
        for (s, v) in toks:
            if skip_same and s == ("eng", e):
                continue
            if seen.get(s, 0) >= v:
                continue
            best[s] = max(best.get(s, 0), v)
        for s, v in best.items():
            seen[s] = v
        return list(best.items())

    def _commit(self, tok, reads, writes):
        for b in reads:
            b.rs.append(tok)
        for b in writes:
            b.w = tok
            b.rs = []
        self.nops += 1

    def op(self, e, fn, reads=(), writes=()):
        waits = self._filter(e, self._deps(reads, writes), skip_same=(e == "pe"))
        self.cnt[e] += 1
        tok = (("eng", e), self.cnt[e])
        self.opidx[e][self.cnt[e]] = len(self.ops[e])
        self.ops[e].append((waits, fn, None))
        self._commit(tok, reads, writes)
        return tok

    N_BG = 8
    N_HW = 20

    def dma(self, e, out, in_, reads=(), writes=(), bg=False, **kw):
        if bg:
            k = self.n_dma_sems + (self.bg_rr % self.N_BG)
            self.bg_rr += 1
        elif e == "pool":
            k = self.N_HW + (self.sw_rr % (self.n_dma_sems - self.N_HW))
            self.sw_rr += 1
        else:
            k = self.dma_rr
            self.dma_rr = (self.dma_rr + 1) % self.N_HW
        s = ("dma", k)
        toks = self._deps(reads, writes)
        if self.dma_cnt[k] > 0:
            toks.append((s, 16 * self.dma_cnt[k]))
        waits = self._filter(e, toks, skip_same=False)
        self.dma_cnt[k] += 1
        tok = (s, 16 * self.dma_cnt[k])

        def fn(h, out=out, in_=in_, kw=kw):
            return h.dma_start(out=out, in_=in_, **kw)
        self.ops[e].append((waits, fn, s))
        self._commit(tok, reads, writes)
        return tok

    def all_tokens(self, bg=True):
        toks = []
        for en in self.ENG:
            if self.cnt[en] > 0:
                toks.append((("eng", en), self.cnt[en]))
        for k in range(self.n_dma_sems + (self.N_BG if bg else 0)):
            if self.dma_cnt[k] > 0:
                toks.append((("dma", k), 16 * self.dma_cnt[k]))
        return toks

    def barrier(self, bg=True):
        toks = self.all_tokens(bg)
        for e in self.ENG:
            waits = self._filter(e, [t for t in toks if t[0] != ("eng", e)], skip_same=False)
            if waits:
                self.ops[e].append((waits, None, None))

    def finish_wait_all(self, e="sp"):
        self.barrier()
        import os
        for en in self.ENG:
            for _ in range(int(os.environ.get("END_NOPS", "0"))):
                self.ops[en].append(([], "nop", None))

    def run(self):
        nc = self.nc
        from contextlib import ExitStack
        ms = {e: set() for e in self.ENG}
        for e in self.ENG:
            for waits, fn, ds in self.ops[e]:
                for (s, v) in waits:
                    if s != "drain" and s[0] == "eng":
                        ms[s[1]].add(v)
        msval = {}
        for e in self.ENG:
            c = 0
            mv = {}
            for n in range(1, self.cnt[e] + 1):
                if n in ms[e]:
                    c += 1
                    mv[n] = c
            msval[e] = mv
        self.n_milestones = {e: len(ms[e]) for e in self.ENG}
        with ExitStack() as st:
            semh = {}
            for en in self.ENG:
                semh[("eng", en)] = st.enter_context(nc.semaphore("s_" + en))
            for k in range(self.n_dma_sems + self.N_BG):
                semh[("dma", k)] = st.enter_context(nc.semaphore("s_dma%d" % k))
            block = st.enter_context(nc.Block())
            ops = self.ops

            def emit(en, h):
                n = 0
                for waits, fn, ds in ops[en]:
                    for (s, v) in waits:
                        if s == "drain":
                            h.drain()
                        elif s[0] == "eng":
                            h.wait_ge(semh[s], msval[s[1]][v])
                        else:
                            h.wait_ge(semh[s], v)
                    if fn is None:
                        continue
                    if fn == "nop":
                        h.nop()
                        continue
                    ins = fn(h)
                    if ds is not None:
                        ins.then_inc(semh[ds], 16)
                    else:
                        n += 1
                        if n in ms[en]:
                            ins.then_inc(semh[("eng", en)], 1)

            @block.tensor
            def _(h):
                emit("pe", h)

            @block.scalar
            def _(h):
                emit("act", h)

            @block.vector
            def _(h):
                emit("dve", h)

            @block.gpsimd
            def _(h):
                emit("pool", h)

            @block.sync
            def _(h):
                emit("sp", h)


class Prog:
    def __init__(self, debug_outputs=(), stages=None, depth=DEPTH):
        self.nc = bass.Bass("TRN2", target_bir_lowering=False)
        self.S = Sched(self.nc)
        self.debug_outputs = set(debug_outputs)
        self.stages = stages
        self.depth = depth
        self.dram = {}
        self.dbuf = {}
        self.sb_off = 0
        self._names = 0

    def din(self, name, shape, dt=F32):
        t = self.nc.dram_tensor(name, list(shape), dt, kind="ExternalInput").ap()
        self.dram[name] = t
        self.dbuf[name] = Buf(name)
        return t

    def dout(self, name, shape, dt=F32):
        t = self.nc.dram_tensor(name, list(shape), dt, kind="ExternalOutput").ap()
        self.dram[name] = t
        self.dbuf[name] = Buf(name)
        return t

    def dscr(self, name, shape, dt):
        kind = "ExternalOutput" if name in self.debug_outputs else "Internal"
        t = self.nc.dram_tensor(name, list(shape), dt, kind=kind).ap()
        self.dram[name] = t
        self.dbuf[name] = Buf(name)
        return t

    def sb(self, name, shape, dt):
        self._names += 1
        t = self.nc.alloc_sbuf_tensor("%s_%d" % (name, self._names), list(shape), dt)
        return t, Buf(name)

    def free_sb(self, *ts):
        pass

    def dbg(self, name, ap, b, shape, dt=F32):
        import os
        if name not in os.environ.get("KDBG", "").split(",") or name in self.dram:
            return
        t = self.dout(name, shape, dt)
        self.S.dma("sp", t, ap, reads=[b])

    def act(self, fn, reads=(), writes=()):
        return self.S.op("act", fn, reads, writes)

    def dve(self, fn, reads=(), writes=()):
        return self.S.op("dve", fn, reads, writes)

    def pool(self, fn, reads=(), writes=()):
        return self.S.op("pool", fn, reads, writes)

    def pe(self, fn, reads=(), writes=()):
        return self.S.op("pe", fn, reads, writes)

    def mm(self, out, lhsT, rhs, start, stop, reads, writes):
        return self.S.op("pe", lambda h: h.matmul(out, lhsT=lhsT, rhs=rhs, start=start, stop=stop), reads, writes)

    def ld(self, out, in_, reads=(), writes=(), q="sp", **kw):
        return self.S.dma(q, out, in_, reads, writes, **kw)

    def st(self, out, in_, reads=(), writes=(), q="pool", **kw):
        return self.S.dma(q, out, in_, reads, writes, **kw)

    ARENA_BYTES = 206 * 1024

    def arena_init(self):
        self.arena = self.nc.alloc_sbuf_tensor("arena", [128, self.ARENA_BYTES // 2], BF16)
        self.a_off = 0
        self.a_perm = 0
        self.psum = []
        for i in range(8):
            t = self.nc.alloc_psum_tensor("psb%d" % i, [128, 512], F32)
            self.psum.append((t, Buf("ps%d" % i)))
        self.ps_rr = 0

    def alloc(self, name, shape, dt, perm=False):
        esz = {F32: 4, BF16: 2, I32: 4}[dt]
        n = 1
        for s in shape[1:]:
            n *= s
        nbytes = (n * esz + 63) // 64 * 64
        off = self.a_off
        assert off + nbytes <= self.ARENA_BYTES, ("SBUF arena overflow", name, off, nbytes)
        self.a_off += nbytes
        v = self.arena[0:shape[0], off // 2: off // 2 + (n * esz) // 2]
        if dt != BF16:
            v = v.bitcast(dt)
        if len(shape) == 3:
            v = v.rearrange("p (a b) -> p a b", a=shape[1])
        elif len(shape) == 4:
            v = v.rearrange("p (a b c) -> p a b c", a=shape[1], b=shape[2])
        elif len(shape) == 5:
            v = v.rearrange("p (a b c d) -> p a b c d", a=shape[1], b=shape[2], c=shape[3])
        return v, Buf(name)

    def ps(self):
        t, b = self.psum[self.ps_rr]
        self.ps_rr = (self.ps_rr + 1) % 8
        return t, b

    def mark_perm(self):
        self.a_perm = self.a_off

    def barrier(self, bg=None):
        self.S.barrier(getattr(self, "bg_wait", True) if bg is None else bg)

    def stage_reset(self, bg=True):
        self.bg_wait = bg
        self.barrier(bg)
        self.a_off = self.a_perm

    WEIGHT_SHAPES = {
        'norm_mix_g': (DEPTH, D), 'w_in': (DEPTH, D, INP),
        's5_b_re': (DEPTH, 16, 64, 16), 's5_b_im': (DEPTH, 16, 64, 16),
        's5_lam_re_f': (DEPTH, 16, 64), 's5_lam_im_f': (DEPTH, 16, 64), 's5_log_step_f': (DEPTH, 16),
        's5_c_re_f': (DEPTH, 16, 16, 64), 's5_c_im_f': (DEPTH, 16, 16, 64),
        's5_lam_re_b': (DEPTH, 16, 64), 's5_lam_im_b': (DEPTH, 16, 64), 's5_log_step_b': (DEPTH, 16),
        's5_c_re_b': (DEPTH, 16, 16, 64), 's5_c_im_b': (DEPTH, 16, 16, 64),
        's5_d': (DEPTH, 256), 's5_w_glu': (DEPTH, 256, 256), 's5_b_glu': (DEPTH, 256),
        'fnet_w': (DEPTH, 4, 64, 64), 'fnet_b': (DEPTH, 4, 64),
        'ssd_conv_w': (DEPTH, 5, 1024), 'ssd_conv_b': (DEPTH, 1024),
        'ssd_a_log_f': (DEPTH, 8), 'ssd_dt_bias_f': (DEPTH, 8), 'ssd_a_log_b': (DEPTH, 8),
        'ssd_dt_bias_b': (DEPTH, 8), 'ssd_d': (DEPTH, 8), 'ssd_norm_g': (DEPTH, 512),
        'w_out': (DEPTH, D, D), 'norm_ffn_g': (DEPTH, D),
        'w_gate': (DEPTH, D, DFF), 'w_up': (DEPTH, D, DFF), 'w_down': (DEPTH, DFF, D),
        'final_norm_g': (D,),
    }

    def declare(self, debug_inputs=()):
        self.w = {}
        for k, shp in self.WEIGHT_SHAPES.items():
            self.w[k] = self.din(k, shp)
        self.x_in = self.din("x", (NT, D))
        self.flags = self.din("flags", (128, 2))
        self.c_ident = self.din("c_ident", (128, 128))
        self.y_out = self.dout("y", (NT, D))
        self.c_m1t = self.din("c_m1t", (64, 192), BF16)
        self.c_E = self.din("c_E", (128, 64, 2, 128), BF16)
        self.c_cspad = self.din("c_cspad", (64, 2, 2, 128))
        self.c_tri = self.din("c_tri", (128, 5, 128))
        self.c_iota = self.din("c_iota", (128, 48))
        self.c_s5mask = self.din("c_s5mask", (128, 2, 128))

        def scr(name, shape, dt):
            if name in debug_inputs:
                return self.din(name, shape, dt)
            return self.dscr(name, shape, dt)
        self.scr = scr
        self.wb = {
            'w_in': scr("wb_in", (DEPTH, D, INP), BF16),
            'w_out': scr("wb_out", (DEPTH, D, D), BF16),
            'w_gate': scr("wb_gate", (DEPTH, D, DFF), BF16),
            'w_up': scr("wb_up", (DEPTH, D, DFF), BF16),
            'w_down': scr("wb_down", (DEPTH, DFF, D), BF16),
            's5_w_glu': scr("wb_glu", (DEPTH, 256, 256), BF16),
        }
        self.x1 = scr("x1", (NT, D), F32)
        self.uvz = scr("uvz_tm", (NT, 1024), BF16)
        self.xbc = scr("xbc_fm", (1024, NT), BF16)
        self.dtr = scr("dt_tm", (NT, 8), F32)
        self.mix = scr("mix_fm", (1024, NT), BF16)
        self.yf_d = scr("yf_tm", (NT, 512), F32)

    def prep_weights(self):
        jobs = [('w_in', 0, False), ('s5_w_glu', 0, False)]
        jobs += [(k, 0, True) for k in ('w_out', 'w_gate', 'w_up', 'w_down')]
        jobs += [(k, l, True) for l in range(1, self.depth) for k in ('w_in', 's5_w_glu', 'w_out', 'w_gate', 'w_up', 'w_down')]
        for k, l, bg in jobs:
            src = self.w[k]
            dst = self.wb[k]
            rows = src.shape[1]
            r = 0
            while r < rows:
                n = min(256, rows - r)
                self.S.dma("pool", dst[l, r:r + n, :], src[l, r:r + n, :], bg=bg)
                r += n

    def load_consts(self):
        self.ident_f, self.b_ident_f = self.alloc("ident_f", [128, 128], F32, perm=True)
        self.ident, self.b_ident = self.alloc("ident", [128, 128], BF16, perm=True)
        self.flg, self.b_flg = self.alloc("flags", [128, 2], F32, perm=True)
        self.ld(self.ident_f, self.c_ident, writes=[self.b_ident_f])
        self.ld(self.flg, self.flags, writes=[self.b_flg])
        self.dve(lambda h: h.tensor_copy(out=self.ident, in_=self.ident_f), [self.b_ident_f], [self.b_ident])
        self.mark_perm()

    def norm_transpose(self, x_tm, b_x, gT, b_gT, hT, b_hT, tmp):
        h_tm, b_h, junk, b_junk, ss, b_ss = tmp
        eps_ap, b_eps_ = self.eps_t, self.b_eps
        self.dve(lambda h: h.memset(ss, 0.0), [], [b_ss])
        for s in range(4):
            self.act(lambda h, s=s: h.activation(out=junk, in_=x_tm[:, s, :], func=AF.Square,
                                                 accum_out=ss[:, s:s + 1]), [b_x], [b_junk, b_ss])
        self.act(lambda h, eps_ap=eps_ap: h.activation(out=ss[:, 4:8], in_=ss[:, 0:4], func=AF.Sqrt, bias=eps_ap[:, 0:1],
                                                       scale=1.0 / D), [b_ss, b_eps_], [b_ss])
        self.dve(lambda h: h.reciprocal(out=ss[:, 8:12], in_=ss[:, 4:8]), [b_ss], [b_ss])
        for s in range(4):
            self.dve(lambda h, s=s: h.tensor_scalar(out=h_tm[:, s, :], in0=x_tm[:, s, :], scalar1=ss[:, 8 + s:9 + s],
                                                    scalar2=None, op0=ALU.mult), [b_x, b_ss], [b_h])
        for s in range(4):
            pt, b_pt = self.ps()
            ptb = pt[:, :].bitcast(BF16)
            for kt in range(8):
                self.pe(lambda h, s=s, kt=kt, ptb=ptb: h.transpose(ptb[:, kt * 128:(kt + 1) * 128],
                                                                    h_tm[:, s, kt * 128:(kt + 1) * 128], self.ident),
                        [b_h, self.b_ident], [b_pt])
            self.dve(lambda h, s=s, ptb=ptb: h.tensor_tensor(
                out=hT[:, :, s * 128:(s + 1) * 128], in0=ptb.rearrange("p (a b) -> p a b", a=8),
                in1=gT.unsqueeze(2).broadcast_to([128, 8, 128]), op=ALU.mult), [b_pt, b_gT], [b_hT])

    def load_cols(self, name, rows_aps):
        n = sum(a.shape[0] for a in rows_aps)
        tmp, b_tmp = self.alloc(name + "_rows", [n, 128], F32)
        r = 0
        for a in rows_aps:
            self.ld(tmp[r:r + a.shape[0], :], a, writes=[b_tmp])
            r += a.shape[0]
        dst, b_dst = self.alloc(name, [128, n], F32)
        pt, b_pt = self.ps()
        self.mm(pt[:, 0:n], tmp[0:n, :], self.ident_f[0:n, 0:n], True, True, [b_tmp, self.b_ident_f], [b_pt])
        self.dve(lambda h: h.tensor_copy(out=dst, in_=pt[:, 0:n]), [b_pt], [b_dst])
        return dst, b_dst

    def load_gT(self, name, g_ap):
        return self.load_cols(name, [g_ap.rearrange("(kt p) -> kt p", p=128)])

    def stage_A(self, l, x_src):
        W, b_W = self.alloc("W_in", [128, 8, INP], BF16)
        wsrc = self.wb['w_in'][l].rearrange("(kt p) n -> p kt n", p=128)
        for kt in range(8):
            self.ld(W[:, kt, :], wsrc[:, kt, :], writes=[b_W])
        gT, b_gT = self.load_gT("gT_mix", self.w['norm_mix_g'][l])
        self.eps_t, self.b_eps = self.alloc("eps", [128, 1], F32)
        self.MS("dve", self.eps_t, EPS, [self.b_eps])
        xs = [self.alloc("x_tm%d" % i, [128, 4, D], F32) for i in range(2)]
        h_tm, b_h = self.alloc("h_tm", [128, 4, D], BF16)
        junk, b_junk = self.alloc("junk", [128, D], BF16)
        ss, b_ss = self.alloc("ss", [128, 12], F32)
        hT, b_hT = self.alloc("hT", [128, 8, TT], BF16)
        stg_tm = [self.alloc("stg_tm%d" % i, [128, 4, 1024], BF16) for i in range(2)]
        stg_fm = [self.alloc("stg_fm%d" % i, [128, 8, TT], BF16) for i in range(2)]
        stg_dt = [self.alloc("stg_dt%d" % i, [128, 4, 8], F32) for i in range(2)]

        def load_x(t):
            x_tm, b_x = xs[t % 2]
            self.ld(x_tm, x_src[t * TT:(t + 1) * TT, :].rearrange("(s p) d -> p s d", p=128), writes=[b_x])
        load_x(0)
        for t in range(NTT):
            if t + 1 < NTT:
                load_x(t + 1)
            x_tm, b_x = xs[t % 2]
            self.norm_transpose(x_tm, b_x, gT, b_gT, hT, b_hT, (h_tm, b_h, junk, b_junk, ss, b_ss))
            s_tm, b_stm = stg_tm[t % 2]
            s_fm, b_sfm = stg_fm[t % 2]
            s_dt, b_sdt = stg_dt[t % 2]
            for s in range(4):
                for half in range(2):
                    pt, b_pt = self.ps()
                    for kt in range(8):
                        self.mm(pt[:, 0:512], hT[:, kt, s * 128:(s + 1) * 128], W[:, kt, half * 512:(half + 1) * 512],
                                kt == 0, kt == 7, [b_hT, b_W], [b_pt])
                    fn = AF.Copy if half == 0 else AF.Silu
                    self.act(lambda h, s=s, half=half, pt=pt, fn=fn, s_tm=s_tm: h.activation(
                        out=s_tm[:, s, half * 512:(half + 1) * 512], in_=pt[:, 0:512], func=fn), [b_pt], [b_stm])
            pt, b_pt = self.ps()
            for s in range(4):
                for kt in range(8):
                    self.mm(pt[:, s * 8:(s + 1) * 8], hT[:, kt, s * 128:(s + 1) * 128], W[:, kt, 2048:2056],
                            kt == 0, kt == 7, [b_hT, b_W], [b_pt])
            self.dve(lambda h, pt=pt, s_dt=s_dt: h.tensor_copy(out=s_dt, in_=pt[:, 0:32].rearrange("p (a b) -> p a b", a=4)),
                     [b_pt], [b_sdt])
            for m in range(8):
                pt, b_pt = self.ps()
                for kt in range(8):
                    self.mm(pt[:, 0:512], W[:, kt, 1024 + m * 128:1024 + (m + 1) * 128], hT[:, kt, :],
                            kt == 0, kt == 7, [b_hT, b_W], [b_pt])
                if m % 2 == 0:
                    self.dve(lambda h, m=m, pt=pt, s_fm=s_fm: h.tensor_copy(out=s_fm[:, m, :], in_=pt[:, 0:512]), [b_pt], [b_sfm])
                else:
                    self.act(lambda h, m=m, pt=pt, s_fm=s_fm: h.copy(out=s_fm[:, m, :], in_=pt[:, 0:512]), [b_pt], [b_sfm])
            self.st(self.uvz[t * TT:(t + 1) * TT, :].rearrange("(s p) c -> p s c", p=128), s_tm, reads=[b_stm])
            self.st(self.xbc[:, t * TT:(t + 1) * TT].rearrange("(m p) n -> p m n", p=128), s_fm, reads=[b_sfm])
            self.st(self.dtr[t * TT:(t + 1) * TT, :].rearrange("(s p) c -> p s c", p=128), s_dt, reads=[b_sdt])

    def stage_C(self, l, x_src, x_dst, last):
        Wo, b_Wo = self.alloc("W_out", [128, 8, D], BF16)
        self.ld(Wo, self.wb['w_out'][l].rearrange("(kt p) n -> p kt n", p=128), writes=[b_Wo])
        Wd, b_Wd = self.alloc("W_down", [128, NFF, D], BF16)
        wdsrc = self.wb['w_down'][l].rearrange("(f p) n -> p f n", p=128)
        for f0 in range(0, NFF, 6):
            f1 = min(NFF, f0 + 6)
            self.ld(Wd[:, f0:f1, :], wdsrc[:, f0:f1, :], writes=[b_Wd])
        gT, b_gT = self.load_gT("gT_ffn", self.w['norm_ffn_g'][l])
        self.eps_t, self.b_eps = self.alloc("eps", [128, 1], F32)
        self.MS("dve", self.eps_t, EPS, [self.b_eps])
        if last:
            grow, b_grow = self.alloc("g_fin", [128, D], F32)
            self.ld(grow, self.w['final_norm_g'].partition_broadcast(128), writes=[b_grow])
        xs = [self.alloc("x_tm%d" % i, [128, 4, D], F32) for i in range(2)]
        ms = [self.alloc("mixT%d" % i, [128, 8, TT], BF16) for i in range(2)]
        h_tm, b_h = self.alloc("h_tm", [128, 4, D], BF16)
        junk, b_junk = self.alloc("junk", [128, D], BF16)
        ss, b_ss = self.alloc("ss", [128, 12], F32)
        hT, b_hT = self.alloc("hT", [128, 8, TT], BF16)
        actb, b_act = self.alloc("act", [128, NFF, TT], BF16)
        sgs = [self.alloc("sg%d" % i, [128, TT], BF16) for i in range(2)]
        GW = 2
        NG = NFF // GW
        wgs = [(self.alloc("Wg%d" % i, [128, 8, GW * 128], BF16), self.alloc("Wu%d" % i, [128, 8, GW * 128], BF16))
               for i in range(2)]
        wg_src = self.wb['w_gate'][l].rearrange("(kt p) n -> p kt n", p=128)
        wu_src = self.wb['w_up'][l].rearrange("(kt p) n -> p kt n", p=128)
        self._wq = 0

        def load_w(q):
            g = q % NG
            (Wg, b_Wg), (Wu, b_Wu) = wgs[q % 2]
            self.ld(Wg, wg_src[:, :, g * GW * 128:(g + 1) * GW * 128], writes=[b_Wg])
            self.ld(Wu, wu_src[:, :, g * GW * 128:(g + 1) * GW * 128], writes=[b_Wu])

        def load_t(t):
            x_tm, b_x = xs[t % 2]
            mT, b_m = ms[t % 2]
            self.ld(x_tm, x_src[t * TT:(t + 1) * TT, :].rearrange("(s p) d -> p s d", p=128), writes=[b_x])
            self.ld(mT, self.mix[:, t * TT:(t + 1) * TT].rearrange("(kt p) n -> p kt n", p=128), writes=[b_m])
        load_t(0)
        load_w(0)
        q = 0
        for t in range(NTT):
            if t + 1 < NTT:
                load_t(t + 1)
            x_tm, b_x = xs[t % 2]
            mT, b_m = ms[t % 2]
            for s in range(4):
                for half in range(2):
                    pt, b_pt = self.ps()
                    for kt in range(8):
                        self.mm(pt[:, 0:512], mT[:, kt, s * 128:(s + 1) * 128], Wo[:, kt, half * 512:(half + 1) * 512],
                                kt == 0, kt == 7, [b_m, b_Wo], [b_pt])
                    self.dve(lambda h, s=s, half=half, pt=pt, x_tm=x_tm: h.tensor_tensor(
                        out=x_tm[:, s, half * 512:(half + 1) * 512], in0=pt[:, 0:512],
                        in1=x_tm[:, s, half * 512:(half + 1) * 512], op=ALU.add), [b_pt, b_x], [b_x])
            self.norm_transpose(x_tm, b_x, gT, b_gT, hT, b_hT, (h_tm, b_h, junk, b_junk, ss, b_ss))
            for g in range(NG):
                if not (t == NTT - 1 and g == NG - 1):
                    load_w(q + 1)
                (Wg, b_Wg), (Wu, b_Wu) = wgs[q % 2]
                for j in range(GW):
                    f = g * GW + j
                    pg, b_pg = self.ps()
                    for kt in range(8):
                        self.mm(pg[:, 0:512], Wg[:, kt, j * 128:(j + 1) * 128], hT[:, kt, :], kt == 0, kt == 7,
                                [b_hT, b_Wg], [b_pg])
                    pu, b_pu = self.ps()
                    for kt in range(8):
                        self.mm(pu[:, 0:512], Wu[:, kt, j * 128:(j + 1) * 128], hT[:, kt, :], kt == 0, kt == 7,
                                [b_hT, b_Wu], [b_pu])
                    sg, b_sg = sgs[f % 2]
                    self.act(lambda h, pg=pg, sg=sg: h.activation(out=sg, in_=pg[:, 0:512], func=AF.Silu), [b_pg], [b_sg])
                    self.dve(lambda h, f=f, pu=pu, sg=sg: h.tensor_tensor(out=actb[:, f, :], in0=pu[:, 0:512], in1=sg,
                                                                       op=ALU.mult), [b_pu, b_sg], [b_act])
                q += 1
            for s in range(4):
                for half in range(2):
                    pt, b_pt = self.ps()
                    for f in range(NFF):
                        self.mm(pt[:, 0:512], actb[:, f, s * 128:(s + 1) * 128], Wd[:, f, half * 512:(half + 1) * 512],
                                f == 0, f == NFF - 1, [b_act, b_Wd], [b_pt])
                    self.dve(lambda h, s=s, half=half, pt=pt, x_tm=x_tm: h.tensor_tensor(
                        out=x_tm[:, s, half * 512:(half + 1) * 512], in0=pt[:, 0:512],
                        in1=x_tm[:, s, half * 512:(half + 1) * 512], op=ALU.add), [b_pt, b_x], [b_x])
            if last:
                self.dve(lambda h: h.memset(ss, 0.0), [], [b_ss])
                for s in range(4):
                    self.act(lambda h, s=s, x_tm=x_tm: h.activation(out=junk, in_=x_tm[:, s, :], func=AF.Square,
                                                                    accum_out=ss[:, s:s + 1]), [b_x], [b_junk, b_ss])
                self.act(lambda h, eps_ap=self.eps_t: h.activation(out=ss[:, 4:8], in_=ss[:, 0:4], func=AF.Sqrt, bias=eps_ap[:, 0:1],
                                                                   scale=1.0 / D), [b_ss, self.b_eps], [b_ss])
                self.dve(lambda h: h.reciprocal(out=ss[:, 8:12], in_=ss[:, 4:8]), [b_ss], [b_ss])
                for s in range(4):
                    self.dve(lambda h, s=s, x_tm=x_tm: h.scalar_tensor_tensor(
                        out=x_tm[:, s, :], in0=x_tm[:, s, :], scalar=ss[:, 8 + s:9 + s], in1=grow,
                        op0=ALU.mult, op1=ALU.mult), [b_x, b_ss, b_grow], [b_x])
            self.st(x_dst[t * TT:(t + 1) * TT, :].rearrange("(s p) d -> p s d", p=128), x_tm, reads=[b_x])

    def stage_fnet(self, l):
        M1T, b_M1T = self.alloc("M1T", [64, 192], BF16)
        self.ld(M1T, self.c_m1t, writes=[b_M1T])
        E, b_E = self.alloc("E", [128, 64, 2, 128], BF16)
        for k0 in range(0, 64, 16):
            self.ld(E[:, k0:k0 + 16], self.c_E[:, k0:k0 + 16], writes=[b_E])
        cspad, b_cs = self.alloc("cspad", [64, 2, 2, 128], F32)
        self.ld(cspad, self.c_cspad, writes=[b_cs])
        wsb, b_wsb = self.alloc("fw", [64, 4, 64], F32)
        self.ld(wsb, self.w['fnet_w'][l].rearrange("g j d -> j g d"), writes=[b_wsb])
        V1, b_V1 = self.alloc("V1", [64, 128, 128], BF16)
        Y, b_Y = self.alloc("Y", [128, 128, 192], BF16)
        PQ, b_PQ = self.alloc("PQ", [128, 2, 64, 128], BF16)
        Wt, b_Wt = self.alloc("Wt", [128, 4, 128], BF16)
        Wf, b_Wf = self.alloc("Wf", [128, 2, 128], F32)
        bcol, b_bcol = self.load_cols("fb", [self.w['fnet_b'][l].rearrange("(c gl) d -> c (gl d)", gl=2)])
        stgs = [self.alloc("fstg%d" % i, [128, 4, TT], BF16) for i in range(2)]
        for ch in range(2):
            for cs in range(2):
                pt, b_pt = self.ps()
                for gl in range(2):
                    self.mm(pt[:, gl * 64:(gl + 1) * 64], cspad[:, cs, gl, :], wsb[:, 2 * ch + gl, :], True, True,
                            [b_cs, b_wsb], [b_pt])
                self.dve(lambda h, cs=cs, pt=pt: h.tensor_copy(out=Wf[:, cs, :], in_=pt[:, 0:128]), [b_pt], [b_Wf])
            for v in range(2):
                self.dve(lambda h, v=v: h.tensor_scalar(out=Wt[:, 2 * v:2 * v + 2, :], in0=Wf, scalar1=self.flg[:, v:v + 1],
                                                        scalar2=None, op0=ALU.mult), [b_Wf, self.b_flg], [b_Wt])
            self.ld(V1, self.uvz[:, 256 + ch * 128:256 + (ch + 1) * 128].rearrange("(a b) c -> a b c", a=64), writes=[b_V1])
            for c0 in range(0, 128, 2):
                pt, b_pt = self.ps()
                for j in range(2):
                    self.mm(pt[:, j * 192:(j + 1) * 192], V1[:, :, c0 + j], M1T, True, True, [b_V1, b_M1T], [b_pt])
                src = pt[:, 0:384].rearrange("p (a b) -> p a b", a=2)
                if (c0 // 2) % 2 == 0:
                    self.act(lambda h, c0=c0, src=src: h.copy(out=Y[:, c0:c0 + 2, :], in_=src), [b_pt], [b_Y])
                else:
                    self.dve(lambda h, c0=c0, src=src: h.tensor_copy(out=Y[:, c0:c0 + 2, :], in_=src), [b_pt], [b_Y])
            for k0 in range(0, 64, 4):
                pP, b_pP = self.ps()
                pQ, b_pQ = self.ps()
                for j in range(4):
                    k1 = k0 + j
                    yr, yi, ynr = Y[:, :, 3 * k1], Y[:, :, 3 * k1 + 1], Y[:, :, 3 * k1 + 2]
                    o = slice(j * 128, (j + 1) * 128)
                    self.mm(pP[:, o], yr, E[:, k1, 0, :], True, False, [b_Y, b_E], [b_pP])
                    self.mm(pP[:, o], yi, E[:, k1, 1, :], False, True, [b_Y, b_E], [b_pP])
                    self.mm(pQ[:, o], yi, E[:, k1, 0, :], True, False, [b_Y, b_E], [b_pQ])
                    self.mm(pQ[:, o], ynr, E[:, k1, 1, :], False, True, [b_Y, b_E], [b_pQ])
                self.act(lambda h, k0=k0, pP=pP: h.copy(out=PQ[:, 0, k0:k0 + 4, :], in_=pP[:, :].rearrange("p (a b) -> p a b", a=4)),
                         [b_pP], [b_PQ])
                self.dve(lambda h, k0=k0, pQ=pQ: h.tensor_copy(out=PQ[:, 1, k0:k0 + 4, :],
                                                                in_=pQ[:, :].rearrange("p (a b) -> p a b", a=4)), [b_pQ], [b_PQ])
            for t in range(NTT):
                stg, b_stg = stgs[(t // 4) % 2]
                pt, b_pt = self.ps()
                sgm, tq = t // 4, t % 4
                for i, (wi, pq, view) in enumerate(((0, 0, 0), (1, 1, 0), (2, 0, 1), (3, 1, 1))):
                    if view == 0:
                        rhs = PQ[:, pq, :, 8 * t:8 * t + 8].rearrange("p a b -> p b a")
                    else:
                        rhs = PQ[:, pq, 16 * sgm:16 * sgm + 16, 32 * tq:32 * tq + 32].rearrange("p a b -> p b a")
                    self.mm(pt[:, 0:512], Wt[:, wi, :], rhs, i == 0, i == 3, [b_Wt, b_PQ], [b_pt])
                self.act(lambda h, t=t, pt=pt, stg=stg, ch=ch: h.activation(out=stg[:, t % 4, :], in_=pt[:, 0:512], func=AF.Identity,
                                                                        bias=bcol[:, ch:ch + 1]), [b_pt, b_bcol], [b_stg])
                if t % 4 == 3:
                    t0 = t - 3
                    self.st(self.mix[256 + ch * 128:256 + (ch + 1) * 128, t0 * TT:(t0 + 4) * TT],
                            stg[:, :, :].rearrange("p a b -> p (a b)"), reads=[b_stg])

    def stage_ssd(self, l):
        import os
        NCH = 4
        tri, b_tri = self.alloc("tri", [128, 5, 128], F32)
        self.ld(tri, self.c_tri, writes=[b_tri])
        hp, b_hp = self.alloc("hp", [128, 5, 8], F32)
        for i, k in enumerate(('ssd_a_log_f', 'ssd_a_log_b', 'ssd_dt_bias_f', 'ssd_dt_bias_b', 'ssd_d')):
            self.ld(hp[:, i, :], self.w[k][l].partition_broadcast(128), writes=[b_hp])
        self.act(lambda h: h.activation(out=hp[:, 0:2, :], in_=hp[:, 0:2, :], func=AF.Exp), [b_hp], [b_hp])
        self.dve(lambda h: h.tensor_scalar(out=hp[:, 0:2, :], in0=hp[:, 0:2, :], scalar1=-1.0, scalar2=None, op0=ALU.mult),
                 [b_hp], [b_hp])
        cwb, b_cw = self.load_cols("cwb", [self.w['ssd_conv_w'][l].rearrange("k (m p) -> (k m) p", p=128),
                                           self.w['ssd_conv_b'][l].rearrange("(m p) -> m p", p=128)])
        cw = cwb[:, 0:40].rearrange("p (k m) -> p k m", k=5)
        cb, b_cb = cwb[:, 40:48], b_cw
        diagW, b_dW = self.alloc("diagW", [128, 8, 5, 128], BF16)
        for m in range(8):
            for k in range(5):
                self.dve(lambda h, m=m, k=k: h.tensor_scalar(out=diagW[:, m, k, :], in0=self.ident_f, scalar1=cw[:, k, m:m + 1],
                                                             scalar2=None, op0=ALU.mult), [b_cw, self.b_ident_f], [b_dW])
        gnorm, b_gn = self.alloc("gnorm", [128, 512], F32)
        self.ld(gnorm, self.w['ssd_norm_g'][l].partition_broadcast(128), writes=[b_gn])
        eps_t, b_eps = self.alloc("eps", [128, 1], F32)
        self.dve(lambda h: h.memset(eps_t, EPS), [], [b_eps])
        xins = [self.alloc("xin%d" % i, [128, 8, TT + 4], BF16) for i in range(2)]
        dtrs = [self.alloc("dtr%d" % i, [128, NCH, 8], F32) for i in range(2)]
        yfs = [self.alloc("yfs%d" % i, [128, NCH, 512], F32) for i in range(2)]
        szs = [self.alloc("szs%d" % i, [128, NCH, 512], BF16) for i in range(2)]
        mstg = [self.alloc("mstg%d" % i, [128, 4, TT], BF16) for i in range(2)]
        css = [self.alloc("cs%d" % i, [128, 8, TT], BF16) for i in range(2)]
        xBs = [self.alloc("sxB_tm%d" % i, [128, NCH, 768], BF16) for i in range(2)]
        sms = [self.alloc("ssm%d" % i, [128, 8, NCH * 8], F32) for i in range(2)]
        xds = [self.alloc("xd%d" % i, [128, NCH, 512], BF16) for i in range(2)]
        xdds = [self.alloc("xdd%d" % i, [128, NCH, 512], BF16) for i in range(2)]
        daTris = [self.alloc("daTri%d" % i, [128, 8, 128], BF16) for i in range(NCH)]
        Lms = [self.alloc("Lm%d" % i, [128, 8, 128], BF16) for i in range(NCH)]
        tri_bf, b_tribf = self.alloc("tri_bf", [128, 2, 128], BF16)
        self.CP("dve", tri_bf, tri[:, 2:4, :], [b_tri], [b_tribf])
        Gs = [self.alloc("G%d" % i, [128, 8, 128], BF16) for i in range(NCH)]
        CBms = [self.alloc("CBm%d" % i, [128, 2, 128], BF16) for i in range(NCH)]
        tmp, b_tmp = self.alloc("stmp", [128, 512], F32)
        tmp2, b_tmp2 = self.alloc("stmp2", [128, 512], F32)
        yb, b_yb = self.alloc("syb", [128, 512], F32)
        yn, b_yn = self.alloc("syn", [128, 512], BF16)
        H, b_H = self.alloc("H", [128, 512], F32)
        St4s = [self.alloc("St4_%d" % i, [128, NCH, 512], F32) for i in range(2)]
        Hin4s = [self.alloc("Hin4_%d" % i, [128, NCH, 512], BF16) for i in range(2)]
        junk, b_junk = self.alloc("sjunk", [128, 512], BF16)
        ssq, b_ssq = self.alloc("sssq", [128, 4], F32)
        link = self.flg[:, 0:1]
        STOP = int(os.environ.get("SSD_STOP", "99"))
        if STOP <= 0:
            return
        for _ in range(int(os.environ.get("SSD_PAD", "0"))):
            self.dve(lambda h: h.memset(junk, 0.0), [], [b_junk])
        for _ in range(int(os.environ.get("SSD_PADA", "0"))):
            self.act(lambda h: h.copy(out=junk, in_=gnorm), [b_gn], [b_junk])

        def load_pro(t, i):
            xin, b_xin = xins[i % 2]
            dtr, b_dtr = dtrs[i % 2]
            lo = t * TT - 2
            hi = t * TT + TT + 2
            src = self.xbc.rearrange("(m p) n -> p m n", p=128)
            if t == 0:
                self.dve(lambda h, xin=xin: h.memset(xin[:, :, 0:2], 0.0), [], [b_xin])
                self.ld(xin[:, :, 2:TT + 4], src[:, :, 0:hi], writes=[b_xin])
            elif t == NTT - 1:
                self.dve(lambda h, xin=xin: h.memset(xin[:, :, TT + 2:TT + 4], 0.0), [], [b_xin])
                self.ld(xin[:, :, 0:TT + 2], src[:, :, lo:NT], writes=[b_xin])
            else:
                self.ld(xin, src[:, :, lo:hi], writes=[b_xin])
            self.ld(dtr, self.dtr[t * TT:(t + 1) * TT, :].rearrange("(c p) e -> p c e", p=128), writes=[b_dtr])

        def load_chk(t, i, sweep):
            if sweep == 1:
                yf, b_yf = yfs[i % 2]
                sz, b_sz = szs[i % 2]
                self.ld(yf, self.yf_d[t * TT:(t + 1) * TT, :].rearrange("(c p) e -> p c e", p=128), writes=[b_yf])
                self.ld(sz, self.uvz[t * TT:(t + 1) * TT, 512:1024].rearrange("(c p) e -> p c e", p=128), writes=[b_sz])

        for sweep in range(int(os.environ.get("SSD_SWEEPS", "2"))):
            order = list(range(NTT)) if sweep == 0 else list(range(NTT - 1, -1, -1))
            order = order[:int(os.environ.get("SSD_TILES", "16"))]
            corder = list(range(NCH)) if sweep == 0 else list(range(NCH - 1, -1, -1))
            if sweep == 1:
                self.barrier()
            TRI_CUM = 0 if sweep == 0 else 1
            TRI_LHS = 2 if sweep == 0 else 3
            self.dve(lambda h: h.memset(H, 0.0), [], [b_H])
            def prologue(i, t):
                cs, b_cs = css[i % 2]
                xB_tm, b_xtm = xBs[i % 2]
                x_tm = xB_tm[:, :, 0:512]
                sm, b_sm = sms[i % 2]
                xd, b_xd = xds[i % 2]
                xdd, b_xdd = xdds[i % 2]
                xin, b_xin = xins[i % 2]
                dtr, b_dtr = dtrs[i % 2]
                if t % 4 == 0 and t > 0:
                    self.dve(lambda h, xin=xin: h.tensor_scalar(out=xin[:, :, 0:2], in0=xin[:, :, 0:2], scalar1=link, scalar2=None,
                                                                op0=ALU.mult), [b_xin, self.b_flg], [b_xin])
                if t % 4 == 3 and t < NTT - 1:
                    self.dve(lambda h, xin=xin: h.tensor_scalar(out=xin[:, :, TT + 2:TT + 4], in0=xin[:, :, TT + 2:TT + 4], scalar1=link,
                                                                scalar2=None, op0=ALU.mult), [b_xin, self.b_flg], [b_xin])
                for m in range(8):
                    pt, b_pt = self.ps()
                    for k in range(5):
                        self.mm(pt[:, 0:512], diagW[:, m, k, :], xin[:, m, k:k + TT], k == 0, k == 4, [b_dW, b_xin], [b_pt])
                    self.act(lambda h, m=m, pt=pt: h.activation(out=cs[:, m, :], in_=pt[:, 0:512], func=AF.Silu, bias=cb[:, m:m + 1]),
                             [b_pt, b_cb], [b_cs])
                if STOP <= 1:
                    return
                self.dbg("d_cs", cs, b_cs, [128, 8, TT], BF16)
                for c in range(NCH):
                    pt, b_pt = self.ps()
                    ptb = pt[:, :].bitcast(BF16)
                    for m in range(6):
                        self.pe(lambda h, m=m, c=c, ptb=ptb: h.transpose(ptb[:, m * 128:(m + 1) * 128], cs[:, m, c * 128:(c + 1) * 128],
                                                                        self.ident), [b_cs, self.b_ident], [b_pt])
                    if c % 2 == 0:
                        self.dve(lambda h, c=c, ptb=ptb: h.tensor_copy(out=xB_tm[:, c, :], in_=ptb[:, 0:768]), [b_pt], [b_xtm])
                    else:
                        self.act(lambda h, c=c, ptb=ptb: h.copy(out=xB_tm[:, c, :], in_=ptb[:, 0:768]), [b_pt], [b_xtm])
                if STOP <= 2:
                    return
                bias_row = hp[:, 2 + sweep, :].unsqueeze(1).broadcast_to([128, NCH, 8])
                a_row = hp[:, sweep, :].unsqueeze(1).broadcast_to([128, NCH, 8])
                v = lambda j: sm[:, j, :].rearrange("p (c e) -> p c e", c=NCH)
                self.dve(lambda h, dtr=dtr, bias_row=bias_row: h.tensor_tensor(out=v(0), in0=dtr, in1=bias_row, op=ALU.add), [b_dtr, b_hp], [b_sm])
                self.act(lambda h: h.activation(out=sm[:, 0, :], in_=sm[:, 0, :], func=AF.Exp), [b_sm], [b_sm])
                self.act(lambda h: h.activation(out=sm[:, 1, :], in_=sm[:, 0, :], func=AF.Ln, bias=1.0), [b_sm], [b_sm])
                self.dve(lambda h, a_row=a_row: h.tensor_tensor(out=v(2), in0=v(1), in1=a_row, op=ALU.mult), [b_sm, b_hp], [b_sm])
                self.dve(lambda h: h.tensor_tensor(
                    out=xd[:, :, :].rearrange("p c (e q) -> p c e q", e=8),
                    in0=x_tm[:, :, :].rearrange("p c (e q) -> p c e q", e=8),
                    in1=v(1).unsqueeze(3).broadcast_to([128, NCH, 8, 64]), op=ALU.mult), [b_xtm, b_sm], [b_xd])
                if STOP <= 3:
                    return
                self.dbg("d_xB", xB_tm, b_xtm, [128, NCH, 768], BF16)
                self.dbg("d_xd", xd, b_xd, [128, NCH, 512], BF16)
                pc, b_pc = self.ps()
                self.mm(pc[:, 0:32], tri[:, TRI_CUM, :], sm[:, 2, :], True, True, [b_tri, b_sm], [b_pc])
                self.mm(pc[:, 32:64], tri[:, 4, :], sm[:, 2, :], True, True, [b_tri, b_sm], [b_pc])
                self.dve(lambda h, pc=pc: h.tensor_copy(out=sm[:, 3, :], in_=pc[:, 0:32]), [b_pc], [b_sm])
                self.dve(lambda h, pc=pc: h.tensor_copy(out=sm[:, 0, :], in_=pc[:, 32:64]), [b_pc], [b_sm])
                self.dve(lambda h: h.tensor_tensor(out=sm[:, 7, :], in0=sm[:, 0, :], in1=sm[:, 3, :], op=ALU.subtract), [b_sm], [b_sm])
                self.act(lambda h: h.activation(out=sm[:, 4, :], in_=sm[:, 3, :], func=AF.Exp), [b_sm], [b_sm])
                self.act(lambda h: h.activation(out=sm[:, 6, :], in_=sm[:, 0, :], func=AF.Exp), [b_sm], [b_sm])
                self.act(lambda h: h.activation(out=sm[:, 5, :], in_=sm[:, 7, :], func=AF.Exp), [b_sm], [b_sm])
                self.dve(lambda h: h.tensor_tensor(
                    out=xdd[:, :, :].rearrange("p c (e q) -> p c e q", e=8),
                    in0=xd[:, :, :].rearrange("p c (e q) -> p c e q", e=8),
                    in1=v(5).unsqueeze(3).broadcast_to([128, NCH, 8, 64]), op=ALU.mult), [b_xd, b_sm], [b_xdd])
                self.dbg("d_sm", sm, b_sm, [128, 8, NCH * 8], F32)

            def chunks(i, t):
                cs, b_cs = css[i % 2]
                xB_tm, b_xtm = xBs[i % 2]
                x_tm = xB_tm[:, :, 0:512]
                B_tm = xB_tm[:, :, 512:768]
                b_Btm = b_xtm
                sm, b_sm = sms[i % 2]
                xd, b_xd = xds[i % 2]
                xdd, b_xdd = xdds[i % 2]
                if STOP <= 4:
                    return
                if sweep == 1:
                    yf, b_yf = yfs[i % 2]
                    sz, b_sz = szs[i % 2]
                    ms_, b_ms = mstg[i % 2]
                else:
                    yf, b_yf = yfs[i % 2]
                for c in corder:
                    daTri, b_daTri = daTris[c]
                    self.TT("dve", daTri, sm[:, 2, c * 8:(c + 1) * 8].unsqueeze(2).broadcast_to([128, 8, 128]),
                            tri[:, TRI_CUM, :].unsqueeze(1).broadcast_to([128, 8, 128]), ALU.mult, [b_sm, b_tri], [b_daTri])
                pcbs, pds = {}, {}
                for c in corder:
                    csl = slice(c * 128, (c + 1) * 128)
                    daTri, b_daTri = daTris[c]
                    pcb, b_pcb = self.ps()
                    for gg in range(2):
                        self.mm(pcb[:, gg * 128:(gg + 1) * 128], cs[:, 4 + gg, csl], cs[:, 6 + gg, csl], True, True, [b_cs], [b_pcb])
                    pcbs[c] = (pcb, b_pcb)
                    self.TT("dve", CBms[c][0], pcb[:, 0:256].rearrange("p (a b) -> p a b", a=2),
                            tri[:, TRI_CUM, :].unsqueeze(1).broadcast_to([128, 2, 128]), ALU.mult, [b_pcb, b_tri], [CBms[c][1]])
                    for hh in range(2):
                        pd, b_pd = self.ps()
                        self.mm(pd[:, 0:512], tri_bf[:, TRI_LHS - 2, :], daTri[:, hh * 4:(hh + 1) * 4, :].rearrange("p a b -> p (a b)"),
                                True, True, [b_tribf, b_daTri], [b_pd])
                        self.ACTF(Lms[c][0][:, hh * 4:(hh + 1) * 4, :].rearrange("p a b -> p (a b)"), pd[:, 0:512], AF.Exp, [b_pd], [Lms[c][1]])
                for c in corder:
                    G, b_G = Gs[c]
                    Lm, b_Lm = Lms[c]
                    CBm, b_CBm = CBms[c]
                    self.TT("dve", G[:, :, :].rearrange("p (g e) l -> p g e l", g=2), Lm[:, :, :].rearrange("p (g e) l -> p g e l", g=2),
                            CBm.unsqueeze(2).broadcast_to([128, 2, 4, 128]), ALU.mult, [b_Lm, b_CBm], [b_G])
                St4, b_St4 = St4s[i % 2]
                Hin4, b_Hin4 = Hin4s[i % 2]
                for c in corder:
                    pst, b_pst = self.ps()
                    for gg in range(2):
                        self.mm(pst[:, gg * 256:(gg + 1) * 256], B_tm[:, c, gg * 128:(gg + 1) * 128], xdd[:, c, gg * 256:(gg + 1) * 256],
                                True, True, [b_Btm, b_xdd], [b_pst])
                    self.CP("act", St4[:, c, :], pst[:, 0:512], [b_pst], [b_St4])
                for c in corder:
                    gc = t * NCH + c
                    self.CP("dve", Hin4[:, c, :], H, [b_H], [b_Hin4])
                    self.TT("dve", H[:, :].rearrange("p (e q) -> p e q", e=8), H[:, :].rearrange("p (e q) -> p e q", e=8),
                            sm[:, 6, c * 8:(c + 1) * 8].unsqueeze(2).broadcast_to([128, 8, 64]), ALU.mult, [b_H, b_sm], [b_H])
                    self.TT("dve", H, H, St4[:, c, :], ALU.add, [b_H, b_St4], [b_H])
                    nxt = gc + 1 if sweep == 0 else gc - 1
                    if 0 <= nxt < NT // 128 and (nxt // 16) != (gc // 16):
                        self.TS("dve", H, H, link, ALU.mult, [b_H, self.b_flg], [b_H])
                for c in corder:
                    csl = slice(c * 128, (c + 1) * 128)
                    G, b_G = Gs[c]
                    py, b_py = self.ps()
                    for e in range(8):
                        self.mm(py[:, e * 64:(e + 1) * 64], G[:, e, :], xd[:, c, e * 64:(e + 1) * 64], True, True, [b_G, b_xd], [b_py])
                    pch, b_pch = self.ps()
                    for gg in range(2):
                        self.mm(pch[:, gg * 256:(gg + 1) * 256], cs[:, 6 + gg, csl], Hin4[:, c, gg * 256:(gg + 1) * 256], True, True,
                                [b_cs, b_Hin4], [b_pch])
                    self.TT("dve", tmp[:, :].rearrange("p (e q) -> p e q", e=8), pch[:, 0:512].rearrange("p (e q) -> p e q", e=8),
                            sm[:, 4, c * 8:(c + 1) * 8].unsqueeze(2).broadcast_to([128, 8, 64]), ALU.mult, [b_pch, b_sm], [b_tmp])
                    if sweep == 0:
                        self.TT("dve", yf[:, c, :], py[:, 0:512], tmp, ALU.add, [b_py, b_tmp], [b_yf])
                    else:
                        self.TT("dve", yb, py[:, 0:512], tmp, ALU.add, [b_py, b_tmp], [b_yb])
                    if sweep == 1:
                        self.dve(lambda h, c=c, yf=yf: h.tensor_tensor(out=yb, in0=yb, in1=yf[:, c, :], op=ALU.add), [b_yb, b_yf], [b_yb])
                        self.dve(lambda h, c=c: h.tensor_tensor(
                            out=tmp2[:, :].rearrange("p (e q) -> p e q", e=8), in0=x_tm[:, c, :].rearrange("p (e q) -> p e q", e=8),
                            in1=hp[:, 4, :].unsqueeze(2).broadcast_to([128, 8, 64]), op=ALU.mult), [b_xtm, b_hp], [b_tmp2])
                        self.dve(lambda h: h.tensor_tensor(out=yb, in0=yb, in1=tmp2, op=ALU.add), [b_yb, b_tmp2], [b_yb])
                        self.dve(lambda h, c=c, sz=sz: h.tensor_tensor(out=yb, in0=yb, in1=sz[:, c, :], op=ALU.mult), [b_yb, b_sz], [b_yb])
                        self.dve(lambda h: h.memset(ssq, 0.0), [], [b_ssq])
                        self.act(lambda h: h.activation(out=junk, in_=yb, func=AF.Square, accum_out=ssq[:, 0:1]), [b_yb], [b_junk, b_ssq])
                        self.act(lambda h: h.activation(out=ssq[:, 1:2], in_=ssq[:, 0:1], func=AF.Sqrt, bias=eps_t[:, 0:1], scale=1.0 / 512),
                                 [b_ssq, b_eps], [b_ssq])
                        self.dve(lambda h: h.reciprocal(out=ssq[:, 2:3], in_=ssq[:, 1:2]), [b_ssq], [b_ssq])
                        self.dve(lambda h: h.scalar_tensor_tensor(out=yn, in0=yb, scalar=ssq[:, 2:3], in1=gnorm, op0=ALU.mult, op1=ALU.mult),
                                 [b_yb, b_ssq, b_gn], [b_yn])
                        pt, b_pt = self.ps()
                        ptb = pt[:, :].bitcast(BF16)
                        for m in range(4):
                            self.pe(lambda h, m=m, ptb=ptb: h.transpose(ptb[:, m * 128:(m + 1) * 128], yn[:, m * 128:(m + 1) * 128], self.ident),
                                    [b_yn, self.b_ident], [b_pt])
                        self.act(lambda h, c=c, ptb=ptb, ms_=ms_: h.copy(out=ms_[:, :, c * 128:(c + 1) * 128],
                                                                       in_=ptb[:, 0:512].rearrange("p (m q) -> p m q", m=4)), [b_pt], [b_ms])
                if sweep == 0:
                    self.dbg("d_yf", yf, b_yf, [128, NCH, 512], F32)
                    self.st(self.yf_d[t * TT:(t + 1) * TT, :].rearrange("(c p) e -> p c e", p=128), yf, reads=[b_yf])
                else:
                    self.st(self.mix[512:1024, t * TT:(t + 1) * TT].rearrange("(m p) n -> p m n", p=128), ms_, reads=[b_ms])

            load_pro(order[0], 0)
            if len(order) > 1:
                load_pro(order[1], 1)
            prologue(0, order[0])
            load_chk(order[0], 0, sweep)
            for i, t in enumerate(order):
                if i + 1 < len(order):
                    prologue(i + 1, order[i + 1])
                    load_chk(order[i + 1], i + 1, sweep)
                if i + 2 < len(order):
                    load_pro(order[i + 2], i + 2)
                chunks(i, t)

    def TT(self, e, out, in0, in1, op, r, w):
        return self.S.op(e, lambda h: h.tensor_tensor(out=out, in0=in0, in1=in1, op=op), r, w)

    def TS(self, e, out, in0, s1, op0, r, w, s2=None, op1=None):
        if op1 is None:
            return self.S.op(e, lambda h: h.tensor_scalar(out=out, in0=in0, scalar1=s1, scalar2=None, op0=op0), r, w)
        return self.S.op(e, lambda h: h.tensor_scalar(out=out, in0=in0, scalar1=s1, scalar2=s2, op0=op0, op1=op1), r, w)

    def STT(self, e, out, in0, scalar, in1, op0, op1, r, w):
        return self.S.op(e, lambda h: h.scalar_tensor_tensor(out=out, in0=in0, scalar=scalar, in1=in1, op0=op0, op1=op1), r, w)

    def CP(self, e, out, in_, r, w):
        if e == "act":
            return self.S.op(e, lambda h: h.copy(out=out, in_=in_), r, w)
        return self.S.op(e, lambda h: h.tensor_copy(out=out, in_=in_), r, w)

    def ACTF(self, out, in_, func, r, w, bias=None, scale=None):
        kw = {}
        if bias is not None:
            kw['bias'] = bias
        if scale is not None:
            kw['scale'] = scale
        return self.S.op("act", lambda h: h.activation(out=out, in_=in_, func=func, **kw), r, w)

    def MS(self, e, ap, val, w):
        return self.S.op(e, lambda h: h.memset(ap, val), [], w)

    def TR(self, out, in_, r, w, ident=None):
        ident = self.ident if ident is None else ident
        return self.S.op("pe", lambda h: h.transpose(out, in_, ident), r, w)

    def stage_s5(self, l):
        import os
        PI2 = 2.0 * math.pi
        NSG = NT // 8
        NCK = NSG // 4
        W_ = self.w
        ADD, MUL, SUB = ALU.add, ALU.mult, ALU.subtract
        NE = 48
        vt, b_vt = self.alloc("vt", [128, 5, 32, NE], F32)
        co, b_co = self.alloc("s5co", [128, 4, 2, 32], F32)
        Bb, b_Bb = self.alloc("Bbar", [128, 2, 2, 16, 16], F32)
        CC, b_CC = self.alloc("CC", [128, 4, 256], F32)
        Dcol, b_Dcol = self.alloc("Dcol", [128, 16], F32)
        Wg, b_Wg = self.alloc("Wglu", [128, 2, 256], BF16)
        Um, b_Um = self.alloc("Um", [128, 16, NSG], BF16)
        Hbf, b_Hbf = self.alloc("Hbf", [128, 2, 16, NCK], BF16)
        s5m, b_s5m = self.alloc("s5mask", [128, 2, 128], F32)
        bglu, b_bglu = self.load_cols("bglu", [W_['s5_b_glu'][l].rearrange("(h p) -> h p", p=128)])
        mark = self.a_off
        iota, b_iota = self.alloc("iota", [128, 48], F32)
        self.ld(iota, self.c_iota, writes=[b_iota])
        self.ld(s5m, self.c_s5mask, writes=[b_s5m])
        Lr, b_Lr = self.alloc("lamrows", [64, 128], F32)
        for i, k in enumerate(('s5_lam_re_f', 's5_lam_im_f', 's5_lam_re_b', 's5_lam_im_b')):
            for hf in range(2):
                self.ld(Lr[i * 16:(i + 1) * 16, hf * 64:(hf + 1) * 64], W_[k][l], writes=[b_Lr])
        LAM, b_LAM = self.alloc("LAM", [128, 64], F32)
        pt, b_pt = self.ps()
        self.mm(pt[:, 0:64], Lr, self.ident_f[0:64, 0:64], True, True, [b_Lr, self.b_ident_f], [b_pt])
        self.CP("dve", LAM, pt[:, 0:64], [b_pt], [b_LAM])
        LAMv = LAM.rearrange("p (d r g) -> p d r g", d=2, r=2)
        sc, b_sc = self.alloc("s5sc", [128, 12, 32], F32)
        stp = sc[:, 0, :]
        self.ld(sc[:, 0, 0:16], W_['s5_log_step_f'][l].partition_broadcast(128), writes=[b_sc])
        self.ld(sc[:, 0, 16:32], W_['s5_log_step_b'][l].partition_broadcast(128), writes=[b_sc])
        self.ACTF(stp, stp, AF.Exp, [b_sc], [b_sc])
        v32 = lambda j: sc[:, j, :]
        v216 = lambda j: sc[:, j, :].rearrange("p (d g) -> p d g", d=2)
        self.CP("dve", v216(1), LAMv[:, :, 0, :], [b_LAM], [b_sc])
        self.CP("dve", v216(2), LAMv[:, :, 1, :], [b_LAM], [b_sc])
        self.TT("dve", v32(3), v32(1), stp, MUL, [b_sc], [b_sc])
        self.TT("dve", v32(4), v32(2), stp, MUL, [b_sc], [b_sc])
        tw, b_tw = self.alloc("tw", [128, 4, 32, NE], F32)
        ki, b_ki = self.alloc("ki", [128, 32, NE], I32)
        io3 = iota.unsqueeze(1).broadcast_to([128, 32, NE])
        lam_i3 = v32(4).unsqueeze(2).broadcast_to([128, 32, NE])
        lam_r3 = v32(3).unsqueeze(2).broadcast_to([128, 32, NE])
        self.TT("dve", tw[:, 2], io3, lam_i3, MUL, [b_iota, b_sc], [b_tw])
        for which, off in ((1, 0.0), (0, math.pi / 2)):
            if off != 0.0:
                self.TS("dve", tw[:, 2], tw[:, 2], off, ADD, [b_tw], [b_tw])
            self.TS("dve", ki, tw[:, 2], 1.0 / PI2, MUL, [b_tw], [b_ki])
            self.CP("dve", tw[:, 3], ki, [b_ki], [b_tw])
            self.STT("dve", tw[:, 3], tw[:, 3], -PI2, tw[:, 2], MUL, ADD, [b_tw], [b_tw])
            self.ACTF(tw[:, which], tw[:, 3], AF.Sin, [b_tw], [b_tw])
        self.TT("dve", tw[:, 2], io3, lam_r3, MUL, [b_iota, b_sc], [b_tw])
        self.ACTF(tw[:, 2], tw[:, 2], AF.Exp, [b_tw], [b_tw])
        self.TT("dve", tw[:, 0], tw[:, 0], tw[:, 2], MUL, [b_tw], [b_tw])
        self.TT("dve", tw[:, 1], tw[:, 1], tw[:, 2], MUL, [b_tw], [b_tw])
        lo, hi = slice(0, 64), slice(64, 128)
        for var, (slo, sglo, shi, sghi) in enumerate(((0, 1, 1, 1), (1, -1, 0, 1), (0, -1, 1, -1), (0, 1, 1, -1), (1, -1, 0, -1))):
            self.TS("dve", vt[lo, var], tw[lo, slo], float(sglo), MUL, [b_tw], [b_vt])
            self.ACTF(vt[hi, var], tw[hi, shi], AF.Copy, [b_tw], [b_vt], scale=float(sghi))
        wr1, wi1 = tw[:, 0, :, 8], tw[:, 1, :, 8]
        self.TS("dve", v32(5), wr1, -1.0, ADD, [b_tw], [b_sc])
        self.TT("dve", v32(6), v32(1), v32(1), MUL, [b_sc], [b_sc])
        self.TT("dve", v32(7), v32(2), v32(2), MUL, [b_sc], [b_sc])
        self.TT("dve", v32(6), v32(6), v32(7), ADD, [b_sc], [b_sc])
        self.S.op("dve", lambda h: h.reciprocal(out=v32(6), in_=v32(6)), [b_sc], [b_sc])
        self.TT("dve", v32(7), v32(5), v32(1), MUL, [b_sc], [b_sc])
        self.TT("dve", v32(8), wi1, v32(2), MUL, [b_sc, b_tw], [b_sc])
        self.TT("dve", v32(7), v32(7), v32(8), ADD, [b_sc], [b_sc])
        self.TT("dve", v32(7), v32(7), v32(6), MUL, [b_sc], [b_sc])
        self.TT("dve", v32(8), wi1, v32(1), MUL, [b_sc, b_tw], [b_sc])
        self.TT("dve", v32(9), v32(5), v32(2), MUL, [b_sc], [b_sc])
        self.TT("dve", v32(8), v32(8), v32(9), SUB, [b_sc], [b_sc])
        self.TT("dve", v32(8), v32(8), v32(6), MUL, [b_sc], [b_sc])
        ar, ai = tw[:, 0, :, 39], tw[:, 1, :, 39]
        for f in range(2):
            self.CP("dve", co[:, 0, f, :], ar, [b_tw], [b_co])
            self.TS("dve", co[:, 1, f, :], ai, 1.0 if f == 0 else -1.0, MUL, [b_tw], [b_co])
        self.TS("dve", co[:, 2:4], co[:, 0:2], self.flg[:, 0:1], MUL, [b_co, self.b_flg], [b_co])
        Bri, b_Bri = self.alloc("Bri", [128, 2, 16, 16], F32)
        for i, k in enumerate(('s5_b_re', 's5_b_im')):
            for hf in range(2):
                self.ld(Bri[hf * 64:(hf + 1) * 64, i], W_[k][l].rearrange("g p c -> p g c"), writes=[b_Bri])
        tB, b_tB = self.alloc("tB", [128, 2, 16, 16], F32)
        qr4 = v216(7).unsqueeze(3).broadcast_to([128, 2, 16, 16])
        qi4 = v216(8).unsqueeze(3).broadcast_to([128, 2, 16, 16])
        bre4 = Bri[:, 0].unsqueeze(1).broadcast_to([128, 2, 16, 16])
        bim4 = Bri[:, 1].unsqueeze(1).broadcast_to([128, 2, 16, 16])
        self.TT("dve", Bb[:, 0], qr4, bre4, MUL, [b_sc, b_Bri], [b_Bb])
        self.TT("dve", tB, qi4, bim4, MUL, [b_sc, b_Bri], [b_tB])
        self.TT("dve", Bb[:, 0], Bb[:, 0], tB, SUB, [b_Bb, b_tB], [b_Bb])
        self.TT("dve", Bb[:, 1], qr4, bim4, MUL, [b_sc, b_Bri], [b_Bb])
        self.TT("dve", tB, qi4, bre4, MUL, [b_sc, b_Bri], [b_tB])
        self.TT("dve", Bb[:, 1], Bb[:, 1], tB, ADD, [b_Bb, b_tB], [b_Bb])
        crow = [self.alloc("crow%d" % i, [128, 128], F32) for i in range(2)]
        n = 0
        for i, k in enumerate(('s5_c_re_f', 's5_c_im_f', 's5_c_re_b', 's5_c_im_b')):
            src = W_[k][l].rearrange("g c p -> (g c) p")
            for hf in range(2):
                cr_, b_cr = crow[n % 2]
                n += 1
                for rep in range(2):
                    self.ld(cr_[:, rep * 64:(rep + 1) * 64], src[hf * 128:(hf + 1) * 128, :], writes=[b_cr])
                pt, b_pt = self.ps()
                self.mm(pt[:, 0:128], cr_, self.ident_f, True, True, [b_cr, self.b_ident_f], [b_pt])
                self.CP("dve", CC[:, i, hf * 128:(hf + 1) * 128], pt[:, 0:128], [b_pt], [b_CC])
        Drep, b_Drep = self.alloc("Drep", [16, 8, 16], F32)
        self.ld(Drep, W_['s5_d'][l].rearrange("(g c) -> g c", c=16).unsqueeze(1).broadcast_to([16, 8, 16]), writes=[b_Drep])
        pt, b_pt = self.ps()
        self.mm(pt[:, 0:16], Drep[:, :, :].rearrange("g a c -> g (a c)"), self.ident_f[0:16, 0:16], True, True, [b_Drep, self.b_ident_f], [b_pt])
        self.CP("dve", Dcol, pt[:, 0:16], [b_pt], [b_Dcol])
        self.ld(Wg, self.wb['s5_w_glu'][l].rearrange("(k p) n -> p k n", p=128), writes=[b_Wg])
        self.barrier()
        self.a_off = mark
        uts = [self.alloc("UT%d" % i, [128, 8, 256], BF16) for i in range(2)]
        utgs = [self.alloc("UTg%d" % i, [128, 16, 128], BF16) for i in range(2)]
        for j in range(NSG // 128):
            UT, b_UT = uts[j % 2]
            self.ld(UT, self.uvz[j * 1024:(j + 1) * 1024, 0:256].rearrange("(p s) c -> p s c", s=8), writes=[b_UT])
            UTg, b_UTg = utgs[j % 2]
            self.CP("dve" if j % 2 == 0 else "act", UTg[:, :, :].rearrange("p g (s c) -> p g s c", s=8),
                    UT[:, :, :].rearrange("p s (g c) -> p g s c", g=16), [b_UT], [b_UTg])
            for hb in range(2):
                pt, b_pt = self.ps()
                ptb = pt[:, :].bitcast(BF16)
                for gg in range(8):
                    g = hb * 8 + gg
                    self.TR(ptb[:, gg * 128:(gg + 1) * 128], UTg[:, g, :], [b_UTg, self.b_ident], [b_pt])
                self.CP("dve" if hb == 0 else "act", Um[:, hb * 8:(hb + 1) * 8, j * 128:(j + 1) * 128],
                        ptb.rearrange("p (g s) -> p g s", g=8), [b_pt], [b_Um])
        X, b_X = self.alloc("X", [128, 2, 2, 16, NCK], F32)
        mark2 = self.a_off
        self.MS("dve", X[:, :, :, :, 0:1], 0.0, [b_X])
        VinTs = [self.alloc("VinT%d" % i, [128, 4, 4, 128], BF16) for i in range(2)]
        Vins = [self.alloc("Vin%d" % i, [128, 4, 4, 128], BF16) for i in range(2)]
        tgs = [[self.alloc("tg%d_%d" % (k, i), [128, 4, 8, 16], F32) for i in range(2)] for k in range(2)]
        cur = {"tg": tgs[0]}

        def tbl(var, d, g):
            return vt[:, var, d * 16 + g, :]

        def cplx_tile(out4, xr, xi, ta, tb_, eng="dve"):
            A, S_ = out4.shape[1], out4.shape[2]
            (t0, b_t0), (t1, b_t1) = cur["tg"]
            t0v, t1v = t0[:, 0:A, 0:S_, :], t1[:, 0:A, 0:S_, :]
            xr4 = xr.unsqueeze(1).unsqueeze(1).broadcast_to([128, A, S_, 16])
            xi4 = xi.unsqueeze(1).unsqueeze(1).broadcast_to([128, A, S_, 16])
            ta4 = ta.unsqueeze(3).broadcast_to([128, A, S_, 16])
            tb4 = tb_.unsqueeze(3).broadcast_to([128, A, S_, 16])
            self.TT(eng, t0v, xr4, ta4, MUL, [b_Bb, b_CC, b_vt], [b_t0])
            self.TT(eng, t1v, xi4, tb4, MUL, [b_Bb, b_CC, b_vt], [b_t1])
            return t0v, t1v, b_t0, b_t1

        def e_fwd(tb_, j0, n_):
            return tb_[:, j0:j0 + n_].rearrange("p (a s) -> p a s", s=8)

        def e_rev(tb_, j0, n_):
            return tb_[:, j0:j0 + n_][:, ::-1].rearrange("p (a s) -> p a s", s=8)

        G_LIST = list(range(int(os.environ.get("S5_GROUPS", "16"))))
        for g in G_LIST:
            VinT, b_VinT = VinTs[g % 2]
            Vin, b_Vin = Vins[g % 2]
            cur["tg"] = tgs[g % 2]
            for d in range(2):
                xr, xi = Bb[:, 0, d, g, :], Bb[:, 1, d, g, :]
                ev = e_rev if d == 0 else e_fwd
                for form, (va, vb) in enumerate(((0, 1), (1, 2))):
                    t0v, t1v, b_t0, b_t1 = cplx_tile(VinT[:, form * 2 + d].rearrange("p a (s c) -> p a s c", s=8), xr, xi,
                                                     ev(tbl(va, d, g), 7, 32), ev(tbl(vb, d, g), 7, 32))
                    self.TT("dve", VinT[:, form * 2 + d].rearrange("p a (s c) -> p a s c", s=8), t0v, t1v, ADD, [b_t0, b_t1], [b_VinT])
            for hb in range(2):
                pt, b_pt = self.ps()
                ptb = pt[:, :].bitcast(BF16)
                for k in range(8):
                    fd, a = (hb * 8 + k) // 4, (hb * 8 + k) % 4
                    self.TR(ptb[:, k * 128:(k + 1) * 128], VinT[:, fd, a, :], [b_VinT, self.b_ident], [b_pt])
                self.CP("act", Vin[:, hb * 2:(hb + 1) * 2].rearrange("p f a c -> p (f a c)"), ptb, [b_pt], [b_Vin])
            Umg = Um[:, g, :].rearrange("p (c a) -> p c a", a=4)
            for form in range(2):
                for d in range(2):
                    pt, b_pt = self.ps()
                    for a in range(4):
                        self.mm(pt[:, 0:NCK], Vin[:, form * 2 + d, a, :], Umg[:, :, a], a == 0, a == 3, [b_Vin, b_Um], [b_pt])
                    src = pt[:, 0:NCK - 1] if d == 0 else pt[:, 0:NCK][:, ::-1][:, 0:NCK - 1]
                    self.CP("dve" if d == 0 else "act", X[:, form, d, g, 1:NCK], src, [b_pt], [b_X])
        self.barrier()
        self.a_off = mark2
        Xs = X[:, :, :, :, :].rearrange("p f d g (s k) -> p f (d g) s k", s=4)
        ts4, b_ts4 = self.alloc("scan_t4", [128, 2, 32, 4], F32)
        co0s = co[:, 0].unsqueeze(3).broadcast_to([128, 2, 32, 4])
        co1s = co[:, 1].unsqueeze(3).broadcast_to([128, 2, 32, 4])
        for k in range(1, 64):
            self.TT("dve", ts4, Xs[:, :, :, :, k - 1], co0s, MUL, [b_X, b_co], [b_ts4])
            self.TT("dve", Xs[:, :, :, :, k], Xs[:, :, :, :, k], ts4, ADD, [b_X, b_ts4], [b_X])
            self.TT("dve", ts4, Xs[:, ::-1, :, :, k - 1], co1s, MUL, [b_X, b_co], [b_ts4])
            self.TT("dve", Xs[:, :, :, :, k], Xs[:, :, :, :, k], ts4, ADD, [b_X, b_ts4], [b_X])
        PW, b_PW = self.alloc("PW", [128, 2, 32, 64], F32)
        pm, b_pm = self.alloc("pm", [128, 4, 32], F32)
        pt_, b_pt_ = self.alloc("pwt", [128, 2, 32, 32], F32)
        ar_, ai_ = co[:, 0, 0, :], co[:, 1, 0, :]
        self.MS("dve", PW[:, 0, :, 0:1], 1.0, [b_PW])
        self.MS("dve", PW[:, 1, :, 0:1], 0.0, [b_PW])
        m = 1
        while m < 64:
            lr_, li_ = PW[:, 0, :, m - 1], PW[:, 1, :, m - 1]
            self.TT("dve", pm[:, 0], lr_, ar_, MUL, [b_PW, b_co], [b_pm])
            self.TT("dve", pm[:, 2], li_, ai_, MUL, [b_PW, b_co], [b_pm])
            self.TT("dve", pm[:, 0], pm[:, 0], pm[:, 2], SUB, [b_pm], [b_pm])
            self.TT("dve", pm[:, 1], lr_, ai_, MUL, [b_PW, b_co], [b_pm])
            self.TT("dve", pm[:, 3], li_, ar_, MUL, [b_PW, b_co], [b_pm])
            self.TT("dve", pm[:, 1], pm[:, 1], pm[:, 3], ADD, [b_pm], [b_pm])
            qr_ = pm[:, 0].unsqueeze(2).broadcast_to([128, 32, m])
            qi_ = pm[:, 1].unsqueeze(2).broadcast_to([128, 32, m])
            pr0, pi0 = PW[:, 0, :, 0:m], PW[:, 1, :, 0:m]
            self.TT("dve", PW[:, 0, :, m:2 * m], pr0, qr_, MUL, [b_PW, b_pm], [b_PW])
            self.TT("dve", pt_[:, 0, :, 0:m], pi0, qi_, MUL, [b_PW, b_pm], [b_pt_])
            self.TT("dve", PW[:, 0, :, m:2 * m], PW[:, 0, :, m:2 * m], pt_[:, 0, :, 0:m], SUB, [b_PW, b_pt_], [b_PW])
            self.TT("dve", PW[:, 1, :, m:2 * m], pr0, qi_, MUL, [b_PW, b_pm], [b_PW])
            self.TT("dve", pt_[:, 1, :, 0:m], pi0, qr_, MUL, [b_PW, b_pm], [b_pt_])
            self.TT("dve", PW[:, 1, :, m:2 * m], PW[:, 1, :, m:2 * m], pt_[:, 1, :, 0:m], ADD, [b_PW, b_pt_], [b_PW])
            m *= 2
        dl, b_dl = self.alloc("sdelta", [128, 2, 32], F32)
        d2, b_d2 = self.alloc("sdelta2", [128, 2, 32], F32)
        fx, b_fx = pt_[:, :, :, :].rearrange("p a b c -> p (a b c)").rearrange("p (b c) -> p b c", b=32), b_pt_
        for sg in range(1, 4):
            hp_ = Xs[:, :, :, sg - 1, 63]
            hps = Xs[:, ::-1, :, sg - 1, 63]
            x0 = Xs[:, :, :, sg, 0]
            self.TT("dve", dl, hp_, co[:, 0], MUL, [b_X, b_co], [b_dl])
            self.TT("dve", d2, hps, co[:, 1], MUL, [b_X, b_co], [b_d2])
            self.TT("dve", dl, dl, d2, ADD, [b_dl, b_d2], [b_dl])
            self.TT("dve", dl, dl, x0, ADD, [b_dl, b_X], [b_dl])
            self.TS("dve", dl, dl, self.flg[:, 0:1], MUL, [b_dl, self.b_flg], [b_dl])
            self.TT("dve", dl, dl, x0, SUB, [b_dl, b_X], [b_dl])
            self.CP("dve", d2[:, 0], dl[:, 1], [b_dl], [b_d2])
            self.TS("dve", d2[:, 1], dl[:, 0], -1.0, MUL, [b_dl], [b_d2])
            for f in range(2):
                dl3 = dl[:, f].unsqueeze(2).broadcast_to([128, 32, 64])
                d23 = d2[:, f].unsqueeze(2).broadcast_to([128, 32, 64])
                xseg = Xs[:, f, :, sg, :]
                self.TT("dve", fx, PW[:, 0], dl3, MUL, [b_PW, b_dl], [b_fx])
                self.TT("dve", xseg, xseg, fx, ADD, [b_X, b_fx], [b_X])
                self.TT("dve", fx, PW[:, 1], d23, MUL, [b_PW, b_d2], [b_fx])
                self.TT("dve", xseg, xseg, fx, ADD, [b_X, b_fx], [b_X])
        self.CP("dve", Hbf[:, 0], X[:, 0, 0], [b_X], [b_Hbf])
        self.CP("act", Hbf[:, 1], X[:, 0, 1, :, ::-1], [b_X], [b_Hbf])
        self.barrier()
        self.a_off = mark2
        tgs = [[self.alloc("tgo%d_%d" % (k, i), [128, 4, 8, 16], F32) for i in range(2)] for k in range(2)]
        Yg, b_Yg = X[:, :, :, :, :].rearrange("p f d g c -> p (f d g c)")[:, 0:8192].bitcast(BF16).rearrange("p (g s) -> p g s", g=16), Buf("Yg")
        CoutTs = [self.alloc("Cout%d" % i, [128, 2, 4, 128], BF16) for i in range(2)]
        LBs = [self.alloc("LB%d" % i, [128, 2, 128], BF16) for i in range(2)]
        RCs = [self.alloc("RC%d" % i, [128, 2, 4, 128], BF16) for i in range(2)]
        Wts = [self.alloc("Wtoep%d" % i, [128, 7, 128], BF16) for i in range(2)]
        w0ts = [self.alloc("w0t%d" % i, [128, 2, 128], F32) for i in range(2)]
        yts = [self.alloc("ytmp%d" % i, [128, 512], F32) for i in range(2)]
        t2s = [self.alloc("yt2%d" % i, [128, 512], F32) for i in range(2)]
        for g in G_LIST:
            CoutT, b_Cout = CoutTs[g % 2]
            LB, b_LB = LBs[g % 2]
            RC, b_RC = RCs[g % 2]
            Wt, b_Wt = Wts[g % 2]
            w0t, b_w0t = w0ts[g % 2]
            cur["tg"] = tgs[g % 2]
            for d in range(2):
                cr_, ci_ = CC[:, 2 * d, g * 16:(g + 1) * 16], CC[:, 2 * d + 1, g * 16:(g + 1) * 16]
                br_, bi_ = Bb[:, 0, d, g, :], Bb[:, 1, d, g, :]
                ev = e_fwd if d == 0 else e_rev
                t0v, t1v, b_t0, b_t1 = cplx_tile(CoutT[:, d].rearrange("p a (s c) -> p a s c", s=8), cr_, ci_,
                                                 ev(tbl(3, d, g), 8, 32), ev(tbl(4, d, g), 8, 32))
                self.TT("dve", CoutT[:, d].rearrange("p a (s c) -> p a s c", s=8), t0v, t1v, ADD, [b_t0, b_t1], [b_Cout])
                ta = e_rev(tbl(0, d, g), 0, 8) if d == 0 else e_fwd(tbl(0, d, g), 7, 8)
                tb_ = e_rev(tbl(1, d, g), 0, 8) if d == 0 else e_fwd(tbl(1, d, g), 7, 8)
                t0v, t1v, b_t0, b_t1 = cplx_tile(LB[:, d].rearrange("p (a s c) -> p a s c", a=1, s=8), br_, bi_, ta, tb_)
                self.TT("dve", LB[:, d].rearrange("p (a s c) -> p a s c", a=1, s=8), t0v, t1v, ADD, [b_t0, b_t1], [b_LB])
                if d == 0:
                    ta, tb_ = e_fwd(tbl(3, d, g), 7, 32), e_fwd(tbl(4, d, g), 7, 32)
                else:
                    ta = tbl(3, d, g)[:, 0:32].rearrange("p (k s) -> p k s", s=8)[:, :, ::-1]
                    tb_ = tbl(4, d, g)[:, 0:32].rearrange("p (k s) -> p k s", s=8)[:, :, ::-1]
                t0v, t1v, b_t0, b_t1 = cplx_tile(RC[:, d].rearrange("p a (s c) -> p a s c", s=8), cr_, ci_, ta, tb_)
                self.TT("dve", RC[:, d].rearrange("p a (s c) -> p a s c", s=8), t0v, t1v, ADD, [b_t0, b_t1], [b_RC])
            pF, b_pF = self.ps()
            pB, b_pB = self.ps()
            for k in range(4):
                self.mm(pF[:, k * 128:(k + 1) * 128], LB[:, 0, :], RC[:, 0, k, :], True, True, [b_LB, b_RC], [b_pF])
                self.mm(pB[:, k * 128:(k + 1) * 128], LB[:, 1, :], RC[:, 1, k, :], True, True, [b_LB, b_RC], [b_pB])
            self.CP("dve", Wt[:, 4:7, :], pF[:, 128:512].rearrange("p (k c) -> p k c", k=3), [b_pF], [b_Wt])
            self.CP("act", Wt[:, 0:3, :][:, ::-1, :], pB[:, 128:512].rearrange("p (k c) -> p k c", k=3), [b_pB], [b_Wt])
            self.TT("dve", w0t[:, 0, :], pF[:, 0:128], s5m[:, 0, :], MUL, [b_pF, b_s5m], [b_w0t])
            self.CP("act", w0t[:, 1, :], pB[:, 0:128], [b_pB], [b_w0t])
            self.TT("dve", w0t[:, 1, :], w0t[:, 1, :], s5m[:, 1, :], MUL, [b_w0t, b_s5m], [b_w0t])
            self.TT("dve", Wt[:, 3, :], w0t[:, 0, :], w0t[:, 1, :], ADD, [b_w0t], [b_Wt])
            for bnk in range(NSG // 512):
                pt, b_pt = self.ps()
                psv = pt[:, 0:512].rearrange("p (c a) -> p c a", a=4)
                Umv = Um[:, g, bnk * 512:(bnk + 1) * 512].rearrange("p (c a) -> p c a", a=4)
                self.mm(pt[:, 0:512], Wt[:, 3, :], Um[:, g, bnk * 512:(bnk + 1) * 512], True, False, [b_Wt, b_Um], [b_pt])
                for dl in (-3, -2, -1, 1, 2, 3):
                    a_lo, a_hi = max(0, dl), min(3, 3 + dl)
                    self.mm(psv[:, :, a_lo:a_hi + 1], Wt[:, dl + 3, :], Umv[:, :, a_lo - dl:a_hi - dl + 1], False, False,
                            [b_Wt, b_Um], [b_pt])
                for d in range(2):
                    for a in range(4):
                        self.mm(psv[:, :, a], CoutT[:, d, a, :], Hbf[:, d, g, bnk * 128:(bnk + 1) * 128], False, (d == 1 and a == 3),
                                [b_Cout, b_Hbf], [b_pt])
                yt, b_yt = yts[bnk % 2]
                t2, b_t2 = t2s[bnk % 2]
                usl = Um[:, g, bnk * 512:(bnk + 1) * 512]
                self.STT("dve", yt, usl, Dcol[:, g:g + 1], pt[:, 0:512], MUL, ADD, [b_Um, b_Dcol, b_pt], [b_yt])
                self.ACTF(t2, yt, AF.Square, [b_yt], [b_t2])
                self.TS("dve", t2, t2, 0.044715, MUL, [b_t2], [b_t2], 1.0, ADD)
                self.TT("dve", t2, t2, yt, MUL, [b_t2, b_yt], [b_t2])
                self.ACTF(t2, t2, AF.Sigmoid, [b_t2], [b_t2], scale=1.5957691216)
                self.TT("dve", Yg[:, g, bnk * 512:(bnk + 1) * 512], t2, yt, MUL, [b_t2, b_yt], [b_Yg])
        self.barrier()
        self.a_off = mark2
        Gfm, b_Gfm = X[:, :, :, :, :].rearrange("p f d g c -> p (f d g c)")[:, 8192:16384].bitcast(BF16).rearrange("p (h n) -> p h n", h=2), Buf("Gfm")
        yTs = [self.alloc("yT%d" % i, [128, 8, 256], BF16) for i in range(2)]
        for j in range(NSG // 128):
            yT, b_yT = yTs[j % 2]
            for hb in range(2):
                pt, b_pt = self.ps()
                ptb = pt[:, :].bitcast(BF16)
                for gg in range(8):
                    self.TR(ptb[:, gg * 128:(gg + 1) * 128], Yg[:, hb * 8 + gg, j * 128:(j + 1) * 128], [b_Yg, self.b_ident], [b_pt])
                self.CP("dve" if hb == 0 else "act",
                        yT[:, :, hb * 128:(hb + 1) * 128].rearrange("p l (g c) -> p g l c", g=8),
                        ptb.rearrange("p (g l c) -> p g l c", g=8, l=8), [b_pt], [b_yT])
            for hb in range(2):
                pt, b_pt = self.ps()
                ptb = pt[:, :].bitcast(BF16)
                for ll in range(8):
                    self.TR(ptb[:, ll * 128:(ll + 1) * 128], yT[:, ll, hb * 128:(hb + 1) * 128], [b_yT, self.b_ident], [b_pt])
                self.CP("dve" if hb == 0 else "act",
                        Gfm[:, hb, j * 1024:(j + 1) * 1024].rearrange("p (s l) -> p l s", l=8),
                        ptb.rearrange("p (l s) -> p l s", l=8), [b_pt], [b_Gfm])
        ostg = [self.alloc("s5o%d" % i, [128, 2, TT], BF16) for i in range(2)]
        sgt = [self.alloc("s5sg%d" % i, [128, TT], BF16) for i in range(2)]
        for t in range(NTT):
            og, b_og = ostg[t % 2]
            for oh in range(2):
                pt, b_pt = self.ps()
                for kh in range(2):
                    self.mm(pt[:, 0:512], Wg[:, kh, oh * 128:(oh + 1) * 128], Gfm[:, kh, t * TT:(t + 1) * TT], kh == 0, kh == 1,
                            [b_Wg, b_Gfm], [b_pt])
                sg, b_sg = sgt[oh]
                self.ACTF(sg, pt[:, 0:512], AF.Sigmoid, [b_pt, b_bglu], [b_sg], bias=bglu[:, oh:oh + 1])
                self.TT("dve", og[:, oh, :], sg, Gfm[:, oh, t * TT:(t + 1) * TT], MUL, [b_sg, b_Gfm], [b_og])
            self.st(self.mix[0:256, t * TT:(t + 1) * TT].rearrange("(h p) n -> p h n", p=128), og, reads=[b_og])


def host_consts(link):
    c = {}
    c['c_ident'] = np.eye(128, dtype=np.float32)
    fl = np.zeros((128, 2), np.float32)
    fl[:, 0] = link
    fl[:, 1] = 1.0 - link
    c['flags'] = fl
    bf = ml_dtypes.bfloat16
    l1 = np.arange(64)
    k1 = np.arange(64)
    if link > 0.5:
        th = 2 * np.pi * np.outer(l1, k1) / 64.0
        blk = np.ones((64, 64))
        Lseq = 8192.0
        kk = k1[:, None] + 64 * np.arange(128)[None, :]
    else:
        th = 2 * np.pi * np.outer(l1 % 16, k1 % 16) / 16.0
        blk = (l1[:, None] // 16 == k1[None, :] // 16).astype(np.float64)
        Lseq = 2048.0
        kk = (k1 % 16)[:, None] + 16 * np.arange(128)[None, :]
    m1t = np.stack([np.cos(th) * blk, -np.sin(th) * blk, -np.cos(th) * blk], axis=-1)
    c['c_m1t'] = np.ascontiguousarray(m1t.reshape(64, 192)).astype(bf)
    l2 = np.arange(128)
    ph = 2 * np.pi * (l2[:, None, None] * kk[None, :, :] % Lseq) / Lseq
    E = np.stack([np.cos(ph), np.sin(ph)], axis=2) / np.sqrt(Lseq)
    c['c_E'] = np.ascontiguousarray(E).astype(bf)
    j = np.arange(64)
    a = 2 * np.pi * np.outer(j, j) / 64.0
    cs = np.stack([np.cos(a), np.sin(a)], axis=0) / 8.0
    pad = np.zeros((64, 2, 2, 128), np.float32)
    for gl in range(2):
        pad[:, 0, gl, gl * 64:(gl + 1) * 64] = cs[0]
        pad[:, 1, gl, gl * 64:(gl + 1) * 64] = cs[1]
    c['c_cspad'] = pad
    r = np.arange(128)
    tri = np.zeros((128, 5, 128), np.float32)
    tri[:, 0, :] = r[:, None] <= r[None, :]
    tri[:, 1, :] = r[:, None] >= r[None, :]
    tri[:, 2, :] = r[:, None] > r[None, :]
    tri[:, 3, :] = r[:, None] < r[None, :]
    tri[:, 4, :] = 1.0
    c['c_tri'] = tri
    c['c_iota'] = np.tile(np.arange(-7, 41, dtype=np.float32)[None, :], (128, 1))
    sp = np.arange(128) // 16
    m = np.zeros((128, 2, 128), np.float32)
    m[:, 0, :] = sp[None, :] >= sp[:, None]
    m[:, 1, :] = sp[None, :] <= sp[:, None]
    c['c_s5mask'] = m
    return c


def build_program(depth=DEPTH, debug_outputs=()):
    P = Prog(debug_outputs=debug_outputs, depth=depth)
    P.declare()
    P.arena_init()
    P.prep_weights()
    P.load_consts()
    x_src = P.x_in
    for l in range(depth):
        last = (l == depth - 1)
        import os
        stg = os.environ.get("K_STAGES", "A,S5,F,SSD,C").split(",")
        bgw = (l > 0)
        if "A" in stg:
            P.stage_reset(bgw)
            P.stage_A(l, x_src)
        if "S5" in stg:
            P.stage_reset(bgw)
            P.stage_s5(l)
        if "F" in stg:
            P.stage_reset(bgw)
            P.stage_fnet(l)
        if "SSD" in stg:
            P.stage_reset(bgw)
            P.stage_ssd(l)
        if "C" in stg:
            P.stage_reset()
            P.stage_C(l, x_src, P.y_out if last else P.x1, last)
        x_src = P.x1
    P.S.finish_wait_all()
    P.S.run()
    return P


def kernel(**inputs):
    x_prompt = np.asarray(inputs['x_prompt'], np.float32)
    x_sample = np.asarray(inputs['x_sample'], np.float32)
    P = build_program()
    weights = {k: np.ascontiguousarray(np.asarray(inputs[k], np.float32)) for k in Prog.WEIGHT_SHAPES}
    consts = {1.0: host_consts(1.0), 0.0: host_consts(0.0)}
    in_maps = []
    for c in range(8):
        m = dict(weights)
        if c < 4:
            m['x'] = np.ascontiguousarray(x_sample[c])
            m.update(consts[1.0])
        else:
            j = c - 4
            x = np.zeros((NT, D), np.float32)
            x[0:SEG] = x_prompt[2 * j]
            x[SEG:2 * SEG] = x_prompt[2 * j + 1]
            m['x'] = x
            m.update(consts[0.0])
        in_maps.append(m)
    res = run_bass_kernel_spmd(P.nc, in_maps, core_ids=list(range(8)))
    y_prompt = np.zeros(x_prompt.shape, np.float32)
    y_sample = np.zeros(x_sample.shape, np.float32)
    for c in range(8):
        y = np.asarray(res.results[c]['y'], np.float32)
        if c < 4:
            y_sample[c] = y
        else:
            j = c - 4
            y_prompt[2 * j] = y[0:SEG]
            y_prompt[2 * j + 1] = y[SEG:2 * SEG]
    return (y_prompt, y_sample)
```
